# Optimizing a Trainium2 kernel written in Bass

```python
import jax, jax.numpy as jnp
from jax import lax
import numpy as np

D_MODEL = 1024
BATCH = 4
SEQ = 4096
DEPTH = 1
DEC_BATCH = 128
DEC_SEQ = 8
PAST_LEN = 16384
PAGE_SIZE = 128

HEAD_DIM = 64
SWA_KV_HEADS = 2
SWA_GROUP = 3
SWA_WINDOW = 128
DIL_HEADS = 6
DIL_PATTERNS = ((128, 1), (512, 4), (2048, 16))
DIL_MAX_WINDOW = 2048
MEM_HEADS = 4
MEM_TOKENS = 256
BLK = 128
ROPE_THETA = 10000.0
EPS = 1e-6
SCALE = HEAD_DIM ** -0.5
Q_A = SWA_KV_HEADS * SWA_GROUP * HEAD_DIM
KV_A = SWA_KV_HEADS * HEAD_DIM
Q_B = DIL_HEADS * HEAD_DIM
KV_B = DIL_HEADS * HEAD_DIM
Q_X = MEM_HEADS * HEAD_DIM
IN_SIZES = (Q_A, KV_A, KV_A, Q_B, KV_B, KV_B, Q_X)
IN_WIDTH = Q_A + 2 * KV_A + Q_B + 2 * KV_B + Q_X
MIX_WIDTH = Q_A + Q_B + Q_X
D_FF = ((8 * D_MODEL + 3 * 256 - 1) // (3 * 256)) * 256

kernel_name = "hymba_swa_sink_dilated_memxattn_decoder_step"


def rms_norm(x, g):
    xf = x.astype(jnp.float32)
    y = xf * lax.rsqrt(jnp.mean(xf * xf, axis=-1, keepdims=True) + EPS)
    return (y * g.astype(jnp.float32)).astype(x.dtype)


def rotary(x, pos):
    half = HEAD_DIM // 2
    inv = ROPE_THETA ** (-jnp.arange(half, dtype=jnp.float32) / half)
    ang = pos.astype(jnp.float32)[:, None] * inv[None, :]
    shape = (1, pos.shape[0]) + (1,) * (x.ndim - 3) + (half,)
    cos = jnp.cos(ang).reshape(shape)
    sin = jnp.sin(ang).reshape(shape)
    xf = x.astype(jnp.float32)
    x1, x2 = xf[..., :half], xf[..., half:]
    return jnp.concatenate([x1 * cos - x2 * sin, x2 * cos + x1 * sin], axis=-1).astype(x.dtype)


def banded_stats(q, k, v, window):
    b, t, hk, g, dh = q.shape
    nb = -(-t // BLK)
    pad = nb * BLK - t
    qb = jnp.pad(q, ((0, 0), (0, pad), (0, 0), (0, 0), (0, 0))).reshape(b, nb, BLK, hk, g, dh)
    kp = jnp.pad(k, ((0, 0), (BLK, pad), (0, 0), (0, 0))).reshape(b, nb + 1, BLK, hk, dh)
    vp = jnp.pad(v, ((0, 0), (BLK, pad), (0, 0), (0, 0))).reshape(b, nb + 1, BLK, hk, dh)
    kw = jnp.concatenate([kp[:, :-1], kp[:, 1:]], axis=2)
    vw = jnp.concatenate([vp[:, :-1], vp[:, 1:]], axis=2)
    s = jnp.einsum('bnqhgd,bnkhd->bnhgqk', qb, kw, preferred_element_type=jnp.float32) * SCALE
    qi = jnp.arange(BLK)[:, None]
    kj = jnp.arange(2 * BLK)[None, :]
    dist = qi + BLK - kj
    kabs = jnp.arange(nb)[:, None, None] * BLK + kj[None] - BLK
    mask = (dist >= 0)[None] & (dist <= window)[None] & (kabs >= 0)
    s = jnp.where(mask[None, :, None, None], s, -jnp.inf)
    m = jnp.max(s, axis=-1, keepdims=True)
    p = jnp.exp(s - m)
    l = jnp.sum(p, axis=-1)
    o = jnp.einsum('bnhgqk,bnkhd->bnqhgd', p, vw.astype(jnp.float32))
    o = o.reshape(b, nb * BLK, hk, g, dh)[:, :t]
    m = jnp.moveaxis(m[..., 0], -1, 2).reshape(b, nb * BLK, hk, g)[:, :t]
    l = jnp.moveaxis(l, -1, 2).reshape(b, nb * BLK, hk, g)[:, :t]
    return o, m, l


def window_gather_stats(q, k_all, v_all, start, window, stride):
    s_len = q.shape[1]
    n_k = window // stride + 1
    idx = start + jnp.arange(s_len)[:, None] - stride * jnp.arange(n_k)[None, :]
    valid = idx >= 0
    idx = jnp.maximum(idx, 0)
    kg = k_all[:, idx]
    vg = v_all[:, idx]
    sc = jnp.einsum('nshgd,nskhd->nshgk', q, kg, preferred_element_type=jnp.float32) * SCALE
    sc = jnp.where(valid[None, :, None, None, :], sc, -jnp.inf)
    m = jnp.max(sc, axis=-1, keepdims=True)
    p = jnp.exp(sc - m)
    l = jnp.sum(p, axis=-1)
    o = jnp.einsum('nshgk,nskhd->nshgd', p, vg.astype(jnp.float32))
    return o, m[..., 0], l


def sink_combine(o, m, l, sink):
    sink = sink.astype(jnp.float32)
    mm = jnp.maximum(m, sink)
    a = jnp.exp(m - mm)
    den = l * a + jnp.exp(sink - mm)
    return o * (a / den)[..., None]


def dilation_combine(stats):
    m_top = jnp.max(jnp.stack([m for _, m, _ in stats]), axis=0)
    num = sum(o * jnp.exp(m - m_top)[..., None] for o, m, _ in stats)
    den = sum(l * jnp.exp(m - m_top) for _, m, l in stats)
    return num / den[..., None]


def fold_stride(x, r):
    b, t = x.shape[:2]
    x = jnp.moveaxis(x.reshape((b, t // r, r) + x.shape[2:]), 2, 1)
    return x.reshape((b * r, t // r) + x.shape[3:])


def unfold_stride(x, b, r):
    br, tr = x.shape[:2]
    x = jnp.moveaxis(x.reshape((b, r, tr) + x.shape[2:]), 1, 2)
    return x.reshape((b, tr * r) + x.shape[3:])


def dilated_prompt(q, k, v):
    b = q.shape[0]
    stats = []
    for w, r in DIL_PATTERNS:
        o, m, l = banded_stats(fold_stride(q, r), fold_stride(k, r), fold_stride(v, r), w // r)
        stats.append((unfold_stride(o, b, r), unfold_stride(m, b, r), unfold_stride(l, b, r)))
    return dilation_combine(stats)


def dilated_sample(q, k_all, v_all, start):
    stats = [window_gather_stats(q, k_all, v_all, start, w, r) for w, r in DIL_PATTERNS]
    return dilation_combine(stats)


def mix_inputs(u, w_in, pos):
    b, t = u.shape[:2]
    z = jnp.einsum('btd,de->bte', u, w_in)
    qa, ka, va, qb, kb, vb, qx = jnp.split(z, np.cumsum(IN_SIZES)[:-1], axis=-1)
    qa = rotary(qa.reshape(b, t, SWA_KV_HEADS, SWA_GROUP, HEAD_DIM), pos)
    ka = rotary(ka.reshape(b, t, SWA_KV_HEADS, HEAD_DIM), pos)
    va = va.reshape(b, t, SWA_KV_HEADS, HEAD_DIM)
    qb = rotary(qb.reshape(b, t, DIL_HEADS, 1, HEAD_DIM), pos)
    kb = rotary(kb.reshape(b, t, DIL_HEADS, HEAD_DIM), pos)
    vb = vb.reshape(b, t, DIL_HEADS, HEAD_DIM)
    qx = qx.reshape(b, t, MEM_HEADS, HEAD_DIM)
    return qa, ka, va, qb, kb, vb, qx


def mem_kv(mem, w_mem_kv):
    b, n = mem.shape[:2]
    mk, mv = jnp.split(jnp.einsum('bmd,de->bme', mem, w_mem_kv), 2, axis=-1)
    return mk.reshape(b, n, MEM_HEADS, HEAD_DIM), mv.reshape(b, n, MEM_HEADS, HEAD_DIM)


def mem_attend(qx, mk, mv):
    s = jnp.einsum('bthd,bmhd->bhtm', qx, mk, preferred_element_type=jnp.float32) * SCALE
    p = jax.nn.softmax(s, axis=-1)
    return jnp.einsum('bhtm,bmhd->bthd', p, mv.astype(jnp.float32))


def swiglu(h, w_gate, w_up, w_down):
    g = jnp.einsum('btd,df->btf', h, w_gate)
    u = jnp.einsum('btd,df->btf', h, w_up)
    return jnp.einsum('btf,fd->btd', jax.nn.silu(g) * u, w_down)


def finish_layer(x, oa, ob, ox, w_o, g_post_mix, g_pre_ffn, w_gate, w_up, w_down, g_post_ffn):
    b, t = x.shape[:2]
    cat = jnp.concatenate([oa.reshape(b, t, -1), ob.reshape(b, t, -1), ox.reshape(b, t, -1)],
                          axis=-1).astype(x.dtype)
    x = x + rms_norm(jnp.einsum('bte,ed->btd', cat, w_o), g_post_mix)
    x = x + rms_norm(swiglu(rms_norm(x, g_pre_ffn), w_gate, w_up, w_down), g_post_ffn)
    return x


def setup_inputs(seed: int = 0) -> dict:
    key = jax.random.key(seed)
    ks = jax.random.split(key, 24)
    f32 = jnp.float32
    lb_a = min(SWA_WINDOW, PAST_LEN)
    lb_b = min(DIL_MAX_WINDOW, PAST_LEN)

    def nrm(k, shape, scale=1.0):
        return scale * jax.random.normal(k, shape, f32)

    def gain(k):
        return 1.0 + 0.02 * jax.random.normal(k, (DEPTH, D_MODEL), f32)

    return {
        "x_prompt": nrm(ks[0], (BATCH, SEQ, D_MODEL)),
        "x_sample": nrm(ks[1], (DEC_BATCH, DEC_SEQ, D_MODEL)),
        "cache_swa_k": nrm(ks[2], (DEPTH, DEC_BATCH, lb_a, SWA_KV_HEADS, HEAD_DIM)),
        "cache_swa_v": nrm(ks[3], (DEPTH, DEC_BATCH, lb_a, SWA_KV_HEADS, HEAD_DIM)),
        "cache_dil_k": nrm(ks[4], (DEPTH, DEC_BATCH, lb_b, DIL_HEADS, HEAD_DIM)),
        "cache_dil_v": nrm(ks[5], (DEPTH, DEC_BATCH, lb_b, DIL_HEADS, HEAD_DIM)),
        "cache_mem_k": nrm(ks[6], (DEPTH, DEC_BATCH, MEM_TOKENS, MEM_HEADS, HEAD_DIM)),
        "cache_mem_v": nrm(ks[7], (DEPTH, DEC_BATCH, MEM_TOKENS, MEM_HEADS, HEAD_DIM)),
        "mem_prompt": nrm(ks[8], (BATCH, MEM_TOKENS, D_MODEL)),
        "g_pre_mix": gain(ks[9]),
        "w_in": nrm(ks[10], (DEPTH, D_MODEL, IN_WIDTH), D_MODEL ** -0.5),
        "sinks": nrm(ks[11], (DEPTH, SWA_KV_HEADS, SWA_GROUP)),
        "w_mem_kv": nrm(ks[12], (DEPTH, D_MODEL, 2 * MEM_HEADS * HEAD_DIM), D_MODEL ** -0.5),
        "w_o": nrm(ks[13], (DEPTH, MIX_WIDTH, D_MODEL), MIX_WIDTH ** -0.5),
        "g_post_mix": gain(ks[14]),
        "g_pre_ffn": gain(ks[15]),
        "w_gate": nrm(ks[16], (DEPTH, D_MODEL, D_FF), D_MODEL ** -0.5),
        "w_up": nrm(ks[17], (DEPTH, D_MODEL, D_FF), D_MODEL ** -0.5),
        "w_down": nrm(ks[18], (DEPTH, D_FF, D_MODEL), D_FF ** -0.5),
        "g_post_ffn": gain(ks[19]),
    }


def reference(x_prompt, x_sample, cache_swa_k, cache_swa_v, cache_dil_k, cache_dil_v,
              cache_mem_k, cache_mem_v, mem_prompt, g_pre_mix, w_in, sinks, w_mem_kv, w_o,
              g_post_mix, g_pre_ffn, w_gate, w_up, w_down, g_post_ffn):
    t_p = x_prompt.shape[1]
    s_len = x_sample.shape[1]
    lb_a = cache_swa_k.shape[2]
    lb_b = cache_dil_k.shape[2]
    pos_p = jnp.arange(t_p)
    pos_s = PAST_LEN + jnp.arange(s_len)
    hp, hs = x_prompt, x_sample
    p_swa_k, p_swa_v, p_dil_k, p_dil_v, p_mem_k, p_mem_v = [], [], [], [], [], []
    s_swa_k, s_swa_v, s_dil_k, s_dil_v = [], [], [], []
    for l in range(DEPTH):
        u = rms_norm(hp, g_pre_mix[l])
        qa, ka, va, qb, kb, vb, qx = mix_inputs(u, w_in[l], pos_p)
        mk, mv = mem_kv(mem_prompt, w_mem_kv[l])
        oa = sink_combine(*banded_stats(qa, ka, va, SWA_WINDOW), sinks[l])
        ob = dilated_prompt(qb, kb, vb)
        ox = mem_attend(qx, mk, mv)
        hp = finish_layer(hp, oa, ob, ox, w_o[l], g_post_mix[l], g_pre_ffn[l],
                          w_gate[l], w_up[l], w_down[l], g_post_ffn[l])
        p_swa_k.append(ka[:, t_p - min(SWA_WINDOW, t_p):])
        p_swa_v.append(va[:, t_p - min(SWA_WINDOW, t_p):])
        p_dil_k.append(kb[:, t_p - min(DIL_MAX_WINDOW, t_p):])
        p_dil_v.append(vb[:, t_p - min(DIL_MAX_WINDOW, t_p):])
        p_mem_k.append(mk)
        p_mem_v.append(mv)

        u = rms_norm(hs, g_pre_mix[l])
        qa, ka, va, qb, kb, vb, qx = mix_inputs(u, w_in[l], pos_s)
        ka_all = jnp.concatenate([cache_swa_k[l], ka.astype(cache_swa_k.dtype)], axis=1)
        va_all = jnp.concatenate([cache_swa_v[l], va.astype(cache_swa_v.dtype)], axis=1)
        kb_all = jnp.concatenate([cache_dil_k[l], kb.astype(cache_dil_k.dtype)], axis=1)
        vb_all = jnp.concatenate([cache_dil_v[l], vb.astype(cache_dil_v.dtype)], axis=1)
        oa = sink_combine(*window_gather_stats(qa, ka_all, va_all, lb_a, SWA_WINDOW, 1), sinks[l])
        ob = dilated_sample(qb, kb_all, vb_all, lb_b)
        ox = mem_attend(qx, cache_mem_k[l], cache_mem_v[l])
        hs = finish_layer(hs, oa, ob, ox, w_o[l], g_post_mix[l], g_pre_ffn[l],
                          w_gate[l], w_up[l], w_down[l], g_post_ffn[l])
        s_swa_k.append(ka_all[:, s_len:])
        s_swa_v.append(va_all[:, s_len:])
        s_dil_k.append(kb_all[:, s_len:])
        s_dil_v.append(vb_all[:, s_len:])
    return (hp, hs,
            jnp.stack(p_swa_k), jnp.stack(p_swa_v), jnp.stack(p_dil_k), jnp.stack(p_dil_v),
            jnp.stack(p_mem_k), jnp.stack(p_mem_v),
            jnp.stack(s_swa_k), jnp.stack(s_swa_v), jnp.stack(s_dil_k), jnp.stack(s_dil_v))
```

```python
import contextlib
import numpy as np
import concourse.bass as bass
import concourse.mybir as mybir
from concourse.ap import AP
from concourse.bass_utils import run_bass_kernel_spmd

F32 = mybir.dt.float32
BF16 = mybir.dt.bfloat16
AF = mybir.ActivationFunctionType
ALU = mybir.AluOpType

ENGS = ("pe", "act", "dve", "pool", "sp")
PSUM_RINGS = {"pT", "zp", "rp", "vp", "st", "ob", "sq", "oc", "on", "yp", "gp", "up", "tp"}
D = 1024
DFF = 2816
NFC = 22
EPS = 1e-6
SCALE = 0.125
NOWN = 2048
NS = 128
PAST = 16384


class Op:
    __slots__ = ("eng", "fn", "deps", "is_dma", "sem", "sem_target", "signal", "sig_idx", "emitted")

    def __init__(self, eng, fn, is_dma):
        self.eng = eng
        self.fn = fn
        self.deps = ()
        self.is_dma = is_dma
        self.sem = None
        self.sem_target = 0
        self.signal = False
        self.sig_idx = 0
        self.emitted = False


class Prog:
    def __init__(self, nc):
        self.nc = nc
        self.seg = {e: [] for e in ENGS}
        self.state = {}
        self.dma_sems = {}
        self.last_dma = {}
        self.last_compute = {}
        self.fence_deps = set()
        self.esems = {e: nc.alloc_semaphore("s_" + e) for e in ENGS}
        self.sigcount = {e: 0 for e in ENGS}
        self.waited = {e: {} for e in ENGS}

    def add(self, eng, fn, reads=(), writes=(), dma_sem=None, nofence=False):
        if _STOPPED[0]:
            return None
        op = Op(eng, fn, dma_sem is not None)
        if dma_sem is not None:
            ent = self.dma_sems.get(dma_sem)
            if ent is None:
                ent = [self.nc.alloc_semaphore("d_" + str(len(self.dma_sems))), 0]
                self.dma_sems[dma_sem] = ent
            ent[1] += 16
            op.sem = dma_sem
            op.sem_target = ent[1]
            if not nofence:
                self.last_dma[dma_sem] = op
        else:
            self.last_compute[eng] = op
        xr = [k for k in reads if k.split("#")[0] in PSUM_RINGS]
        if xr:
            reads = [k for k in reads if k not in xr]
            writes = list(writes) + xr
        deps = set(self.fence_deps)
        for k in reads:
            s = self.state.get(k)
            if s is None:
                s = [None, []]
                self.state[k] = s
            if s[0] is not None:
                deps.add(s[0])
            s[1].append(op)
        for k in writes:
            s = self.state.get(k)
            if s is None:
                s = [None, []]
                self.state[k] = s
            if s[0] is not None:
                deps.add(s[0])
            deps.update(s[1])
            s[0] = op
            s[1] = []
        deps.discard(op)
        op.deps = deps
        self.seg[eng].append(op)
        return op

    @staticmethod
    def _skip(d, op):
        return (not d.is_dma) and d.eng == op.eng and (not op.is_dma) and d.eng == "pe"

    def flush(self, final=False):
        nc = self.nc
        for e in ENGS:
            for op in self.seg[e]:
                for d in op.deps:
                    if d.is_dma or d.emitted or self._skip(d, op):
                        continue
                    d.signal = True
        for e in ENGS:
            lc = self.last_compute.get(e)
            if lc is not None and not lc.emitted:
                lc.signal = True
        for e in ENGS:
            for op in self.seg[e]:
                if op.signal and not op.is_dma:
                    self.sigcount[e] += 1
                    op.sig_idx = self.sigcount[e]
        with nc.Block() as block:
            engmap = {"pe": block.tensor, "act": block.scalar, "dve": block.vector,
                      "pool": block.gpsimd, "sp": block.sync}

            def make(e):
                def body(eng):
                    waited = self.waited[e]
                    for op in self.seg[e]:
                        need = {}
                        for d in op.deps:
                            if d.is_dma:
                                key = ("d", d.sem)
                                val = d.sem_target
                                h = self.dma_sems[d.sem][0]
                            else:
                                if self._skip(d, op):
                                    continue
                                if d.emitted and not d.signal:
                                    continue
                                key = ("e", d.eng)
                                val = d.sig_idx
                                h = self.esems[d.eng]
                            if waited.get(key, 0) >= val:
                                continue
                            if key not in need or need[key][1] < val:
                                need[key] = (h, val)
                        for key, (h, val) in need.items():
                            eng.wait_ge(h, val)
                            waited[key] = val
                        ins = op.fn(eng)
                        if op.is_dma:
                            ins.then_inc(self.dma_sems[op.sem][0], 16)
                        elif op.signal:
                            ins.then_inc(self.esems[e], 1)
                    if final and e == "sp":
                        for k, ent in self.dma_sems.items():
                            eng.wait_ge(ent[0], ent[1])
                return body

            for e in ENGS:
                engmap[e](make(e))
        for e in ENGS:
            for op in self.seg[e]:
                op.emitted = True
                op.fn = None
            self.seg[e] = []
        f = set(self.last_compute.values())
        f.update(self.last_dma.values())
        self.fence_deps = f


class Ring:
    def __init__(self, name, tiles):
        self.name = name
        self.tiles = tiles
        self.i = 0

    def next(self):
        k = self.i % len(self.tiles)
        self.i += 1
        return self.tiles[k], "%s#%d" % (self.name, k)


STOP = [99]


class _StopBuild(Exception):
    pass


_STOPPED = [False]


def chk(x):
    if STOP[0] <= x:
        _STOPPED[0] = True


def build():
    _STOPPED[0] = False
    nc = bass.Bass("TRN2", target_bir_lowering=False)
    P = Prog(nc)

    def din(name, shape):
        return nc.dram_tensor(name, list(shape), F32, kind="ExternalInput").ap()

    def dout(name, shape):
        return nc.dram_tensor(name, list(shape), F32, kind="ExternalOutput").ap()

    xo = din("xo", [NOWN, D]); xh = din("xh", [NOWN, D]); xs = din("xs", [NS, D])
    memp = din("memp", [256, D])
    w_in = din("w_in", [D, 2048]); w_mem = din("w_mem", [D, 512]); w_o = din("w_o", [D, D])
    w_gate = din("w_gate", [D, DFF]); w_up = din("w_up", [D, DFF]); w_down = din("w_down", [DFF, D])
    g_pre = din("g_pre", [1, D]); g_post = din("g_post", [1, D]); g_ffn = din("g_ffn", [1, D]); g_pffn = din("g_pffn", [1, D])
    sinks = din("sinks", [1, 6])
    csk = din("csk", [16, 128, 128]); csv = din("csv", [16, 128, 128])
    cdk = din("cdk", [16, 2048, 384]); cdv = din("cdv", [16, 2048, 384])
    cmk = din("cmk", [16, 256, 256]); cmv = din("cmv", [16, 256, 256])
    c_ident = din("c_ident", [128, 128]); c_rot = din("c_rot", [128, 128])
    c_cos = din("c_cos", [128, 4224]); c_sin = din("c_sin", [128, 4224])
    c_mstd = din("c_mstd", [128, 512]); c_mfirst = din("c_mfirst", [128, 512])
    c_msw = din("c_msw", [128, 768]); c_mnsw = din("c_mnsw", [128, 128])
    c_mdil = din("c_mdil", [128, 256]); c_mndil = din("c_mndil", [128, 128])
    c_nhalf = din("c_nhalf", [128, 1])

    yo = dout("yo", [NOWN, D]); ys = dout("ys", [NS, D])
    o_swa_k = dout("o_swa_k", [128, 128]); o_swa_v = dout("o_swa_v", [128, 128])
    o_dil_k = dout("o_dil_k", [NOWN, 384]); o_dil_v = dout("o_dil_v", [NOWN, 384])
    o_mem_k = dout("o_mem_k", [256, 256]); o_mem_v = dout("o_mem_v", [256, 256])
    s_swa_k = dout("s_swa_k", [16, 128, 128]); s_swa_v = dout("s_swa_v", [16, 128, 128])
    s_dil_k = dout("s_dil_k", [16, 2048, 384]); s_dil_v = dout("s_dil_v", [16, 2048, 384])

    def dap(t, off, dims):
        return AP(t.tensor, off, [list(d) for d in dims])

    es = contextlib.ExitStack()
    with es:
        uid = [0]

        def sb(name, cols, dt=BF16, st=None):
            uid[0] += 1
            return (st or es).enter_context(nc.sbuf_tensor("%s_%d" % (name, uid[0]), [128, cols], dt))

        def ps(name, cols, dt=F32, st=None):
            uid[0] += 1
            return (st or es).enter_context(nc.psum_tensor("%s_%d" % (name, uid[0]), [128, cols], dt))

        def sap(t, off, dims, nparts=128, p0=0):
            W = t.shape[1]
            return AP(t, p0 * W + off, [[W, nparts]] + [list(d) for d in dims])

        ident = sb("ident", 128); rot = sb("rot", 128)
        mstd = sb("mstd", 512); mfirst = sb("mfirst", 512)
        msw = sb("msw", 768); mnsw = sb("mnsw", 128); mdil = sb("mdil", 256); mndil = sb("mndil", 128)
        nhalf = sb("nhalf", 1, F32)
        esink = sb("esink", 6, F32)
        catT = sb("catT", 8 * 2176)
        CT = 2176

        for dst, src, key in ((ident, c_ident, "ident"), (rot, c_rot, "rot"), (mstd, c_mstd, "mstd"),
                              (mfirst, c_mfirst, "mfirst"), (msw, c_msw, "msw"), (mnsw, c_mnsw, "mnsw"),
                              (mdil, c_mdil, "mdil"), (mndil, c_mndil, "mndil")):
            P.add("pool", (lambda e, d=dst, s=src: e.dma_start(out=d[:], in_=s[:, :])), writes=[key], dma_sem="c_" + key)
        P.add("sp", lambda e: e.dma_start(out=nhalf[:], in_=c_nhalf[:, :]), writes=["nhalf"], dma_sem="c_nhalf")
        P.add("sp", lambda e: e.dma_start(out=esink[:], in_=dap(sinks, 0, [[0, 128], [1, 6]])), writes=["esink"], dma_sem="c_sink")
        P.add("act", lambda e: e.activation(out=esink[:], in_=esink[:], func=AF.Exp), reads=["esink"], writes=["esink"])

        epsb = sb("epsb", 1, F32)
        P.add("pool", lambda e: e.memset(epsb[:], EPS), writes=["epsb"])

        def rstd_ops(ss, sk, rs, rk):
            P.add("act", (lambda e: e.activation(out=rs[:], in_=ss[:], func=AF.Sqrt, scale=1.0 / D, bias=epsb[:, 0:1])), reads=[sk, "epsb"], writes=[rk])
            P.add("dve", (lambda e: e.reciprocal(out=rs[:], in_=rs[:])), reads=[rk], writes=[rk])

        identf = sb("identf", 128, F32)
        P.add("sp", lambda e: e.dma_start(out=identf[:], in_=c_ident[:, :]), writes=["identf"], dma_sem="c_identf")
        pending_cp = []
        for s_ in range(16):
            pending_cp.append((s_dil_k, cdk, s_, "cp_dk"))
            pending_cp.append((s_dil_v, cdv, s_, "cp_dv"))
        cp_ctr = [0]

        def maybe_copy(force=False):
            cp_ctr[0] += 1
            if pending_cp and (force or cp_ctr[0] % 4 == 0):
                dst, src, s_, key = pending_cp.pop(0)
                P.add("act", (lambda e: e.dma_start(out=dst[s_, 0:2040, :], in_=src[s_, 8:2048, :])), dma_sem=key, nofence=True)

        wob = nc.dram_tensor("wob", [D, D], BF16, kind="Internal").ap()
        wgb = nc.dram_tensor("wgb", [D, DFF], BF16, kind="Internal").ap()
        wub = nc.dram_tensor("wub", [D, DFF], BF16, kind="Internal").ap()
        wdb = nc.dram_tensor("wdb", [DFF, D], BF16, kind="Internal").ap()

        def cast_weights():
            P.add("pool", lambda e: e.dma_start(out=wob[:, :], in_=w_o[:, :]), writes=["wob"], dma_sem="wob")
            P.add("pool", lambda e: e.dma_start(out=wdb[:, :], in_=w_down[:, :]), writes=["wdb"], dma_sem="wdb")
            P.add("pool", lambda e: e.dma_start(out=dap(wgb, 0, [[DFF, D], [1408, 2], [1, 1408]]), in_=dap(w_gate, 0, [[DFF, D], [1408, 2], [1, 1408]])), writes=["wgb"], dma_sem="wgb")
            P.add("pool", lambda e: e.dma_start(out=dap(wub, 0, [[DFF, D], [1408, 2], [1, 1408]]), in_=dap(w_up, 0, [[DFF, D], [1408, 2], [1, 1408]])), writes=["wub"], dma_sem="wub")
        P.add("act", lambda e: e.dma_start(out=s_swa_k[:, 0:120, :], in_=csk[:, 8:128, :]), dma_sem="cp_sk", nofence=True)
        P.add("act", lambda e: e.dma_start(out=s_swa_v[:, 0:120, :], in_=csv[:, 8:128, :]), dma_sem="cp_sv", nofence=True)

        P.flush()
        for pas in range(4):
            if STOP[0] <= 2 * pas:
                break
            try:
              with contextlib.ExitStack() as pst:
                  if pas == 0:
                      QA = sb("QA", 3 * 2176, st=pst)
                      KAx = sb("KAx", 2304, st=pst); KAy = sb("KAy", 2304, st=pst)
                      QX = sb("QX", 2 * 2176, st=pst)
                      VN = sb("VA", 18 * 320, st=pst)
                      MKT = sb("MKT", 512, st=pst)
                      MV = sb("MV", 2 * 384, st=pst)
                      vbufs = [(VN, "VN")]
                      vmem = [(MV, "MV")]
                  else:
                      QB = sb("QB", 2176, st=pst)
                      KB = sb("KB", 4224, st=pst)
                      VN = sb("V1", 18 * 192, st=pst)
                      V4 = sb("V4", 20 * 192, st=pst)
                      V16 = sb("V16", 32 * 192, st=pst)
                      vbufs = [(VN, "VN"), (V4, "V4"), (V16, "V16")]
                      vmem = []
                  if pas == 0:
                      P.add("pool", lambda e: e.memset(sap(VN, 64, [[320, 18], [128, 2], [1, 64]]), 1.0), writes=["VN_ones"])
                      P.add("pool", lambda e: e.memset(sap(MV, 64, [[192, 4], [1, 64]]), 1.0), writes=["MV_ones"])
                  else:
                      for vt, vk in vbufs:
                          P.add("pool", (lambda e, t=vt: e.memset(sap(t, 64, [[192, t.shape[1] // 192], [1, 64]]), 1.0)), writes=[vk + "_ones"])

                  with contextlib.ExitStack() as ast:
                      wcols = 1024 if pas == 0 else 384
                      wb = sb("wb", 8 * wcols, st=ast)
                      uT = sb("uT", 8 * 2048, st=ast)
                      gpre = sb("gpre", D, F32, st=ast)
                      xts = Ring("xt", [sb("xt%d" % i, D, F32, st=ast) for i in range(3)])
                      hbs = Ring("hb", [sb("hb%d" % i, D, st=ast) for i in range(3)])
                      junk = sb("junk", D, st=ast)
                      sss = Ring("ss", [sb("ss%d" % i, 1, F32, st=ast) for i in range(3)])
                      rss = Ring("rs", [sb("rs%d" % i, 1, F32, st=ast) for i in range(3)])
                      coss = Ring("cos", [sb("cos%d" % i, 512, F32, st=ast) for i in range(2)])
                      sins = Ring("sin", [sb("sin%d" % i, 512, F32, st=ast) for i in range(2)])
                      zbs = Ring("zb", [sb("zb%d" % i, 512, st=ast) for i in range(2)])
                      t1s = Ring("t1", [sb("t1_%d" % i, 512, F32, st=ast) for i in range(2)])
                      t2s = Ring("t2", [sb("t2_%d" % i, 512, F32, st=ast) for i in range(2)])
                      ksts = Ring("kst", [sb("kst%d" % i, 512, st=ast) for i in range(2)])
                      pTs = Ring("pT", [ps("pT%d" % i, 1024, BF16, st=ast) for i in range(2)])
                      zps = Ring("zp", [ps("zp%d" % i, 512, st=ast) for i in range(2)])
                      rps = Ring("rp", [ps("rp%d" % i, 512, st=ast) for i in range(2)])
                      vps = Ring("vp", [ps("vp%d" % i, 512, st=ast) for i in range(2)])

                      P.add("sp", lambda e: e.dma_start(out=gpre[:], in_=dap(g_pre, 0, [[0, 128], [1, D]])), writes=["gpre"], dma_sem="gpre")

                      if pas == 0:
                          segs = [(0, 384, 0), (384, 128, 384), (448, 64, 512), (384, 64, 576), (1792, 256, 640), (512, 128, 896)]
                      else:
                          p_ = pas - 1
                          segs = [(640 + 128 * p_, 128, 0), (1024 + 128 * p_, 128, 128), (1408 + 128 * p_, 128, 256)]
                      for (sc, n, off) in segs:
                          P.add("pool", (lambda e, sc=sc, n=n, off=off: e.dma_start(
                              out=sap(wb, off, [[wcols, 8], [1, n]]),
                              in_=dap(w_in, sc, [[2048, 128], [128 * 2048, 8], [1, n]]))),
                              writes=["wb"], dma_sem="wb")

                      def norm_T(xsrc, row0, ntok, gate_key="uT"):
                          for t in range(ntok // 128):
                              xt, xk = xts.next(); hb, hk = hbs.next(); ss, sk = sss.next(); rs, rk = rss.next()
                              pT, pk = pTs.next()
                              r0 = row0 + t * 128
                              maybe_copy()
                              P.add("sp", (lambda e, xt=xt, r0=r0: e.dma_start(out=xt[:], in_=xsrc[r0:r0 + 128, :])), writes=[xk], dma_sem=xk)
                              P.add("act", (lambda e, xt=xt, ss=ss: e.activation(out=junk[:], in_=xt[:], func=AF.Square, accum_out=ss[:])),
                                    reads=[xk], writes=["junk", sk])
                              rstd_ops(ss, sk, rs, rk)
                              P.add("dve", (lambda e, xt=xt, rs=rs, hb=hb: e.scalar_tensor_tensor(out=hb[:], in0=xt[:], scalar=rs[:, 0:1], in1=gpre[:], op0=ALU.mult, op1=ALU.mult)),
                                    reads=[xk, rk, "gpre"], writes=[hk])
                              for dc in range(8):
                                  P.add("pe", (lambda e, pT=pT, hb=hb, dc=dc: e.transpose(out=pT[:, dc * 128:(dc + 1) * 128], in_=hb[:, dc * 128:(dc + 1) * 128], identity=ident[:])),
                                        reads=[hk, "ident"], writes=[pk])
                              P.add("act", (lambda e, pT=pT, t=t: e.activation(out=sap(uT, t * 128, [[2048, 8], [1, 128]]), in_=sap(pT, 0, [[128, 8], [1, 128]]), func=AF.Copy)),
                                    reads=[pk], writes=["uT%d" % (t // 4)])

                      def fm_proj(woff, dst, dkey, dcol, tb0, ntb, ctab0, rope):
                          zp, zk = zps.next()
                          for dc in range(8):
                              P.add("pe", (lambda e, zp=zp, dc=dc: e.matmul(zp[:, 0:ntb], lhsT=wb[:, dc * wcols + woff: dc * wcols + woff + 128],
                                                                             rhs=uT[:, dc * 2048 + tb0: dc * 2048 + tb0 + ntb], start=(dc == 0), stop=(dc == 7))),
                                    reads=["wb", "uT%d" % (tb0 // 512)], writes=[zk])
                          if not rope:
                              P.add("act", (lambda e, zp=zp: e.activation(out=dst[:, dcol:dcol + ntb], in_=zp[:, 0:ntb], func=AF.Copy)),
                                    reads=[zk], writes=[dkey])
                              return
                          zb, zbk = zbs.next(); rp, rk = rps.next(); t1, t1k = t1s.next(); t2, t2k = t2s.next()
                          cs, ck = cur_tab["cos"]; sn, snk = cur_tab["sin"]
                          P.add("act", (lambda e, zp=zp, zb=zb: e.activation(out=zb[:, 0:ntb], in_=zp[:, 0:ntb], func=AF.Copy)), reads=[zk], writes=[zbk])
                          P.add("pe", (lambda e, rp=rp, zb=zb: e.matmul(rp[:, 0:ntb], lhsT=rot[:], rhs=zb[:, 0:ntb], start=True, stop=True)),
                                reads=[zbk, "rot"], writes=[rk])
                          P.add("dve", (lambda e, zp=zp, t1=t1, cs=cs: e.tensor_tensor(out=t1[:, 0:ntb], in0=zp[:, 0:ntb], in1=cs[:, 0:ntb], op=ALU.mult)),
                                reads=[zk, ck], writes=[t1k])
                          P.add("dve", (lambda e, rp=rp, t2=t2, sn=sn: e.tensor_tensor(out=t2[:, 0:ntb], in0=rp[:, 0:ntb], in1=sn[:, 0:ntb], op=ALU.mult)),
                                reads=[rk, snk], writes=[t2k])
                          P.add("pool", (lambda e, t1=t1, t2=t2: e.tensor_tensor(out=dst[:, dcol:dcol + ntb], in0=t1[:, 0:ntb], in1=t2[:, 0:ntb], op=ALU.add)),
                                reads=[t1k, t2k], writes=[dkey])

                      cur_tab = {}

                      def load_tab(tcol, ntb):
                          cs, ck = coss.next(); sn, snk = sins.next()
                          P.add("sp", (lambda e, cs=cs: e.dma_start(out=cs[:, 0:ntb], in_=c_cos[:, tcol:tcol + ntb])), writes=[ck], dma_sem=ck)
                          P.add("sp", (lambda e, sn=sn: e.dma_start(out=sn[:, 0:ntb], in_=c_sin[:, tcol:tcol + ntb])), writes=[snk], dma_sem=snk)
                          cur_tab["cos"] = (cs, ck); cur_tab["sin"] = (sn, snk)

                      def tm_proj(woff, ncols, tok_off, tok_step, dst, dkey, blk):
                          vp, vk = vps.next()
                          last = tok_off + 127 * tok_step
                          ukeys = ["uT%d" % b for b in range(tok_off // 512, last // 512 + 1)]
                          for dc in range(8):
                              P.add("pe", (lambda e, vp=vp, dc=dc: e.matmul(vp[:, 0:ncols], lhsT=sap(uT, dc * 2048 + tok_off, [[tok_step, 128]]),
                                                                             rhs=wb[:, dc * wcols + woff: dc * wcols + woff + ncols], start=(dc == 0), stop=(dc == 7))),
                                    reads=["wb"] + ukeys, writes=[vk])
                          bs = 320 if pas == 0 else 192
                          P.add("act", (lambda e, vp=vp: e.activation(out=sap(dst, blk * bs, [[128, 2], [1, 64]]), in_=sap(vp, 0, [[64, 2], [1, 64]]), func=AF.Copy)),
                                reads=[vk], writes=[dkey])
                          if pas == 0:
                              P.add("act", (lambda e, vp=vp: e.activation(out=dst[:, blk * 320 + 256: blk * 320 + 320], in_=vp[:, 0:64], func=AF.Copy)),
                                    reads=[vk], writes=[dkey])

                      def k_out(src, skey, scol, ntile, emit_dma):
                          pT, pk = pTs.next(); kst, kk = ksts.next()
                          for j in range(ntile):
                              P.add("pe", (lambda e, pT=pT, j=j: e.transpose(out=pT[:, j * 128:(j + 1) * 128], in_=src[:, scol + j * 128: scol + (j + 1) * 128], identity=ident[:])),
                                    reads=[skey, "ident"], writes=[pk])
                          P.add("act", (lambda e, pT=pT, kst=kst: e.activation(out=kst[:, 0:ntile * 128], in_=pT[:, 0:ntile * 128], func=AF.Copy)),
                                reads=[pk], writes=[kk])
                          emit_dma(kst, kk)

                      chk(0.1)
                      if pas == 0:
                          wm = sb("wm", 8 * 512, st=ast)
                          P.add("pool", lambda e: e.dma_start(out=sap(wm, 0, [[512, 8], [1, 512]]), in_=dap(w_mem, 0, [[512, 128], [128 * 512, 8], [1, 512]])),
                                writes=["wm"], dma_sem="wm")
                          for blk in range(2):
                              hb, hk = hbs.next(); pT, pk = pTs.next()
                              P.add("pool", (lambda e, hb=hb, blk=blk: e.dma_start(out=hb[:], in_=memp[blk * 128:(blk + 1) * 128, :])), writes=[hk], dma_sem=hk)
                              for dc in range(8):
                                  P.add("pe", (lambda e, pT=pT, hb=hb, dc=dc: e.transpose(out=pT[:, dc * 128:(dc + 1) * 128], in_=hb[:, dc * 128:(dc + 1) * 128], identity=ident[:])),
                                        reads=[hk, "ident"], writes=[pk])
                              P.add("act", (lambda e, pT=pT, blk=blk: e.activation(out=sap(uT, blk * 128, [[2048, 8], [1, 128]]), in_=sap(pT, 0, [[128, 8], [1, 128]]), func=AF.Copy)),
                                    reads=[pk], writes=["uT0"])
                          for ch in range(2):
                              zp, zk = zps.next()
                              for dc in range(8):
                                  P.add("pe", (lambda e, zp=zp, dc=dc, ch=ch: e.matmul(zp[:, 0:256], lhsT=wm[:, dc * 512 + ch * 128: dc * 512 + (ch + 1) * 128],
                                                                                       rhs=uT[:, dc * 2048: dc * 2048 + 256], start=(dc == 0), stop=(dc == 7))),
                                        reads=["wm", "uT0"], writes=[zk])
                              P.add("act", (lambda e, zp=zp, ch=ch: e.activation(out=MKT[:, ch * 256:(ch + 1) * 256], in_=zp[:, 0:256], func=AF.Copy)),
                                    reads=[zk], writes=["MKT"])
                          mkn = sb("mkn", 512, st=ast)
                          for blk in range(2):
                              vp, vk = vps.next()
                              for dc in range(8):
                                  P.add("pe", (lambda e, vp=vp, dc=dc, blk=blk: e.matmul(vp[:, 0:512], lhsT=uT[:, dc * 2048 + blk * 128: dc * 2048 + (blk + 1) * 128],
                                                                                         rhs=wm[:, dc * 512:(dc + 1) * 512], start=(dc == 0), stop=(dc == 7))),
                                        reads=["wm", "uT0"], writes=[vk])
                              P.add("act", (lambda e, vp=vp, blk=blk: e.activation(out=mkn[:, blk * 256:(blk + 1) * 256], in_=vp[:, 0:256], func=AF.Copy)),
                                    reads=[vk], writes=["mkn"])
                              P.add("act", (lambda e, vp=vp, blk=blk: e.activation(out=sap(MV, blk * 384, [[192, 2], [128, 2], [1, 64]]), in_=sap(vp, 256, [[128, 2], [64, 2], [1, 64]]), func=AF.Copy)),
                                    reads=[vk], writes=["MV"])
                          P.add("pool", lambda e: e.dma_start(out=dap(o_mem_k, 0, [[256, 128], [128 * 256, 2], [1, 256]]), in_=sap(mkn, 0, [[256, 2], [1, 256]])),
                                reads=["mkn"], dma_sem="o_mem_k")
                          for blk in range(2):
                              for pr in range(2):
                                  P.add("pool", (lambda e, blk=blk, pr=pr: e.dma_start(out=dap(o_mem_v, blk * 128 * 256 + pr * 128, [[256, 128], [64, 2], [1, 64]]),
                                                                                       in_=sap(MV, blk * 384 + pr * 192, [[128, 2], [1, 64]]))),
                                        reads=["MV"], dma_sem="o_mem_v")

                          chk(0.2)
                          norm_T(xh, 1920, 128)
                          chk(0.21)
                          load_tab(1920, 128)
                          chk(0.22)
                          fm_proj(384, KAx, "KAx", 0, 0, 128, None, True)
                          chk(0.23)
                          fm_proj(512, KAy, "KAy", 0, 0, 128, None, True)
                          chk(0.24)
                          tm_proj(896, 128, 0, 1, VN, "VN", 0)
                          chk(0.3)
                          norm_T(xo, 0, 2048)
                          chk(0.4)
                          for tb in range(4):
                              load_tab(2048 + tb * 512, 512)
                              for c in range(3):
                                  fm_proj(c * 128, QA, "QA", c * 2176 + tb * 512, tb * 512, 512, None, True)
                              fm_proj(384, KAx, "KAx", 128 + tb * 512, tb * 512, 512, None, True)
                              fm_proj(512, KAy, "KAy", 128 + tb * 512, tb * 512, 512, None, True)
                              for c in range(2):
                                  fm_proj(640 + c * 128, QX, "QX", c * 2176 + tb * 512, tb * 512, 512, None, False)
                          for j in range(16):
                              tm_proj(896, 128, j * 128, 1, VN, "VN", 1 + j)
                          chk(0.5)
                          k_out(KAx, "KAx", 128 + 15 * 128, 1,
                                lambda kst, kk: P.add("pool", (lambda e: e.dma_start(out=o_swa_k[:, :], in_=kst[:, 0:128])), reads=[kk], dma_sem=kk))
                          P.add("pool", lambda e: e.dma_start(out=dap(o_swa_v, 0, [[128, 128], [64, 2], [1, 64]]), in_=sap(VN, 16 * 320, [[128, 2], [1, 64]])), reads=["VN"], dma_sem="o_swa_v")
                          chk(0.6)
                          norm_T(xs, 0, 128)
                          load_tab(4096, 128)
                          for c in range(3):
                              fm_proj(c * 128, QA, "QA", c * 2176 + 2048, 0, 128, None, True)
                          fm_proj(384, KAx, "KAx", 2176, 0, 128, None, True)
                          fm_proj(512, KAy, "KAy", 2176, 0, 128, None, True)
                          for c in range(2):
                              fm_proj(640 + c * 128, QX, "QX", c * 2176 + 2048, 0, 128, None, False)
                          tm_proj(896, 128, 0, 1, VN, "VN", 17)

                          def sw_new_k(kst, kk):
                              for s in range(16):
                                  P.add("pool", (lambda e, s=s: e.dma_start(out=s_swa_k[s, 120:128, :], in_=kst[s * 8:(s + 1) * 8, 0:128])), reads=[kk], dma_sem=kk)
                          k_out(KAx, "KAx", 2176, 1, sw_new_k)
                          for s in range(16):
                              P.add("pool", (lambda e, s=s: e.dma_start(out=dap(s_swa_v, s * 128 * 128 + 120 * 128, [[128, 8], [64, 2], [1, 64]]), in_=sap(VN, 17 * 320, [[128, 2], [1, 64]], nparts=8, p0=s * 8))),
                                    reads=["VN"], dma_sem="s_swa_v_new")
                      else:
                          p_ = pas - 1
                          norm_T(xh, 0, 2048)
                          for tb in range(4):
                              load_tab(tb * 512, 512)
                              fm_proj(128, KB, "KB", tb * 512, tb * 512, 512, None, True)
                          tm_proj(256, 128, 15 * 128, 1, VN, "VN", 0)
                          for c in range(4):
                              tm_proj(256, 128, 1536 + c, 4, V4, "V4", c)
                          for c in range(16):
                              tm_proj(256, 128, c, 16, V16, "V16", c)
                          norm_T(xo, 0, 2048)
                          for tb in range(4):
                              load_tab(2048 + tb * 512, 512)
                              fm_proj(0, QB, "QB", tb * 512, tb * 512, 512, None, True)
                              fm_proj(128, KB, "KB", 2048 + tb * 512, tb * 512, 512, None, True)
                          for j in range(16):
                              tm_proj(256, 128, j * 128, 1, VN, "VN", 1 + j)
                          for n in range(4):
                              for c in range(4):
                                  tm_proj(256, 128, n * 512 + c, 4, V4, "V4", (n + 1) * 4 + c)
                          for c in range(16):
                              tm_proj(256, 128, c, 16, V16, "V16", 16 + c)
                          for q4 in range(4):
                              k_out(KB, "KB", 2048 + q4 * 512, 4,
                                    lambda kst, kk, q4=q4: P.add("pool", (lambda e: e.dma_start(
                                        out=dap(o_dil_k, q4 * 512 * 384 + p_ * 128, [[384, 128], [128 * 384, 4], [1, 128]]),
                                        in_=sap(kst, 0, [[128, 4], [1, 128]]))), reads=[kk], dma_sem=kk))
                          for hd in range(2):
                              P.add("pool", (lambda e, hd=hd: e.dma_start(out=dap(o_dil_v, p_ * 128 + hd * 64, [[384, 128], [128 * 384, 16], [1, 64]]),
                                                                          in_=sap(VN, 192 + hd * 128, [[192, 16], [1, 64]]))), reads=["VN"], dma_sem="o_dil_v")
                          norm_T(xs, 0, 128)
                          load_tab(4096, 128)
                          fm_proj(0, QB, "QB", 2048, 0, 128, None, True)
                          fm_proj(128, KB, "KB", 4096, 0, 128, None, True)
                          tm_proj(256, 128, 0, 1, VN, "VN", 17)

                          def dil_new_k(kst, kk):
                              for s in range(16):
                                  P.add("pool", (lambda e, s=s: e.dma_start(out=s_dil_k[s, 2040:2048, p_ * 128:(p_ + 1) * 128], in_=kst[s * 8:(s + 1) * 8, 0:128])),
                                        reads=[kk], dma_sem=kk)
                          k_out(KB, "KB", 4096, 1, dil_new_k)
                          for s in range(16):
                              P.add("pool", (lambda e, s=s: e.dma_start(out=dap(s_dil_v, s * 2048 * 384 + 2040 * 384 + p_ * 128, [[384, 8], [64, 2], [1, 64]]), in_=sap(VN, 17 * 192, [[128, 2], [1, 64]], nparts=8, p0=s * 8))),
                                    reads=["VN"], dma_sem="s_dil_v_new")

                      P.flush()
                  if STOP[0] <= 2 * pas + 1:
                      break
                  with contextlib.ExitStack() as bst:
                      acc = sb("acc", 2 * 2176, F32, st=bst)
                      rl = sb("rl", 2176, F32, st=bst)
                      pts = Ring("pt", [sb("pt%d" % i, 512, st=bst) for i in range(4)])

                      if pas == 0:
                          cast_weights()
                      accs = sb("accs", 6 * 128, F32, st=bst)
                      accx = sb("accx", 4 * 128, F32, st=bst)
                      with contextlib.ExitStack() as sst:
                          k32s = Ring("k32", [sb("k32_%d" % i, 2048, F32, st=sst) for i in range(2)])
                          v32s = Ring("v32", [sb("v32_%d" % i, 2048, F32, st=sst) for i in range(2)])
                          kdts = Ring("kdt", [sb("kdt%d" % i, 2048, st=sst) for i in range(2)])
                          ptn = sb("ptn", 256, st=sst)
                          tps = Ring("tp", [ps("tp%d" % i, 512, st=sst) for i in range(2)])
                          sqs = Ring("sq", [ps("sq%d" % i, 512, st=sst) for i in range(2)])
                          snr = Ring("st", [ps("sn%d" % i, 512, st=sst) for i in range(2)])
                          ocA = ps("ocA", 512, st=sst); ocB = ps("ocB", 512, st=sst)

                          def transposes32(src, skey, ncols_src, n, dst, dkey):
                              for q in range((n + 3) // 4):
                                  tp, tpk = tps.next()
                                  m_ = min(4, n - q * 4)
                                  for c4 in range(m_):
                                      c = q * 4 + c4
                                      P.add("pe", (lambda e, tp=tp, c4=c4, c=c: e.transpose(out=tp[:, c4 * 128:(c4 + 1) * 128], in_=src[:, c * 128:(c + 1) * 128], identity=identf[:])),
                                            reads=[skey, "identf"], writes=[tpk])
                                  P.add("act", (lambda e, tp=tp, q=q, m_=m_: e.activation(out=dst[:, q * 512: q * 512 + m_ * 128], in_=tp[:, 0:m_ * 128], func=AF.Copy)),
                                        reads=[tpk], writes=[dkey])

                          def new_keys(pairs, ktiles, qt, qbase, kcol, vt, vkey, voffs, masknew, mnkey, outs):
                              for i, (half, (kt, kkey)) in enumerate(zip((0, 1), ktiles)):
                                  sn, snk = snr.next()
                                  P.add("pe", (lambda e, sn=sn, kt=kt, half=half, i=i: e.matmul(sn[:, 0:128], lhsT=sap(kt, kcol, [[1, 128]], nparts=64, p0=half * 64),
                                                                                                rhs=sap(qt, qbase, [[1, 128]], nparts=64, p0=half * 64), start=True, stop=True)),
                                        reads=[kkey, qkey_s], writes=[snk])
                                  P.add("act", (lambda e, sn=sn, i=i: e.activation(out=ptn[:, i * 128:(i + 1) * 128], in_=sn[:, 0:128], func=AF.Exp, scale=SCALE)),
                                        reads=[snk], writes=["ptn"])
                              P.add("pool", (lambda e: e.tensor_tensor(out=sap(ptn, 0, [[128, 2], [1, 128]]), in0=sap(ptn, 0, [[128, 2], [1, 128]]),
                                                                       in1=sap(masknew, 0, [[0, 2], [1, 128]]), op=ALU.mult)), reads=["ptn", mnkey], writes=["ptn"])
                              for i in range(2):
                                  ob_, ok_, ocol = outs[i]
                                  P.add("pe", (lambda e, i=i, ob_=ob_, ocol=ocol: e.matmul(ob_[:, ocol:ocol + 128], lhsT=vt[:, voffs[i]: voffs[i] + 128],
                                                                                           rhs=ptn[:, i * 128:(i + 1) * 128], start=True, stop=True)),
                                        reads=[vkey, vkey + "_ones", "ptn"], writes=[ok_])

                          if pas == 0:
                              SWOFF = {(0, 0): 0, (1, 1): 64, (0, 1): 128, (1, 0): 192}
                              qkey_s = "QA"
                              vc = sb("vc", 16 * 320, st=sst)
                              qbs = sb("qbs", 768, st=sst)
                              pts_ = sb("pts_", 768, st=sst)
                              P.add("pool", lambda e: e.memset(sap(vc, 64, [[320, 16], [128, 2], [1, 64]]), 1.0), writes=["vc_ones"])
                              P.add("pool", lambda e: e.memset(qbs[:], 0.0), writes=["qbs"])
                              for h in range(6):
                                  P.add("dve", (lambda e, h=h: e.tensor_copy(out=sap(qbs, h * 8, [[48, 16], [1, 8]], nparts=64, p0=(h // 3) * 64),
                                                                             in_=sap(QA, (h // 2) * 2176 + 2048, [[8, 16], [1, 8]], nparts=64, p0=(h % 2) * 64))),
                                        reads=["QA"], writes=["qbs"])
                              k32, k32k = k32s.next(); v32, v32k = v32s.next(); kdt, kdtk = kdts.next()
                              P.add("sp", lambda e: e.dma_start(out=sap(k32, 0, [[128, 16], [1, 128]]), in_=dap(csk, 0, [[128, 128], [128 * 128, 16], [1, 128]])), writes=[k32k], dma_sem=k32k)
                              P.add("sp", lambda e: e.dma_start(out=sap(v32, 0, [[128, 16], [1, 128]]), in_=dap(csv, 0, [[128, 128], [128 * 128, 16], [1, 128]])), writes=[v32k], dma_sem=v32k)
                              P.add("pool", lambda e: e.tensor_copy(out=sap(vc, 0, [[320, 16], [128, 2], [1, 64]]), in_=sap(v32, 0, [[128, 16], [64, 2], [1, 64]])), reads=[v32k], writes=["vc"])
                              P.add("pool", lambda e: e.tensor_copy(out=sap(vc, 256, [[320, 16], [1, 64]]), in_=sap(v32, 0, [[128, 16], [1, 64]])), reads=[v32k], writes=["vc"])
                              transposes32(k32, k32k, 2048, 16, kdt, kdtk)
                              sqa, sqak = sqs.next(); sqb, sqbk = sqs.next()
                              for s_ in range(16):
                                  sq_, sqk_ = (sqa, sqak) if s_ < 8 else (sqb, sqbk)
                                  P.add("pe", (lambda e, s_=s_, sq_=sq_: e.matmul(sq_[:, (s_ % 8) * 48:(s_ % 8 + 1) * 48], lhsT=kdt[:, s_ * 128:(s_ + 1) * 128],
                                                                                  rhs=qbs[:, s_ * 48:(s_ + 1) * 48], start=True, stop=True)), reads=[kdtk, "qbs"], writes=[sqk_])
                              for i_, (sq_, sqk_) in enumerate(((sqa, sqak), (sqb, sqbk))):
                                  P.add("act", (lambda e, i_=i_, sq_=sq_: e.activation(out=pts_[:, i_ * 384:(i_ + 1) * 384], in_=sq_[:, 0:384], func=AF.Exp, scale=SCALE)),
                                        reads=[sqk_], writes=["pts_"])
                              P.add("pool", lambda e: e.tensor_tensor(out=pts_[:], in0=pts_[:], in1=msw[:], op=ALU.mult), reads=["pts_", "msw"], writes=["pts_"])
                              for s_ in range(16):
                                  for h in range(6):
                                      oc_, ock_, oco = (ocA, "oc#0", h * 128) if h < 4 else (ocB, "oc#1", (h - 4) * 128)
                                      voff = s_ * 320 + SWOFF[(h % 2, h // 3)]
                                      P.add("pe", (lambda e, s_=s_, h=h, oc_=oc_, oco=oco, voff=voff: e.matmul(oc_[:, oco + s_ * 8: oco + s_ * 8 + 8], lhsT=vc[:, voff:voff + 128],
                                                                                                            rhs=pts_[:, s_ * 48 + h * 8: s_ * 48 + h * 8 + 8], start=True, stop=True)),
                                            reads=["vc", "vc_ones", "pts_"], writes=[ock_])
                              P.add("dve", lambda e: e.tensor_copy(out=accs[:, 0:512], in_=ocA[:, 0:512]), reads=["oc#0"], writes=["accs"])
                              P.add("dve", lambda e: e.tensor_copy(out=accs[:, 512:768], in_=ocB[:, 0:256]), reads=["oc#1"], writes=["accs"])
                              for cpair in range(3):
                                  he, ho = 2 * cpair, 2 * cpair + 1
                                  ktl = [(KAx, "KAx") if (h // 3) == (h % 2) else (KAy, "KAy") for h in (he, ho)]
                                  voffs = [17 * 320 + SWOFF[(h % 2, h // 3)] for h in (he, ho)]
                                  new_keys(None, ktl, QA, cpair * 2176 + 2048, 2176, VN, "VN", voffs, mnsw, "mnsw", [(ocA, "oc#0", 0), (ocA, "oc#0", 128)])
                                  P.add("dve", (lambda e, cpair=cpair: e.tensor_tensor(out=accs[:, cpair * 256:(cpair + 1) * 256], in0=ocA[:, 0:256], in1=accs[:, cpair * 256:(cpair + 1) * 256], op=ALU.add)),
                                        reads=["oc#0", "accs"], writes=["accs"])
                              vmq = sb("vmq", 8 * 384, st=sst)
                              qbx = sb("qbx", 512, st=sst)
                              ptm = sb("ptm", 256, st=sst)
                              P.add("pool", lambda e: e.memset(sap(vmq, 64, [[192, 16], [1, 64]]), 1.0), writes=["vmq_ones"])
                              P.add("pool", lambda e: e.memset(qbx[:], 0.0), writes=["qbx"])
                              for ch in range(2):
                                  for hd in range(2):
                                      P.add("dve", (lambda e, ch=ch, hd=hd: e.tensor_copy(out=sap(qbx, ch * 16 + hd * 8, [[32, 16], [1, 8]], nparts=64, p0=hd * 64),
                                                                                        in_=sap(QX, ch * 2176 + 2048, [[8, 16], [1, 8]], nparts=64, p0=hd * 64))),
                                            reads=["QX"], writes=["qbx"])
                              for qd in range(4):
                                  k32, k32k = k32s.next(); v32, v32k = v32s.next(); kdt, kdtk = kdts.next()
                                  P.add("sp", (lambda e, k32=k32, qd=qd: e.dma_start(out=sap(k32, 0, [[256, 8], [1, 256]]), in_=dap(cmk, qd * 4 * 65536, [[256, 128], [128 * 256, 8], [1, 256]]))),
                                        writes=[k32k], dma_sem=k32k)
                                  P.add("sp", (lambda e, v32=v32, qd=qd: e.dma_start(out=sap(v32, 0, [[256, 8], [1, 256]]), in_=dap(cmv, qd * 4 * 65536, [[256, 128], [128 * 256, 8], [1, 256]]))),
                                        writes=[v32k], dma_sem=v32k)
                                  for pr in range(2):
                                      P.add("pool", (lambda e, v32=v32, pr=pr: e.tensor_copy(out=sap(vmq, pr * 192, [[384, 8], [128, 2], [1, 64]]), in_=sap(v32, pr * 128, [[256, 8], [64, 2], [1, 64]]))),
                                            reads=[v32k], writes=["vmq"])
                                  transposes32(k32, k32k, 2048, 16, kdt, kdtk)
                                  sq_, sqk_ = sqs.next()
                                  for sl in range(4):
                                      for blk in range(2):
                                          for ch in range(2):
                                              idx = (sl * 2 + blk) * 2 + ch
                                              sg_ = qd * 4 + sl
                                              P.add("pe", (lambda e, idx=idx, sg_=sg_, ch=ch, sq_=sq_, kdt=kdt: e.matmul(sq_[:, idx * 16:(idx + 1) * 16], lhsT=kdt[:, idx * 128:(idx + 1) * 128],
                                                                                                                     rhs=qbx[:, (sg_ * 2 + ch) * 16:(sg_ * 2 + ch + 1) * 16], start=True, stop=True)),
                                                    reads=[kdtk, "qbx"], writes=[sqk_])
                                  P.add("act", (lambda e, sq_=sq_: e.activation(out=ptm[:], in_=sq_[:, 0:256], func=AF.Exp, scale=SCALE)), reads=[sqk_], writes=["ptm"])
                                  for sl in range(4):
                                      sg_ = qd * 4 + sl
                                      for x in range(4):
                                          for blk in range(2):
                                              voff = (sl * 2 + blk) * 384 + (x // 2) * 192 + (x % 2) * 64
                                              pcol = ((sl * 2 + blk) * 2 + x // 2) * 16 + (x % 2) * 8
                                              P.add("pe", (lambda e, sg_=sg_, x=x, blk=blk, voff=voff, pcol=pcol: e.matmul(ocB[:, x * 128 + sg_ * 8: x * 128 + sg_ * 8 + 8], lhsT=vmq[:, voff:voff + 128],
                                                                                                                       rhs=ptm[:, pcol:pcol + 8], start=(blk == 0), stop=(blk == 1))),
                                                    reads=["vmq", "vmq_ones", "ptm"], writes=["oc#1"])
                              P.add("dve", lambda e: e.tensor_copy(out=accx[:], in_=ocB[:]), reads=["oc#1"], writes=["accx"])
                          else:
                              p_ = pas - 1
                              qkey_s = "QB"
                              vds = [sb("vd%d" % i, 16 * 192, st=sst) for i in range(2)]
                              vdr = Ring("vd", vds)
                              qbd = sb("qbd", 256, st=sst)
                              ptds = Ring("ptd", [sb("ptd%d" % i, 256, st=sst) for i in range(2)])
                              for i, vt in enumerate(vds):
                                  P.add("pool", (lambda e, vt=vt: e.memset(sap(vt, 64, [[192, 16], [1, 64]]), 1.0)), writes=["vd#%d_ones" % i])
                              P.add("pool", lambda e: e.memset(qbd[:], 0.0), writes=["qbd"])
                              for hd in range(2):
                                  P.add("dve", (lambda e, hd=hd: e.tensor_copy(out=sap(qbd, hd * 8, [[16, 16], [1, 8]], nparts=64, p0=hd * 64),
                                                                               in_=sap(QB, 2048, [[8, 16], [1, 8]], nparts=64, p0=hd * 64))), reads=["QB"], writes=["qbd"])
                              def mk_seq(s_):
                                  st_ = {}

                                  def A1():
                                      k32, k32k = k32s.next(); v32, v32k = v32s.next(); kdt, kdtk = kdts.next(); vd, vdk = vdr.next()
                                      st_.update(kdt=kdt, kdtk=kdtk, vd=vd, vdk=vdk)
                                      P.add("sp", (lambda e: e.dma_start(out=sap(k32, 0, [[128, 16], [1, 128]]),
                                                                         in_=dap(cdk, s_ * 2048 * 384 + p_ * 128, [[16 * 384, 128], [384, 16], [1, 128]]))), writes=[k32k], dma_sem=k32k)
                                      P.add("sp", (lambda e: e.dma_start(out=sap(v32, 0, [[128, 16], [1, 128]]),
                                                                         in_=dap(cdv, s_ * 2048 * 384 + p_ * 128, [[16 * 384, 128], [384, 16], [1, 128]]))), writes=[v32k], dma_sem=v32k)
                                      P.add("dve", (lambda e: e.tensor_copy(out=sap(vd, 0, [[192, 16], [128, 2], [1, 64]]), in_=sap(v32, 0, [[128, 16], [64, 2], [1, 64]]))),
                                            reads=[v32k], writes=[vdk])
                                      transposes32(k32, k32k, 2048, 16, kdt, kdtk)

                                  def A2():
                                      kdt, kdtk = st_["kdt"], st_["kdtk"]
                                      sq_, sqk_ = sqs.next(); ptd, ptdk = ptds.next()
                                      st_.update(ptd=ptd, ptdk=ptdk)
                                      for c in range(16):
                                          P.add("pe", (lambda e, c=c: e.matmul(sq_[:, c * 16:(c + 1) * 16], lhsT=kdt[:, c * 128:(c + 1) * 128],
                                                                               rhs=qbd[:, s_ * 16:(s_ + 1) * 16], start=True, stop=True)), reads=[kdtk, "qbd"], writes=[sqk_])
                                      P.add("act", (lambda e: e.activation(out=ptd[:], in_=sq_[:, 0:256], func=AF.Exp, scale=SCALE)), reads=[sqk_], writes=[ptdk])
                                      P.add("dve", (lambda e: e.tensor_tensor(out=ptd[:], in0=ptd[:], in1=mdil[:], op=ALU.mult)), reads=[ptdk, "mdil"], writes=[ptdk])

                                  def B():
                                      vd, vdk, ptd, ptdk = st_["vd"], st_["vdk"], st_["ptd"], st_["ptdk"]
                                      for hd in range(2):
                                          for c in range(16):
                                              P.add("pe", (lambda e, hd=hd, c=c: e.matmul(ocA[:, hd * 128 + s_ * 8: hd * 128 + s_ * 8 + 8], lhsT=vd[:, c * 192 + hd * 64: c * 192 + hd * 64 + 128],
                                                                                           rhs=ptd[:, c * 16 + hd * 8: c * 16 + hd * 8 + 8], start=(c == 0), stop=(c == 15))),
                                                    reads=[vdk, vdk + "_ones", ptdk], writes=["oc#0"])
                                  return A1, A2, B

                              seqs = [mk_seq(s_) for s_ in range(16)]
                              LAGS = (1, 0)
                              for i_ in range(16 + LAGS[0] + LAGS[1]):
                                  if i_ < 16:
                                      seqs[i_][0]()
                                  if LAGS[0] <= i_ < 16 + LAGS[0]:
                                      seqs[i_ - LAGS[0]][1]()
                                  if i_ >= LAGS[0] + LAGS[1]:
                                      seqs[i_ - LAGS[0] - LAGS[1]][2]()
                              P.add("dve", lambda e: e.tensor_copy(out=accs[:, 0:256], in_=ocA[:, 0:256]), reads=["oc#0"], writes=["accs"])
                              new_keys(None, [(KB, "KB"), (KB, "KB")], QB, 2048, 4096, VN, "VN", [17 * 192, 17 * 192 + 64], mndil, "mndil", [(ocB, "oc#1", 0), (ocB, "oc#1", 128)])
                              P.add("dve", lambda e: e.tensor_tensor(out=accs[:, 0:256], in0=ocB[:, 0:256], in1=accs[:, 0:256], op=ALU.add), reads=["oc#1", "accs"], writes=["accs"])
                          P.flush()
                      sts = Ring("st", [ps("st%d" % i, 512, st=bst) for i in range(4)])
                      obs = Ring("ob", [ps("ob%d" % i, 512, st=bst) for i in range(2)])

                      def v_lhsT(vt, off, odd):
                          return vt[:, off:off + 128]

                      def unit_pair(qsrc, qkey, qcols, ksrcs, kcols2, vsrc, vkey, voffs2, mask, mkey, acc_cols, first):
                          pt, pk = pts.next()
                          for hd in range(2):
                              st, sk = sts.next()
                              kt, kkey = ksrcs[hd]
                              for blk in range(2):
                                  ks, kstep = kcols2[blk]
                                  P.add("pe", (lambda e, st=st, hd=hd, blk=blk, kt=kt, ks=ks, kstep=kstep: e.matmul(
                                      st[:, blk * 128: (blk + 1) * 128],
                                      lhsT=sap(kt, ks, [[kstep, 128]], nparts=64, p0=hd * 64),
                                      rhs=sap(qsrc, qcols[0], [[qcols[1], 128]], nparts=64, p0=hd * 64), start=True, stop=True)),
                                      reads=[kkey, qkey], writes=[sk])
                              P.add("act", (lambda e, st=st, pt=pt, hd=hd: e.activation(out=pt[:, hd * 256:(hd + 1) * 256], in_=st[:, 0:256], func=AF.Exp, scale=SCALE)),
                                    reads=[sk], writes=[pk])
                          if mask is not None:
                              P.add("dve", (lambda e, pt=pt: e.tensor_tensor(out=pt[:], in0=pt[:], in1=mask[:], op=ALU.mult)), reads=[pk, mkey], writes=[pk])

                          def stage2():
                              ob, ok = obs.next()
                              for hd in range(2):
                                  for blk in range(2):
                                      P.add("pe", (lambda e, ob=ob, pt=pt, hd=hd, blk=blk: e.matmul(
                                          ob[:, hd * 128:(hd + 1) * 128], lhsT=v_lhsT(vsrc, voffs2[hd][blk], hd == 1),
                                          rhs=pt[:, hd * 256 + blk * 128: hd * 256 + (blk + 1) * 128], start=(blk == 0), stop=(blk == 1))),
                                          reads=[vkey, vkey + "_ones", pk], writes=[ok])
                              a0, astep = acc_cols
                              dst = sap(acc, a0, [[2176, 2], [astep, 128]])
                              src = sap(ob, 0, [[128, 2], [1, 128]])
                              if first:
                                  P.add("dve", (lambda e: e.tensor_copy(out=dst, in_=src)), reads=[ok], writes=["acc"])
                              else:
                                  P.add("dve", (lambda e: e.tensor_tensor(out=dst, in0=src, in1=dst, op=ALU.add)), reads=[ok, "acc"], writes=["acc"])
                          pend.append(stage2)
                          if len(pend) > 2:
                              pend.pop(0)()

                      pend = []

                      def drain():
                          while pend:
                              pend.pop(0)()

                      def finish_pair(chunk, c0, n, sink_heads=None):
                          if sink_heads is not None:
                              he, ho = sink_heads
                              P.add("act", (lambda e: e.activation(out=sap(acc, c0, [[1, n]], nparts=64, p0=64), in_=sap(acc, c0, [[1, n]], nparts=64, p0=64),
                                                                   func=AF.Identity, bias=esink[64:128, he:he + 1])), reads=["acc", "esink"], writes=["acc"])
                              P.add("act", (lambda e: e.activation(out=sap(acc, 2176 + c0, [[1, n]], nparts=64, p0=0), in_=sap(acc, 2176 + c0, [[1, n]], nparts=64, p0=0),
                                                                   func=AF.Identity, bias=esink[0:64, ho:ho + 1])), reads=["acc", "esink"], writes=["acc"])
                          P.add("dve", (lambda e: e.reciprocal(out=sap(rl, c0, [[1, n]], nparts=64, p0=0), in_=sap(acc, c0, [[1, n]], nparts=64, p0=64))),
                                reads=["acc"], writes=["rl"])
                          P.add("dve", (lambda e: e.reciprocal(out=sap(rl, c0, [[1, n]], nparts=64, p0=64), in_=sap(acc, 2176 + c0, [[1, n]], nparts=64, p0=0))),
                                reads=["acc"], writes=["rl"])
                          P.add("dve", (lambda e: e.tensor_tensor(out=sap(catT, chunk * CT + c0, [[1, n]], nparts=64, p0=0), in0=sap(acc, c0, [[1, n]], nparts=64, p0=0),
                                                                  in1=sap(rl, c0, [[1, n]], nparts=64, p0=0), op=ALU.mult)), reads=["acc", "rl"], writes=["catT"])
                          P.add("dve", (lambda e: e.tensor_tensor(out=sap(catT, chunk * CT + c0, [[1, n]], nparts=64, p0=64), in0=sap(acc, 2176 + c0, [[1, n]], nparts=64, p0=64),
                                                                  in1=sap(rl, c0, [[1, n]], nparts=64, p0=64), op=ALU.mult)), reads=["acc", "rl"], writes=["catT"])

                      if pas == 0:
                          for cpair in range(3):
                              he, ho = 2 * cpair, 2 * cpair + 1
                              ksrcs = []
                              for h in (he, ho):
                                  kv = h // 3
                                  ksrcs.append((KAx, "KAx") if kv == (h % 2) else (KAy, "KAy"))
                              for j in range(16):
                                  voffs2 = [[(j + blk) * 320 + {(0, 0): 0, (1, 1): 64, (0, 1): 128, (1, 0): 192}[(h % 2, h // 3)] for blk in range(2)] for h in (he, ho)]
                                  m, mk_ = (mfirst, "mfirst") if j == 0 else (mstd, "mstd")
                                  unit_pair(QA, "QA", (cpair * 2176 + j * 128, 1), ksrcs, [(j * 128, 1), (128 + j * 128, 1)],
                                            VN, "VN", voffs2, m, mk_, (j * 128, 1), True)
                              drain()
                              P.add("dve", (lambda e, cpair=cpair: e.tensor_copy(out=sap(acc, 2048, [[2176, 2], [1, 128]]), in_=sap(accs, cpair * 256, [[128, 2], [1, 128]]))),
                                    reads=["accs"], writes=["acc"])
                              finish_pair(cpair, 0, 2176, sink_heads=(he, ho))
                          for cpair in range(2):
                              for j in range(16):
                                  voffs2 = [[blk * 384 + cpair * 192 + hd * 64 for blk in range(2)] for hd in range(2)]
                                  unit_pair(QX, "QX", (cpair * 2176 + j * 128, 1), [(MKT, "MKT"), (MKT, "MKT")],
                                            [(cpair * 256, 1), (cpair * 256 + 128, 1)], MV, "MV", voffs2, None, None, (j * 128, 1), True)
                              drain()
                              P.add("dve", (lambda e, cpair=cpair: e.tensor_copy(out=sap(acc, 2048, [[2176, 2], [1, 128]]), in_=sap(accx, cpair * 256, [[128, 2], [1, 128]]))),
                                    reads=["accx"], writes=["acc"])
                              finish_pair(6 + cpair, 0, 2176)
                      else:
                          p_ = pas - 1
                          kb2 = [(KB, "KB"), (KB, "KB")]
                          for j in range(16):
                              voffs2 = [[(j + blk) * 192 + hd * 64 for blk in range(2)] for hd in range(2)]
                              m, mk_ = (mfirst, "mfirst") if j == 0 else (mstd, "mstd")
                              unit_pair(QB, "QB", (j * 128, 1), kb2, [(2048 + (j - 1) * 128, 1), (2048 + j * 128, 1)],
                                        VN, "VN", voffs2, m, mk_, (j * 128, 1), True)
                          for n in range(4):
                              for c in range(4):
                                  voffs2 = [[((n + blk) * 4 + c) * 192 + hd * 64 for blk in range(2)] for hd in range(2)]
                                  m, mk_ = (mfirst, "mfirst") if n == 0 else (mstd, "mstd")
                                  unit_pair(QB, "QB", (n * 512 + c, 4), kb2, [(2048 + (n - 1) * 512 + c, 4), (2048 + n * 512 + c, 4)],
                                            V4, "V4", voffs2, m, mk_, (n * 512 + c, 4), False)
                          for c in range(16):
                              voffs2 = [[(blk * 16 + c) * 192 + hd * 64 for blk in range(2)] for hd in range(2)]
                              unit_pair(QB, "QB", (c, 16), kb2, [(c, 16), (2048 + c, 16)], V16, "V16", voffs2, mfirst, "mfirst", (c, 16), False)
                          drain()
                          P.add("dve", lambda e: e.tensor_copy(out=sap(acc, 2048, [[2176, 2], [1, 128]]), in_=sap(accs, 0, [[128, 2], [1, 128]])), reads=["accs"], writes=["acc"])
                          finish_pair(3 + p_, 0, 2176)
                      P.flush()


            except _StopBuild:
                P.flush()
                break

        if STOP[0] > 8:
            with contextlib.ExitStack() as cst:
                NTGM = 640
                while pending_cp:
                    maybe_copy(force=True)
                x1 = sb("x1", 5 * 1024, F32, st=cst)
                hT = sb("hT", 8 * NTGM, st=cst)
                actT = sb("actT", NFC * NTGM, st=cst)
                wd = sb("wd", NFC * 1024, st=cst)
                wo = sb("wo", 8 * 1024, st=cst)
                wgs = Ring("wg", [sb("wg%d" % i, 2048, st=cst) for i in range(2)])
                wus = Ring("wu", [sb("wu%d" % i, 2048, st=cst) for i in range(2)])
                gpost = sb("gpost", D, F32, st=cst); gffn = sb("gffn", D, F32, st=cst); gpffn = sb("gpffn", D, F32, st=cst)
                xts = Ring("xt", [sb("cxt%d" % i, D, F32, st=cst) for i in range(1)])
                ysbs = Ring("ysb", [sb("ysb%d" % i, D, F32, st=cst) for i in range(1)])
                hbs = Ring("hb", [sb("chb%d" % i, D, st=cst) for i in range(2)])
                junk = sb("cjunk", D, st=cst)
                sgs = Ring("sg", [sb("sg%d" % i, 512, F32, st=cst) for i in range(2)])
                ssa = Ring("ssa", [sb("ssa%d" % i, 2, F32, st=cst) for i in range(2)])
                sss = Ring("ss", [sb("css%d" % i, 1, F32, st=cst) for i in range(2)])
                rss = Ring("rs", [sb("crs%d" % i, 1, F32, st=cst) for i in range(2)])
                yps = Ring("yp", [ps("yp%d" % i, 512, st=cst) for i in range(4)])
                gps = Ring("gp", [ps("gp%d" % i, 512, st=cst) for i in range(3)])
                ups = gps
                pTs = Ring("pT", [ps("cpT%d" % i, 1024, BF16, st=cst) for i in range(1)])
                P.add("sp", lambda e: e.dma_start(out=sap(wo, 0, [[1024, 8], [1, 1024]]), in_=dap(wob, 0, [[1024, 128], [128 * 1024, 8], [1, 1024]])),
                      reads=["wob"], writes=["wo"], dma_sem="wo")
                P.add("sp", lambda e: e.dma_start(out=sap(wd, 0, [[1024, NFC], [1, 1024]]), in_=dap(wdb, 0, [[1024, 128], [128 * 1024, NFC], [1, 1024]])),
                      reads=["wdb"], writes=["wd"], dma_sem="wd")
                for gt, gsrc, gk in ((gpost, g_post, "gpost"), (gffn, g_ffn, "gffn"), (gpffn, g_pffn, "gpffn")):
                    P.add("sp", (lambda e, gt=gt, gsrc=gsrc: e.dma_start(out=gt[:], in_=dap(gsrc, 0, [[0, 128], [1, D]]))), writes=[gk], dma_sem=gk)

                def sumsq(src_aps, src_keys, ss, sk):
                    sa, sak = ssa.next()
                    for i, (ap_, k_) in enumerate(zip(src_aps, src_keys)):
                        P.add("act", (lambda e, ap_=ap_, i=i, sa=sa: e.activation(out=junk[:, 0:512], in_=ap_, func=AF.Square, accum_out=sa[:, i:i + 1])),
                              reads=[k_], writes=["cjunk", sak])
                    P.add("dve", (lambda e, sa=sa, ss=ss: e.tensor_tensor(out=ss[:], in0=sa[:, 0:1], in1=sa[:, 1:2], op=ALU.add)), reads=[sak], writes=[sk])

                groups = [[0, 1, 2, 3], [4, 5, 6, 7], [8, 9, 10, 11], [12, 13, 14, 15, 16]]
                for grp in groups:
                    ntile = len(grp)
                    NTG = ntile * 128
                    for slot, T in enumerate(grp):
                        xsrc, xrow = (xo, T * 128) if T < 16 else (xs, 0)
                        ccol = T * 128
                        ypa, yka = yps.next(); ypb, ykb = yps.next()
                        for eh, yp in ((0, ypa), (1, ypb)):
                            for k in range(8):
                                P.add("pe", (lambda e, yp=yp, eh=eh, k=k, ccol=ccol: e.matmul(yp[:, 0:512], lhsT=catT[:, k * CT + ccol: k * CT + ccol + 128],
                                                                                               rhs=wo[:, k * 1024 + eh * 512: k * 1024 + (eh + 1) * 512], start=(k == 0), stop=(k == 7))),
                                      reads=["catT", "wo"], writes=[yka if eh == 0 else ykb])
                        ss, sk = sss.next(); rs, rk = rss.next()
                        sumsq([ypa[:, 0:512], ypb[:, 0:512]], [yka, ykb], ss, sk)
                        rstd_ops(ss, sk, rs, rk)
                        xt, xk = xts.next()
                        P.add("sp", (lambda e, xt=xt, xsrc=xsrc, xrow=xrow: e.dma_start(out=xt[:], in_=xsrc[xrow:xrow + 128, :])), writes=[xk], dma_sem=xk)
                        x1k = "x1_%d" % slot
                        for eh, yp, yk in ((0, ypa, yka), (1, ypb, ykb)):
                            P.add("dve", (lambda e, yp=yp, eh=eh, rs=rs, slot=slot: e.scalar_tensor_tensor(
                                out=x1[:, slot * 1024 + eh * 512: slot * 1024 + (eh + 1) * 512], in0=yp[:, 0:512], scalar=rs[:, 0:1],
                                in1=gpost[:, eh * 512:(eh + 1) * 512], op0=ALU.mult, op1=ALU.mult)), reads=[yk, rk, "gpost"], writes=[x1k])
                        P.add("pool", (lambda e, xt=xt, slot=slot: e.tensor_tensor(out=x1[:, slot * 1024:(slot + 1) * 1024], in0=x1[:, slot * 1024:(slot + 1) * 1024],
                                                                                   in1=xt[:], op=ALU.add)), reads=[x1k, xk], writes=[x1k])
                        ss2, sk2 = sss.next(); rs2, rk2 = rss.next(); hb, hk = hbs.next(); pT, pk = pTs.next()
                        sumsq([x1[:, slot * 1024: slot * 1024 + 512], x1[:, slot * 1024 + 512:(slot + 1) * 1024]], [x1k, x1k], ss2, sk2)
                        rstd_ops(ss2, sk2, rs2, rk2)
                        P.add("dve", (lambda e, hb=hb, rs2=rs2, slot=slot: e.scalar_tensor_tensor(out=hb[:], in0=x1[:, slot * 1024:(slot + 1) * 1024], scalar=rs2[:, 0:1],
                                                                                                  in1=gffn[:], op0=ALU.mult, op1=ALU.mult)), reads=[x1k, rk2, "gffn"], writes=[hk])
                        for dc in range(8):
                            P.add("pe", (lambda e, pT=pT, hb=hb, dc=dc: e.transpose(out=pT[:, dc * 128:(dc + 1) * 128], in_=hb[:, dc * 128:(dc + 1) * 128], identity=ident[:])),
                                  reads=[hk, "ident"], writes=[pk])
                        P.add("act", (lambda e, pT=pT, slot=slot, NTG=NTG: e.activation(out=sap(hT, slot * 128, [[NTG, 8], [1, 128]]), in_=sap(pT, 0, [[128, 8], [1, 128]]), func=AF.Copy)),
                              reads=[pk], writes=["hT"])
                    tbs = [(0, 512)] if ntile == 4 else [(0, 512), (512, 128)]
                    for fc in range(NFC):
                        if fc % 2 == 0:
                            nf = min(2, NFC - fc) * 128
                            wg, wgk = wgs.next(); wu, wuk = wus.next()
                            P.add("sp", (lambda e, wg=wg, fc=fc, nf=nf: e.dma_start(out=sap(wg, 0, [[256, 8], [1, nf]]), in_=dap(wgb, fc * 128, [[DFF, 128], [128 * DFF, 8], [1, nf]]))),
                                  reads=["wgb"], writes=[wgk], dma_sem=wgk)
                            P.add("sp", (lambda e, wu=wu, fc=fc, nf=nf: e.dma_start(out=sap(wu, 0, [[256, 8], [1, nf]]), in_=dap(wub, fc * 128, [[DFF, 128], [128 * DFF, 8], [1, nf]]))),
                                  reads=["wub"], writes=[wuk], dma_sem=wuk)
                        fo = (fc % 2) * 128
                        for (tb0, n) in tbs:
                            gp, gk = gps.next(); up, uk = ups.next(); sg, sgk = sgs.next()
                            for wt, wk_, pp, ppk in ((wg, wgk, gp, gk), (wu, wuk, up, uk)):
                                for dc in range(8):
                                    P.add("pe", (lambda e, wt=wt, pp=pp, dc=dc, tb0=tb0, n=n, NTG=NTG, fo=fo: e.matmul(pp[:, 0:n], lhsT=wt[:, dc * 256 + fo: dc * 256 + fo + 128],
                                                                                                               rhs=hT[:, dc * NTG + tb0: dc * NTG + tb0 + n], start=(dc == 0), stop=(dc == 7))),
                                          reads=[wk_, "hT"], writes=[ppk])
                            P.add("act", (lambda e, gp=gp, sg=sg, n=n: e.activation(out=sg[:, 0:n], in_=gp[:, 0:n], func=AF.Silu)), reads=[gk], writes=[sgk])
                            P.add("dve", (lambda e, up=up, sg=sg, fc=fc, tb0=tb0, n=n, NTG=NTG: e.tensor_tensor(out=actT[:, fc * NTG + tb0: fc * NTG + tb0 + n], in0=up[:, 0:n],
                                                                                                              in1=sg[:, 0:n], op=ALU.mult)), reads=[uk, sgk], writes=["actT"])
                    for slot, T in enumerate(grp):
                        ypa, yka = yps.next(); ypb, ykb = yps.next()
                        for eh, yp in ((0, ypa), (1, ypb)):
                            for fc in range(NFC):
                                P.add("pe", (lambda e, yp=yp, eh=eh, fc=fc, slot=slot, NTG=NTG: e.matmul(yp[:, 0:512], lhsT=actT[:, fc * NTG + slot * 128: fc * NTG + (slot + 1) * 128],
                                                                                                        rhs=wd[:, fc * 1024 + eh * 512: fc * 1024 + (eh + 1) * 512], start=(fc == 0), stop=(fc == NFC - 1))),
                                      reads=["actT", "wd"], writes=[yka if eh == 0 else ykb])
                        ss, sk = sss.next(); rs, rk = rss.next(); ysb, ysk = ysbs.next()
                        sumsq([ypa[:, 0:512], ypb[:, 0:512]], [yka, ykb], ss, sk)
                        rstd_ops(ss, sk, rs, rk)
                        x1k = "x1_%d" % slot
                        for eh, yp, yk in ((0, ypa, yka), (1, ypb, ykb)):
                            P.add("dve", (lambda e, yp=yp, eh=eh, rs=rs, ysb=ysb: e.scalar_tensor_tensor(
                                out=ysb[:, eh * 512:(eh + 1) * 512], in0=yp[:, 0:512], scalar=rs[:, 0:1],
                                in1=gpffn[:, eh * 512:(eh + 1) * 512], op0=ALU.mult, op1=ALU.mult)), reads=[yk, rk, "gpffn"], writes=[ysk])
                        P.add("pool", (lambda e, ysb=ysb, slot=slot: e.tensor_tensor(out=ysb[:], in0=ysb[:], in1=x1[:, slot * 1024:(slot + 1) * 1024], op=ALU.add)),
                              reads=[ysk, x1k], writes=[ysk])
                        if T < 16:
                            P.add("sp", (lambda e, ysb=ysb, T=T: e.dma_start(out=yo[T * 128:(T + 1) * 128, :], in_=ysb[:])), reads=[ysk], dma_sem=ysk)
                        else:
                            P.add("sp", (lambda e, ysb=ysb: e.dma_start(out=ys[:, :], in_=ysb[:])), reads=[ysk], dma_sem=ysk)
                P.flush()
        P.flush(final=True)
    return nc


def _consts(hf):
    f32 = np.float32
    c = {}
    c["c_ident"] = np.eye(128, dtype=f32)
    R = np.zeros((128, 128), f32)
    for pp in range(128):
        if (pp % 64) < 32:
            R[pp + 32, pp] = -1.0
        else:
            R[pp - 32, pp] = 1.0
    c["c_rot"] = R
    inv = np.power(f32(10000.0), -(np.arange(32, dtype=f32) / f32(32))).astype(f32)
    pos = np.concatenate([(hf - 1) * 2048 + np.arange(2048), hf * 2048 + np.arange(2048), PAST + (np.arange(128) % 8)]).astype(f32)
    ang = (pos[:, None] * inv[None, :]).astype(f32)
    fidx = (np.arange(128) % 64) % 32
    c["c_cos"] = np.ascontiguousarray(np.cos(ang).astype(f32)[:, fidx].T)
    c["c_sin"] = np.ascontiguousarray(np.sin(ang).astype(f32)[:, fidx].T)
    kk = np.arange(128)[:, None]
    qq = np.arange(128)[None, :]
    prev = (kk >= qq).astype(f32)
    diag = (kk <= qq).astype(f32)
    c["c_mstd"] = np.concatenate([prev, diag, prev, diag], axis=1)
    c["c_mfirst"] = np.concatenate([prev * f32(hf), diag, prev * f32(hf), diag], axis=1)
    i8 = np.arange(8)
    msw = (np.arange(128)[:, None] >= i8[None, :]).astype(f32)
    c["c_msw"] = np.ascontiguousarray(np.tile(msw[:, None, :], (1, 96, 1)).reshape(128, 768))
    s_ = np.arange(128) // 8
    j_ = np.arange(128) % 8
    same = (s_[:, None] == s_[None, :])
    dji = j_[None, :] - j_[:, None]
    c["c_mnsw"] = (same & (dji >= 0)).astype(f32)
    mult_new = (dji >= 0).astype(f32) + ((dji >= 0) & (dji % 4 == 0)).astype(f32) + (dji == 0).astype(f32)
    c["c_mndil"] = (same.astype(f32) * mult_new).astype(f32)
    g = np.arange(128)[:, None, None]
    cc = np.arange(16)[None, :, None]
    ii = np.arange(8)[None, None, :]
    p3 = (cc == ii)
    p2 = ((cc % 4) == (ii % 4)) & ((g > 96) | ((g == 96) & (cc >= ii)))
    p1 = (g > 120) | ((g == 120) & (cc >= ii))
    mult = p3.astype(f32) + p2.astype(f32) + p1.astype(f32)
    c["c_mdil"] = np.ascontiguousarray(np.tile(mult[:, :, None, :], (1, 1, 2, 1)).reshape(128, 256))
    c["c_nhalf"] = np.full((128, 1), -0.5, f32)
    return c


_NC_CACHE = {}


def kernel(x_prompt, x_sample, cache_swa_k, cache_swa_v, cache_dil_k, cache_dil_v,
           cache_mem_k, cache_mem_v, mem_prompt, g_pre_mix, w_in, sinks, w_mem_kv, w_o,
           g_post_mix, g_pre_ffn, w_gate, w_up, w_down, g_post_ffn):
    f32 = np.float32
    A = lambda a: np.ascontiguousarray(np.asarray(a, dtype=f32))
    x_prompt = A(x_prompt); x_sample = A(x_sample)
    shared = {
        "w_in": A(w_in)[0], "w_mem": A(w_mem_kv)[0], "w_o": A(w_o)[0],
        "w_gate": A(w_gate)[0], "w_up": A(w_up)[0], "w_down": A(w_down)[0],
        "g_pre": A(g_pre_mix).reshape(1, D), "g_post": A(g_post_mix).reshape(1, D),
        "g_ffn": A(g_pre_ffn).reshape(1, D), "g_pffn": A(g_post_ffn).reshape(1, D),
        "sinks": A(sinks).reshape(1, 6),
    }
    csk = A(cache_swa_k)[0].reshape(128, 128, 128); csv = A(cache_swa_v)[0].reshape(128, 128, 128)
    cdk = A(cache_dil_k)[0].reshape(128, 2048, 384); cdv = A(cache_dil_v)[0].reshape(128, 2048, 384)
    cmk = A(cache_mem_k)[0].reshape(128, 256, 256); cmv = A(cache_mem_v)[0].reshape(128, 256, 256)
    mem_prompt = A(mem_prompt)
    cst = [_consts(0), _consts(1)]
    in_maps = []
    for c in range(8):
        b, hf = c // 2, c % 2
        m = dict(shared)
        m["xo"] = x_prompt[b, hf * 2048:(hf + 1) * 2048]
        m["xh"] = x_prompt[b, 0:2048] if hf == 1 else np.zeros((2048, D), f32)
        m["xs"] = x_sample[c * 16:(c + 1) * 16].reshape(128, D)
        m["memp"] = mem_prompt[b]
        sl = slice(c * 16, (c + 1) * 16)
        m["csk"] = csk[sl]; m["csv"] = csv[sl]; m["cdk"] = cdk[sl]; m["cdv"] = cdv[sl]
        m["cmk"] = cmk[sl]; m["cmv"] = cmv[sl]
        m.update(cst[hf])
        in_maps.append(m)
    if "nc" not in _NC_CACHE:
        _NC_CACHE["nc"] = build()
    res = run_bass_kernel_spmd(_NC_CACHE["nc"], in_maps, core_ids=list(range(8)))
    R = res.results
    y_prompt = np.zeros((4, 4096, D), f32); y_sample = np.zeros((128, 8, D), f32)
    p_swa_k = np.zeros((1, 4, 128, 2, 64), f32); p_swa_v = np.zeros_like(p_swa_k)
    p_dil_k = np.zeros((1, 4, 2048, 6, 64), f32); p_dil_v = np.zeros_like(p_dil_k)
    p_mem_k = np.zeros((1, 4, 256, 4, 64), f32); p_mem_v = np.zeros_like(p_mem_k)
    s_swa_k = np.zeros((1, 128, 128, 2, 64), f32); s_swa_v = np.zeros_like(s_swa_k)
    s_dil_k = np.zeros((1, 128, 2048, 6, 64), f32); s_dil_v = np.zeros_like(s_dil_k)
    for c in range(8):
        b, hf = c // 2, c % 2
        r = R[c]
        y_prompt[b, hf * 2048:(hf + 1) * 2048] = r["yo"]
        y_sample[c * 16:(c + 1) * 16] = r["ys"].reshape(16, 8, D)
        if hf == 1:
            p_swa_k[0, b] = r["o_swa_k"].reshape(128, 2, 64); p_swa_v[0, b] = r["o_swa_v"].reshape(128, 2, 64)
            p_dil_k[0, b] = r["o_dil_k"].reshape(2048, 6, 64); p_dil_v[0, b] = r["o_dil_v"].reshape(2048, 6, 64)
        else:
            p_mem_k[0, b] = r["o_mem_k"].reshape(256, 4, 64); p_mem_v[0, b] = r["o_mem_v"].reshape(256, 4, 64)
        sl = slice(c * 16, (c + 1) * 16)
        s_swa_k[0, sl] = r["s_swa_k"].reshape(16, 128, 2, 64); s_swa_v[0, sl] = r["s_swa_v"].reshape(16, 128, 2, 64)
        s_dil_k[0, sl] = r["s_dil_k"].reshape(16, 2048, 6, 64); s_dil_v[0, sl] = r["s_dil_v"].reshape(16, 2048, 6, 64)
    return (y_prompt, y_sample, p_swa_k, p_swa_v, p_dil_k, p_dil_v, p_mem_k, p_mem_v,
            s_swa_k, s_swa_v, s_dil_k, s_dil_v)
```

```python
import contextlib
import numpy as np
import concourse.bass as bass
import concourse.mybir as mybir
from concourse.ap import AP
from concourse.bass_utils import run_bass_kernel_spmd

F32 = mybir.dt.float32
BF16 = mybir.dt.bfloat16
AF = mybir.ActivationFunctionType
ALU = mybir.AluOpType

ENGS = ("pe", "act", "dve", "pool", "sp")
PSUM_RINGS = {"pT", "zp", "rp", "vp", "st", "ob", "sq", "oc", "on", "yp", "gp", "up", "tp"}
D = 1024
DFF = 2816
NFC = 22
EPS = 1e-6
SCALE = 0.125
NOWN = 2048
NS = 128
PAST = 16384


class Op:
    __slots__ = ("eng", "fn", "deps", "is_dma", "sem", "sem_target", "signal", "sig_idx", "emitted")

    def __init__(self, eng, fn, is_dma):
        self.eng = eng
        self.fn = fn
        self.deps = ()
        self.is_dma = is_dma
        self.sem = None
        self.sem_target = 0
        self.signal = False
        self.sig_idx = 0
        self.emitted = False


class Prog:
    def __init__(self, nc):
        self.nc = nc
        self.seg = {e: [] for e in ENGS}
        self.state = {}
        self.dma_sems = {}
        self.last_dma = {}
        self.last_compute = {}
        self.fence_deps = set()
        self.esems = {e: nc.alloc_semaphore("s_" + e) for e in ENGS}
        self.sigcount = {e: 0 for e in ENGS}
        self.waited = {e: {} for e in ENGS}

    def add(self, eng, fn, reads=(), writes=(), dma_sem=None, nofence=False):
        if _STOPPED[0]:
            return None
        op = Op(eng, fn, dma_sem is not None)
        if dma_sem is not None:
            ent = self.dma_sems.get(dma_sem)
            if ent is None:
                ent = [self.nc.alloc_semaphore("d_" + str(len(self.dma_sems))), 0]
                self.dma_sems[dma_sem] = ent
            ent[1] += 16
            op.sem = dma_sem
            op.sem_target = ent[1]
            if not nofence:
                self.last_dma[dma_sem] = op
        else:
            self.last_compute[eng] = op
        xr = [k for k in reads if k.split("#")[0] in PSUM_RINGS]
        if xr:
            reads = [k for k in reads if k not in xr]
            writes = list(writes) + xr
        deps = set(self.fence_deps)
        for k in reads:
            s = self.state.get(k)
            if s is None:
                s = [None, []]
                self.state[k] = s
            if s[0] is not None:
                deps.add(s[0])
            s[1].append(op)
        for k in writes:
            s = self.state.get(k)
            if s is None:
                s = [None, []]
                self.state[k] = s
            if s[0] is not None:
                deps.add(s[0])
            deps.update(s[1])
            s[0] = op
            s[1] = []
        deps.discard(op)
        op.deps = deps
        self.seg[eng].append(op)
        return op

    @staticmethod
    def _skip(d, op):
        return (not d.is_dma) and d.eng == op.eng and (not op.is_dma) and d.eng == "pe"

    def flush(self, final=False):
        nc = self.nc
        for e in ENGS:
            for op in self.seg[e]:
                for d in op.deps:
                    if d.is_dma or d.emitted or self._skip(d, op):
                        continue
                    d.signal = True
        for e in ENGS:
            lc = self.last_compute.get(e)
            if lc is not None and not lc.emitted:
                lc.signal = True
        for e in ENGS:
            for op in self.seg[e]:
                if op.signal and not op.is_dma:
                    self.sigcount[e] += 1
                    op.sig_idx = self.sigcount[e]
        with nc.Block() as block:
            engmap = {"pe": block.tensor, "act": block.scalar, "dve": block.vector,
                      "pool": block.gpsimd, "sp": block.sync}

            def make(e):
                def body(eng):
                    waited = self.waited[e]
                    for op in self.seg[e]:
                        need = {}
                        for d in op.deps:
                            if d.is_dma:
                                key = ("d", d.sem)
                                val = d.sem_target
                                h = self.dma_sems[d.sem][0]
                            else:
                                if self._skip(d, op):
                                    continue
                                if d.emitted and not d.signal:
                                    continue
                                key = ("e", d.eng)
                                val = d.sig_idx
                                h = self.esems[d.eng]
                            if waited.get(key, 0) >= val:
                                continue
                            if key not in need or need[key][1] < val:
                                need[key] = (h, val)
                        for key, (h, val) in need.items():
                            eng.wait_ge(h, val)
                            waited[key] = val
                        ins = op.fn(eng)
                        if op.is_dma:
                            ins.then_inc(self.dma_sems[op.sem][0], 16)
                        elif op.signal:
                            ins.then_inc(self.esems[e], 1)
                    if final and e == "sp":
                        for k, ent in self.dma_sems.items():
                            eng.wait_ge(ent[0], ent[1])
                return body

            for e in ENGS:
                engmap[e](make(e))
        for e in ENGS:
            for op in self.seg[e]:
                op.emitted = True
                op.fn = None
            self.seg[e] = []
        f = set(self.last_compute.values())
        f.update(self.last_dma.values())
        self.fence_deps = f


class Ring:
    def __init__(self, name, tiles):
        self.name = name
        self.tiles = tiles
        self.i = 0

    def next(self):
        k = self.i % len(self.tiles)
        self.i += 1
        return self.tiles[k], "%s#%d" % (self.name, k)


STOP = [99]


class _StopBuild(Exception):
    pass


_STOPPED = [False]


def chk(x):
    if STOP[0] <= x:
        _STOPPED[0] = True


def build():
    _STOPPED[0] = False
    nc = bass.Bass("TRN2", target_bir_lowering=False)
    P = Prog(nc)

    def din(name, shape):
        return nc.dram_tensor(name, list(shape), F32, kind="ExternalInput").ap()

    def dout(name, shape):
        return nc.dram_tensor(name, list(shape), F32, kind="ExternalOutput").ap()

    xo = din("xo", [NOWN, D]); xh = din("xh", [NOWN, D]); xs = din("xs", [NS, D])
    memp = din("memp", [256, D])
    w_in = din("w_in", [D, 2048]); w_mem = din("w_mem", [D, 512]); w_o = din("w_o", [D, D])
    w_gate = din("w_gate", [D, DFF]); w_up = din("w_up", [D, DFF]); w_down = din("w_down", [DFF, D])
    g_pre = din("g_pre", [1, D]); g_post = din("g_post", [1, D]); g_ffn = din("g_ffn", [1, D]); g_pffn = din("g_pffn", [1, D])
    sinks = din("sinks", [1, 6])
    csk = din("csk", [16, 128, 128]); csv = din("csv", [16, 128, 128])
    cdk = din("cdk", [16, 2048, 384]); cdv = din("cdv", [16, 2048, 384])
    cmk = din("cmk", [16, 256, 256]); cmv = din("cmv", [16, 256, 256])
    c_ident = din("c_ident", [128, 128]); c_rot = din("c_rot", [128, 128])
    c_cos = din("c_cos", [128, 4224]); c_sin = din("c_sin", [128, 4224])
    c_mstd = din("c_mstd", [128, 512]); c_mfirst = din("c_mfirst", [128, 512])
    c_msw = din("c_msw", [128, 768]); c_mnsw = din("c_mnsw", [128, 128])
    c_mdil = din("c_mdil", [128, 256]); c_mndil = din("c_mndil", [128, 128])
    c_nhalf = din("c_nhalf", [128, 1])

    yo = dout("yo", [NOWN, D]); ys = dout("ys", [NS, D])
    o_swa_k = dout("o_swa_k", [128, 128]); o_swa_v = dout("o_swa_v", [128, 128])
    o_dil_k = dout("o_dil_k", [NOWN, 384]); o_dil_v = dout("o_dil_v", [NOWN, 384])
    o_mem_k = dout("o_mem_k", [256, 256]); o_mem_v = dout("o_mem_v", [256, 256])
    s_swa_k = dout("s_swa_k", [16, 128, 128]); s_swa_v = dout("s_swa_v", [16, 128, 128])
    s_dil_k = dout("s_dil_k", [16, 2048, 384]); s_dil_v = dout("s_dil_v", [16, 2048, 384])

    def dap(t, off, dims):
        return AP(t.tensor, off, [list(d) for d in dims])

    es = contextlib.ExitStack()
    with es:
        uid = [0]

        def sb(name, cols, dt=BF16, st=None):
            uid[0] += 1
            return (st or es).enter_context(nc.sbuf_tensor("%s_%d" % (name, uid[0]), [128, cols], dt))

        def ps(name, cols, dt=F32, st=None):
            uid[0] += 1
            return (st or es).enter_context(nc.psum_tensor("%s_%d" % (name, uid[0]), [128, cols], dt))

        def sap(t, off, dims, nparts=128, p0=0):
            W = t.shape[1]
            return AP(t, p0 * W + off, [[W, nparts]] + [list(d) for d in dims])

        ident = sb("ident", 128); rot = sb("rot", 128)
        mstd = sb("mstd", 512); mfirst = sb("mfirst", 512)
        msw = sb("msw", 768); mnsw = sb("mnsw", 128); mdil = sb("mdil", 256); mndil = sb("mndil", 128)
        nhalf = sb("nhalf", 1, F32)
        esink = sb("esink", 6, F32)
        catT = sb("catT", 8 * 2176)
        CT = 2176

        for dst, src, key in ((ident, c_ident, "ident"), (rot, c_rot, "rot"), (mstd, c_mstd, "mstd"),
                              (mfirst, c_mfirst, "mfirst"), (msw, c_msw, "msw"), (mnsw, c_mnsw, "mnsw"),
                              (mdil, c_mdil, "mdil"), (mndil, c_mndil, "mndil")):
            P.add("pool", (lambda e, d=dst, s=src: e.dma_start(out=d[:], in_=s[:, :])), writes=[key], dma_sem="c_" + key)
        P.add("sp", lambda e: e.dma_start(out=nhalf[:], in_=c_nhalf[:, :]), writes=["nhalf"], dma_sem="c_nhalf")
        P.add("sp", lambda e: e.dma_start(out=esink[:], in_=dap(sinks, 0, [[0, 128], [1, 6]])), writes=["esink"], dma_sem="c_sink")
        P.add("act", lambda e: e.activation(out=esink[:], in_=esink[:], func=AF.Exp), reads=["esink"], writes=["esink"])

        epsb = sb("epsb", 1, F32)
        P.add("pool", lambda e: e.memset(epsb[:], EPS), writes=["epsb"])

        def rstd_ops(ss, sk, rs, rk):
            P.add("act", (lambda e: e.activation(out=rs[:], in_=ss[:], func=AF.Sqrt, scale=1.0 / D, bias=epsb[:, 0:1])), reads=[sk, "epsb"], writes=[rk])
            P.add("dve", (lambda e: e.reciprocal(out=rs[:], in_=rs[:])), reads=[rk], writes=[rk])

        identf = sb("identf", 128, F32)
        P.add("sp", lambda e: e.dma_start(out=identf[:], in_=c_ident[:, :]), writes=["identf"], dma_sem="c_identf")
        pending_cp = []
        for s_ in range(16):
            pending_cp.append((s_dil_k, cdk, s_, "cp_dk"))
            pending_cp.append((s_dil_v, cdv, s_, "cp_dv"))
        cp_ctr = [0]

        def maybe_copy(force=False):
            cp_ctr[0] += 1
            if pending_cp and (force or cp_ctr[0] % 4 == 0):
                dst, src, s_, key = pending_cp.pop(0)
                P.add("act", (lambda e: e.dma_start(out=dst[s_, 0:2040, :], in_=src[s_, 8:2048, :])), dma_sem=key, nofence=True)

        wob = nc.dram_tensor("wob", [D, D], BF16, kind="Internal").ap()
        wgb = nc.dram_tensor("wgb", [D, DFF], BF16, kind="Internal").ap()
        wub = nc.dram_tensor("wub", [D, DFF], BF16, kind="Internal").ap()
        wdb = nc.dram_tensor("wdb", [DFF, D], BF16, kind="Internal").ap()

        def cast_weights():
            P.add("pool", lambda e: e.dma_start(out=wob[:, :], in_=w_o[:, :]), writes=["wob"], dma_sem="wob")
            P.add("pool", lambda e: e.dma_start(out=wdb[:, :], in_=w_down[:, :]), writes=["wdb"], dma_sem="wdb")
            P.add("pool", lambda e: e.dma_start(out=dap(wgb, 0, [[DFF, D], [1408, 2], [1, 1408]]), in_=dap(w_gate, 0, [[DFF, D], [1408, 2], [1, 1408]])), writes=["wgb"], dma_sem="wgb")
            P.add("pool", lambda e: e.dma_start(out=dap(wub, 0, [[DFF, D], [1408, 2], [1, 1408]]), in_=dap(w_up, 0, [[DFF, D], [1408, 2], [1, 1408]])), writes=["wub"], dma_sem="wub")
        P.add("act", lambda e: e.dma_start(out=s_swa_k[:, 0:120, :], in_=csk[:, 8:128, :]), dma_sem="cp_sk", nofence=True)
        P.add("act", lambda e: e.dma_start(out=s_swa_v[:, 0:120, :], in_=csv[:, 8:128, :]), dma_sem="cp_sv", nofence=True)

        P.flush()
        for pas in range(4):
            if STOP[0] <= 2 * pas:
                break
            try:
              with contextlib.ExitStack() as pst:
                  if pas == 0:
                      QA = sb("QA", 3 * 2176, st=pst)
                      KAx = sb("KAx", 2304, st=pst); KAy = sb("KAy", 2304, st=pst)
                      QX = sb("QX", 2 * 2176, st=pst)
                      VN = sb("VA", 18 * 320, st=pst)
                      MKT = sb("MKT", 512, st=pst)
                      MV = sb("MV", 2 * 384, st=pst)
                      vbufs = [(VN, "VN")]
                      vmem = [(MV, "MV")]
                  else:
                      QB = sb("QB", 2176, st=pst)
                      KB = sb("KB", 4224, st=pst)
                      VN = sb("V1", 18 * 192, st=pst)
                      V4 = sb("V4", 20 * 192, st=pst)
                      V16 = sb("V16", 32 * 192, st=pst)
                      vbufs = [(VN, "VN"), (V4, "V4"), (V16, "V16")]
                      vmem = []
                  if pas == 0:
                      P.add("pool", lambda e: e.memset(sap(VN, 64, [[320, 18], [128, 2], [1, 64]]), 1.0), writes=["VN_ones"])
                      P.add("pool", lambda e: e.memset(sap(MV, 64, [[192, 4], [1, 64]]), 1.0), writes=["MV_ones"])
                  else:
                      for vt, vk in vbufs:
                          P.add("pool", (lambda e, t=vt: e.memset(sap(t, 64, [[192, t.shape[1] // 192], [1, 64]]), 1.0)), writes=[vk + "_ones"])

                  with contextlib.ExitStack() as ast:
                      wcols = 1024 if pas == 0 else 384
                      wb = sb("wb", 8 * wcols, st=ast)
                      uT = sb("uT", 8 * 2048, st=ast)
                      gpre = sb("gpre", D, F32, st=ast)
                      xts = Ring("xt", [sb("xt%d" % i, D, F32, st=ast) for i in range(3)])
                      hbs = Ring("hb", [sb("hb%d" % i, D, st=ast) for i in range(3)])
                      junk = sb("junk", D, st=ast)
                      sss = Ring("ss", [sb("ss%d" % i, 1, F32, st=ast) for i in range(3)])
                      rss = Ring("rs", [sb("rs%d" % i, 1, F32, st=ast) for i in range(3)])
                      coss = Ring("cos", [sb("cos%d" % i, 512, F32, st=ast) for i in range(2)])
                      sins = Ring("sin", [sb("sin%d" % i, 512, F32, st=ast) for i in range(2)])
                      zbs = Ring("zb", [sb("zb%d" % i, 512, st=ast) for i in range(2)])
                      t1s = Ring("t1", [sb("t1_%d" % i, 512, F32, st=ast) for i in range(2)])
                      t2s = Ring("t2", [sb("t2_%d" % i, 512, F32, st=ast) for i in range(2)])
                      ksts = Ring("kst", [sb("kst%d" % i, 512, st=ast) for i in range(2)])
                      pTs = Ring("pT", [ps("pT%d" % i, 1024, BF16, st=ast) for i in range(2)])
                      zps = Ring("zp", [ps("zp%d" % i, 512, st=ast) for i in range(2)])
                      rps = Ring("rp", [ps("rp%d" % i, 512, st=ast) for i in range(2)])
                      vps = Ring("vp", [ps("vp%d" % i, 512, st=ast) for i in range(2)])

                      P.add("sp", lambda e: e.dma_start(out=gpre[:], in_=dap(g_pre, 0, [[0, 128], [1, D]])), writes=["gpre"], dma_sem="gpre")

                      if pas == 0:
                          segs = [(0, 384, 0), (384, 128, 384), (448, 64, 512), (384, 64, 576), (1792, 256, 640), (512, 128, 896)]
                      else:
                          p_ = pas - 1
                          segs = [(640 + 128 * p_, 128, 0), (1024 + 128 * p_, 128, 128), (1408 + 128 * p_, 128, 256)]
                      for (sc, n, off) in segs:
                          P.add("pool", (lambda e, sc=sc, n=n, off=off: e.dma_start(
                              out=sap(wb, off, [[wcols, 8], [1, n]]),
                              in_=dap(w_in, sc, [[2048, 128], [128 * 2048, 8], [1, n]]))),
                              writes=["wb"], dma_sem="wb")

                      def norm_T(xsrc, row0, ntok, gate_key="uT"):
                          for t in range(ntok // 128):
                              xt, xk = xts.next(); hb, hk = hbs.next(); ss, sk = sss.next(); rs, rk = rss.next()
                              pT, pk = pTs.next()
                              r0 = row0 + t * 128
                              maybe_copy()
                              P.add("sp", (lambda e, xt=xt, r0=r0: e.dma_start(out=xt[:], in_=xsrc[r0:r0 + 128, :])), writes=[xk], dma_sem=xk)
                              P.add("act", (lambda e, xt=xt, ss=ss: e.activation(out=junk[:], in_=xt[:], func=AF.Square, accum_out=ss[:])),
                                    reads=[xk], writes=["junk", sk])
                              rstd_ops(ss, sk, rs, rk)
                              P.add("dve", (lambda e, xt=xt, rs=rs, hb=hb: e.scalar_tensor_tensor(out=hb[:], in0=xt[:], scalar=rs[:, 0:1], in1=gpre[:], op0=ALU.mult, op1=ALU.mult)),
                                    reads=[xk, rk, "gpre"], writes=[hk])
                              for dc in range(8):
                                  P.add("pe", (lambda e, pT=pT, hb=hb, dc=dc: e.transpose(out=pT[:, dc * 128:(dc + 1) * 128], in_=hb[:, dc * 128:(dc + 1) * 128], identity=ident[:])),
                                        reads=[hk, "ident"], writes=[pk])
                              P.add("act", (lambda e, pT=pT, t=t: e.activation(out=sap(uT, t * 128, [[2048, 8], [1, 128]]), in_=sap(pT, 0, [[128, 8], [1, 128]]), func=AF.Copy)),
                                    reads=[pk], writes=["uT%d" % (t // 4)])

                      def fm_proj(woff, dst, dkey, dcol, tb0, ntb, ctab0, rope):
                          zp, zk = zps.next()
                          for dc in range(8):
                              P.add("pe", (lambda e, zp=zp, dc=dc: e.matmul(zp[:, 0:ntb], lhsT=wb[:, dc * wcols + woff: dc * wcols + woff + 128],
                                                                             rhs=uT[:, dc * 2048 + tb0: dc * 2048 + tb0 + ntb], start=(dc == 0), stop=(dc == 7))),
                                    reads=["wb", "uT%d" % (tb0 // 512)], writes=[zk])
                          if not rope:
                              P.add("act", (lambda e, zp=zp: e.activation(out=dst[:, dcol:dcol + ntb], in_=zp[:, 0:ntb], func=AF.Copy)),
                                    reads=[zk], writes=[dkey])
                              return
                          zb, zbk = zbs.next(); rp, rk = rps.next(); t1, t1k = t1s.next(); t2, t2k = t2s.next()
                          cs, ck = cur_tab["cos"]; sn, snk = cur_tab["sin"]
                          P.add("act", (lambda e, zp=zp, zb=zb: e.activation(out=zb[:, 0:ntb], in_=zp[:, 0:ntb], func=AF.Copy)), reads=[zk], writes=[zbk])
                          P.add("pe", (lambda e, rp=rp, zb=zb: e.matmul(rp[:, 0:ntb], lhsT=rot[:], rhs=zb[:, 0:ntb], start=True, stop=True)),
                                reads=[zbk, "rot"], writes=[rk])
                          P.add("dve", (lambda e, zp=zp, t1=t1, cs=cs: e.tensor_tensor(out=t1[:, 0:ntb], in0=zp[:, 0:ntb], in1=cs[:, 0:ntb], op=ALU.mult)),
                                reads=[zk, ck], writes=[t1k])
                          P.add("dve", (lambda e, rp=rp, t2=t2, sn=sn: e.tensor_tensor(out=t2[:, 0:ntb], in0=rp[:, 0:ntb], in1=sn[:, 0:ntb], op=ALU.mult)),
                                reads=[rk, snk], writes=[t2k])
                          P.add("pool", (lambda e, t1=t1, t2=t2: e.tensor_tensor(out=dst[:, dcol:dcol + ntb], in0=t1[:, 0:ntb], in1=t2[:, 0:ntb], op=ALU.add)),
                                reads=[t1k, t2k], writes=[dkey])

                      cur_tab = {}

                      def load_tab(tcol, ntb):
                          cs, ck = coss.next(); sn, snk = sins.next()
                          P.add("sp", (lambda e, cs=cs: e.dma_start(out=cs[:, 0:ntb], in_=c_cos[:, tcol:tcol + ntb])), writes=[ck], dma_sem=ck)
                          P.add("sp", (lambda e, sn=sn: e.dma_start(out=sn[:, 0:ntb], in_=c_sin[:, tcol:tcol + ntb])), writes=[snk], dma_sem=snk)
                          cur_tab["cos"] = (cs, ck); cur_tab["sin"] = (sn, snk)

                      def tm_proj(woff, ncols, tok_off, tok_step, dst, dkey, blk):
                          vp, vk = vps.next()
                          last = tok_off + 127 * tok_step
                          ukeys = ["uT%d" % b for b in range(tok_off // 512, last // 512 + 1)]
                          for dc in range(8):
                              P.add("pe", (lambda e, vp=vp, dc=dc: e.matmul(vp[:, 0:ncols], lhsT=sap(uT, dc * 2048 + tok_off, [[tok_step, 128]]),
                                                                             rhs=wb[:, dc * wcols + woff: dc * wcols + woff + ncols], start=(dc == 0), stop=(dc == 7))),
                                    reads=["wb"] + ukeys, writes=[vk])
                          bs = 320 if pas == 0 else 192
                          P.add("act", (lambda e, vp=vp: e.activation(out=sap(dst, blk * bs, [[128, 2], [1, 64]]), in_=sap(vp, 0, [[64, 2], [1, 64]]), func=AF.Copy)),
                                reads=[vk], writes=[dkey])
                          if pas == 0:
                              P.add("act", (lambda e, vp=vp: e.activation(out=dst[:, blk * 320 + 256: blk * 320 + 320], in_=vp[:, 0:64], func=AF.Copy)),
                                    reads=[vk], writes=[dkey])

                      def k_out(src, skey, scol, ntile, emit_dma):
                          pT, pk = pTs.next(); kst, kk = ksts.next()
                          for j in range(ntile):
                              P.add("pe", (lambda e, pT=pT, j=j: e.transpose(out=pT[:, j * 128:(j + 1) * 128], in_=src[:, scol + j * 128: scol + (j + 1) * 128], identity=ident[:])),
                                    reads=[skey, "ident"], writes=[pk])
                          P.add("act", (lambda e, pT=pT, kst=kst: e.activation(out=kst[:, 0:ntile * 128], in_=pT[:, 0:ntile * 128], func=AF.Copy)),
                                reads=[pk], writes=[kk])
                          emit_dma(kst, kk)

                      chk(0.1)
                      if pas == 0:
                          wm = sb("wm", 8 * 512, st=ast)
                          P.add("pool", lambda e: e.dma_start(out=sap(wm, 0, [[512, 8], [1, 512]]), in_=dap(w_mem, 0, [[512, 128], [128 * 512, 8], [1, 512]])),
                                writes=["wm"], dma_sem="wm")
                          for blk in range(2):
                              hb, hk = hbs.next(); pT, pk = pTs.next()
                              P.add("pool", (lambda e, hb=hb, blk=blk: e.dma_start(out=hb[:], in_=memp[blk * 128:(blk + 1) * 128, :])), writes=[hk], dma_sem=hk)
                              for dc in range(8):
                                  P.add("pe", (lambda e, pT=pT, hb=hb, dc=dc: e.transpose(out=pT[:, dc * 128:(dc + 1) * 128], in_=hb[:, dc * 128:(dc + 1) * 128], identity=ident[:])),
                                        reads=[hk, "ident"], writes=[pk])
                              P.add("act", (lambda e, pT=pT, blk=blk: e.activation(out=sap(uT, blk * 128, [[2048, 8], [1, 128]]), in_=sap(pT, 0, [[128, 8], [1, 128]]), func=AF.Copy)),
                                    reads=[pk], writes=["uT0"])
                          for ch in range(2):
                              zp, zk = zps.next()
                              for dc in range(8):
                                  P.add("pe", (lambda e, zp=zp, dc=dc, ch=ch: e.matmul(zp[:, 0:256], lhsT=wm[:, dc * 512 + ch * 128: dc * 512 + (ch + 1) * 128],
                                                                                       rhs=uT[:, dc * 2048: dc * 2048 + 256], start=(dc == 0), stop=(dc == 7))),
                                        reads=["wm", "uT0"], writes=[zk])
                              P.add("act", (lambda e, zp=zp, ch=ch: e.activation(out=MKT[:, ch * 256:(ch + 1) * 256], in_=zp[:, 0:256], func=AF.Copy)),
                                    reads=[zk], writes=["MKT"])
                          mkn = sb("mkn", 512, st=ast)
                          for blk in range(2):
                              vp, vk = vps.next()
                              for dc in range(8):
                                  P.add("pe", (lambda e, vp=vp, dc=dc, blk=blk: e.matmul(vp[:, 0:512], lhsT=uT[:, dc * 2048 + blk * 128: dc * 2048 + (blk + 1) * 128],
                                                                                         rhs=wm[:, dc * 512:(dc + 1) * 512], start=(dc == 0), stop=(dc == 7))),
                                        reads=["wm", "uT0"], writes=[vk])
                              P.add("act", (lambda e, vp=vp, blk=blk: e.activation(out=mkn[:, blk * 256:(blk + 1) * 256], in_=vp[:, 0:256], func=AF.Copy)),
                                    reads=[vk], writes=["mkn"])
                              P.add("act", (lambda e, vp=vp, blk=blk: e.activation(out=sap(MV, blk * 384, [[192, 2], [128, 2], [1, 64]]), in_=sap(vp, 256, [[128, 2], [64, 2], [1, 64]]), func=AF.Copy)),
                                    reads=[vk], writes=["MV"])
                          P.add("pool", lambda e: e.dma_start(out=dap(o_mem_k, 0, [[256, 128], [128 * 256, 2], [1, 256]]), in_=sap(mkn, 0, [[256, 2], [1, 256]])),
                                reads=["mkn"], dma_sem="o_mem_k")
                          for blk in range(2):
                              for pr in range(2):
                                  P.add("pool", (lambda e, blk=blk, pr=pr: e.dma_start(out=dap(o_mem_v, blk * 128 * 256 + pr * 128, [[256, 128], [64, 2], [1, 64]]),
                                                                                       in_=sap(MV, blk * 384 + pr * 192, [[128, 2], [1, 64]]))),
                                        reads=["MV"], dma_sem="o_mem_v")

                          chk(0.2)
                          norm_T(xh, 1920, 128)
                          chk(0.21)
                          load_tab(1920, 128)
                          chk(0.22)
                          fm_proj(384, KAx, "KAx", 0, 0, 128, None, True)
                          chk(0.23)
                          fm_proj(512, KAy, "KAy", 0, 0, 128, None, True)
                          chk(0.24)
                          tm_proj(896, 128, 0, 1, VN, "VN", 0)
                          chk(0.3)
                          norm_T(xo, 0, 2048)
                          chk(0.4)
                          for tb in range(4):
                              load_tab(2048 + tb * 512, 512)
                              for c in range(3):
                                  fm_proj(c * 128, QA, "QA", c * 2176 + tb * 512, tb * 512, 512, None, True)
                              fm_proj(384, KAx, "KAx", 128 + tb * 512, tb * 512, 512, None, True)
                              fm_proj(512, KAy, "KAy", 128 + tb * 512, tb * 512, 512, None, True)
                              for c in range(2):
                                  fm_proj(640 + c * 128, QX, "QX", c * 2176 + tb * 512, tb * 512, 512, None, False)
                          for j in range(16):
                              tm_proj(896, 128, j * 128, 1, VN, "VN", 1 + j)
                          chk(0.5)
                          k_out(KAx, "KAx", 128 + 15 * 128, 1,
                                lambda kst, kk: P.add("pool", (lambda e: e.dma_start(out=o_swa_k[:, :], in_=kst[:, 0:128])), reads=[kk], dma_sem=kk))
                          P.add("pool", lambda e: e.dma_start(out=dap(o_swa_v, 0, [[128, 128], [64, 2], [1, 64]]), in_=sap(VN, 16 * 320, [[128, 2], [1, 64]])), reads=["VN"], dma_sem="o_swa_v")
                          chk(0.6)
                          norm_T(xs, 0, 128)
                          load_tab(4096, 128)
                          for c in range(3):
                              fm_proj(c * 128, QA, "QA", c * 2176 + 2048, 0, 128, None, True)
                          fm_proj(384, KAx, "KAx", 2176, 0, 128, None, True)
                          fm_proj(512, KAy, "KAy", 2176, 0, 128, None, True)
                          for c in range(2):
                              fm_proj(640 + c * 128, QX, "QX", c * 2176 + 2048, 0, 128, None, False)
                          tm_proj(896, 128, 0, 1, VN, "VN", 17)

                          def sw_new_k(kst, kk):
                              for s in range(16):
                                  P.add("pool", (lambda e, s=s: e.dma_start(out=s_swa_k[s, 120:128, :], in_=kst[s * 8:(s + 1) * 8, 0:128])), reads=[kk], dma_sem=kk)
                          k_out(KAx, "KAx", 2176, 1, sw_new_k)
                          for s in range(16):
                              P.add("pool", (lambda e, s=s: e.dma_start(out=dap(s_swa_v, s * 128 * 128 + 120 * 128, [[128, 8], [64, 2], [1, 64]]), in_=sap(VN, 17 * 320, [[128, 2], [1, 64]], nparts=8, p0=s * 8))),
                                    reads=["VN"], dma_sem="s_swa_v_new")
                      else:
                          p_ = pas - 1
                          norm_T(xh, 0, 2048)
                          for tb in range(4):
                              load_tab(tb * 512, 512)
                              fm_proj(128, KB, "KB", tb * 512, tb * 512, 512, None, True)
                          tm_proj(256, 128, 15 * 128, 1, VN, "VN", 0)
                          for c in range(4):
                              tm_proj(256, 128, 1536 + c, 4, V4, "V4", c)
                          for c in range(16):
                              tm_proj(256, 128, c, 16, V16, "V16", c)
                          norm_T(xo, 0, 2048)
                          for tb in range(4):
                              load_tab(2048 + tb * 512, 512)
                              fm_proj(0, QB, "QB", tb * 512, tb * 512, 512, None, True)
                              fm_proj(128, KB, "KB", 2048 + tb * 512, tb * 512, 512, None, True)
                          for j in range(16):
                              tm_proj(256, 128, j * 128, 1, VN, "VN", 1 + j)
                          for n in range(4):
                              for c in range(4):
                                  tm_proj(256, 128, n * 512 + c, 4, V4, "V4", (n + 1) * 4 + c)
                          for c in range(16):
                              tm_proj(256, 128, c, 16, V16, "V16", 16 + c)
                          for q4 in range(4):
                              k_out(KB, "KB", 2048 + q4 * 512, 4,
                                    lambda kst, kk, q4=q4: P.add("pool", (lambda e: e.dma_start(
                                        out=dap(o_dil_k, q4 * 512 * 384 + p_ * 128, [[384, 128], [128 * 384, 4], [1, 128]]),
                                        in_=sap(kst, 0, [[128, 4], [1, 128]]))), reads=[kk], dma_sem=kk))
                          for hd in range(2):
                              P.add("pool", (lambda e, hd=hd: e.dma_start(out=dap(o_dil_v, p_ * 128 + hd * 64, [[384, 128], [128 * 384, 16], [1, 64]]),
                                                                          in_=sap(VN, 192 + hd * 128, [[192, 16], [1, 64]]))), reads=["VN"], dma_sem="o_dil_v")
                          norm_T(xs, 0, 128)
                          load_tab(4096, 128)
                          fm_proj(0, QB, "QB", 2048, 0, 128, None, True)
                          fm_proj(128, KB, "KB", 4096, 0, 128, None, True)
                          tm_proj(256, 128, 0, 1, VN, "VN", 17)

                          def dil_new_k(kst, kk):
                              for s in range(16):
                                  P.add("pool", (lambda e, s=s: e.dma_start(out=s_dil_k[s, 2040:2048, p_ * 128:(p_ + 1) * 128], in_=kst[s * 8:(s + 1) * 8, 0:128])),
                                        reads=[kk], dma_sem=kk)
                          k_out(KB, "KB", 4096, 1, dil_new_k)
                          for s in range(16):
                              P.add("pool", (lambda e, s=s: e.dma_start(out=dap(s_dil_v, s * 2048 * 384 + 2040 * 384 + p_ * 128, [[384, 8], [64, 2], [1, 64]]), in_=sap(VN, 17 * 192, [[128, 2], [1, 64]], nparts=8, p0=s * 8))),
                                    reads=["VN"], dma_sem="s_dil_v_new")

                      P.flush()
                  if STOP[0] <= 2 * pas + 1:
                      break
                  with contextlib.ExitStack() as bst:
                      acc = sb("acc", 2 * 2176, F32, st=bst)
                      rl = sb("rl", 2176, F32, st=bst)
                      pts = Ring("pt", [sb("pt%d" % i, 512, st=bst) for i in range(4)])

                      if pas == 0:
                          cast_weights()
                      accs = sb("accs", 6 * 128, F32, st=bst)
                      accx = sb("accx", 4 * 128, F32, st=bst)
                      with contextlib.ExitStack() as sst:
                          k32s = Ring("k32", [sb("k32_%d" % i, 2048, F32, st=sst) for i in range(2)])
                          v32s = Ring("v32", [sb("v32_%d" % i, 2048, F32, st=sst) for i in range(2)])
                          kdts = Ring("kdt", [sb("kdt%d" % i, 2048, st=sst) for i in range(2)])
                          ptn = sb("ptn", 256, st=sst)
                          tps = Ring("tp", [ps("tp%d" % i, 512, st=sst) for i in range(2)])
                          sqs = Ring("sq", [ps("sq%d" % i, 512, st=sst) for i in range(2)])
                          snr = Ring("st", [ps("sn%d" % i, 512, st=sst) for i in range(2)])
                          ocA = ps("ocA", 512, st=sst); ocB = ps("ocB", 512, st=sst)

                          def transposes32(src, skey, ncols_src, n, dst, dkey):
                              for q in range((n + 3) // 4):
                                  tp, tpk = tps.next()
                                  m_ = min(4, n - q * 4)
                                  for c4 in range(m_):
                                      c = q * 4 + c4
                                      P.add("pe", (lambda e, tp=tp, c4=c4, c=c: e.transpose(out=tp[:, c4 * 128:(c4 + 1) * 128], in_=src[:, c * 128:(c + 1) * 128], identity=identf[:])),
                                            reads=[skey, "identf"], writes=[tpk])
                                  P.add("act", (lambda e, tp=tp, q=q, m_=m_: e.activation(out=dst[:, q * 512: q * 512 + m_ * 128], in_=tp[:, 0:m_ * 128], func=AF.Copy)),
                                        reads=[tpk], writes=[dkey])

                          def new_keys(pairs, ktiles, qt, qbase, kcol, vt, vkey, voffs, masknew, mnkey, outs):
                              for i, (half, (kt, kkey)) in enumerate(zip((0, 1), ktiles)):
                                  sn, snk = snr.next()
                                  P.add("pe", (lambda e, sn=sn, kt=kt, half=half, i=i: e.matmul(sn[:, 0:128], lhsT=sap(kt, kcol, [[1, 128]], nparts=64, p0=half * 64),
                                                                                                rhs=sap(qt, qbase, [[1, 128]], nparts=64, p0=half * 64), start=True, stop=True)),
                                        reads=[kkey, qkey_s], writes=[snk])
                                  P.add("act", (lambda e, sn=sn, i=i: e.activation(out=ptn[:, i * 128:(i + 1) * 128], in_=sn[:, 0:128], func=AF.Exp, scale=SCALE)),
                                        reads=[snk], writes=["ptn"])
                              P.add("pool", (lambda e: e.tensor_tensor(out=sap(ptn, 0, [[128, 2], [1, 128]]), in0=sap(ptn, 0, [[128, 2], [1, 128]]),
                                                                       in1=sap(masknew, 0, [[0, 2], [1, 128]]), op=ALU.mult)), reads=["ptn", mnkey], writes=["ptn"])
                              for i in range(2):
                                  ob_, ok_, ocol = outs[i]
                                  P.add("pe", (lambda e, i=i, ob_=ob_, ocol=ocol: e.matmul(ob_[:, ocol:ocol + 128], lhsT=vt[:, voffs[i]: voffs[i] + 128],
                                                                                           rhs=ptn[:, i * 128:(i + 1) * 128], start=True, stop=True)),
                                        reads=[vkey, vkey + "_ones", "ptn"], writes=[ok_])

                          if pas == 0:
                              SWOFF = {(0, 0): 0, (1, 1): 64, (0, 1): 128, (1, 0): 192}
                              qkey_s = "QA"
                              vc = sb("vc", 16 * 320, st=sst)
                              qbs = sb("qbs", 768, st=sst)
                              pts_ = sb("pts_", 768, st=sst)
                              P.add("pool", lambda e: e.memset(sap(vc, 64, [[320, 16], [128, 2], [1, 64]]), 1.0), writes=["vc_ones"])
                              P.add("pool", lambda e: e.memset(qbs[:], 0.0), writes=["qbs"])
                              for h in range(6):
                                  P.add("dve", (lambda e, h=h: e.tensor_copy(out=sap(qbs, h * 8, [[48, 16], [1, 8]], nparts=64, p0=(h // 3) * 64),
                                                                             in_=sap(QA, (h // 2) * 2176 + 2048, [[8, 16], [1, 8]], nparts=64, p0=(h % 2) * 64))),
                                        reads=["QA"], writes=["qbs"])
                              k32, k32k = k32s.next(); v32, v32k = v32s.next(); kdt, kdtk = kdts.next()
                              P.add("sp", lambda e: e.dma_start(out=sap(k32, 0, [[128, 16], [1, 128]]), in_=dap(csk, 0, [[128, 128], [128 * 128, 16], [1, 128]])), writes=[k32k], dma_sem=k32k)
                              P.add("sp", lambda e: e.dma_start(out=sap(v32, 0, [[128, 16], [1, 128]]), in_=dap(csv, 0, [[128, 128], [128 * 128, 16], [1, 128]])), writes=[v32k], dma_sem=v32k)
                              P.add("pool", lambda e: e.tensor_copy(out=sap(vc, 0, [[320, 16], [128, 2], [1, 64]]), in_=sap(v32, 0, [[128, 16], [64, 2], [1, 64]])), reads=[v32k], writes=["vc"])
                              P.add("pool", lambda e: e.tensor_copy(out=sap(vc, 256, [[320, 16], [1, 64]]), in_=sap(v32, 0, [[128, 16], [1, 64]])), reads=[v32k], writes=["vc"])
                              transposes32(k32, k32k, 2048, 16, kdt, kdtk)
                              sqa, sqak = sqs.next(); sqb, sqbk = sqs.next()
                              for s_ in range(16):
                                  sq_, sqk_ = (sqa, sqak) if s_ < 8 else (sqb, sqbk)
                                  P.add("pe", (lambda e, s_=s_, sq_=sq_: e.matmul(sq_[:, (s_ % 8) * 48:(s_ % 8 + 1) * 48], lhsT=kdt[:, s_ * 128:(s_ + 1) * 128],
                                                                                  rhs=qbs[:, s_ * 48:(s_ + 1) * 48], start=True, stop=True)), reads=[kdtk, "qbs"], writes=[sqk_])
                              for i_, (sq_, sqk_) in enumerate(((sqa, sqak), (sqb, sqbk))):
                                  P.add("act", (lambda e, i_=i_, sq_=sq_: e.activation(out=pts_[:, i_ * 384:(i_ + 1) * 384], in_=sq_[:, 0:384], func=AF.Exp, scale=SCALE)),
                                        reads=[sqk_], writes=["pts_"])
                              P.add("pool", lambda e: e.tensor_tensor(out=pts_[:], in0=pts_[:], in1=msw[:], op=ALU.mult), reads=["pts_", "msw"], writes=["pts_"])
                              for s_ in range(16):
                                  for h in range(6):
                                      oc_, ock_, oco = (ocA, "oc#0", h * 128) if h < 4 else (ocB, "oc#1", (h - 4) * 128)
                                      voff = s_ * 320 + SWOFF[(h % 2, h // 3)]
                                      P.add("pe", (lambda e, s_=s_, h=h, oc_=oc_, oco=oco, voff=voff: e.matmul(oc_[:, oco + s_ * 8: oco + s_ * 8 + 8], lhsT=vc[:, voff:voff + 128],
                                                                                                            rhs=pts_[:, s_ * 48 + h * 8: s_ * 48 + h * 8 + 8], start=True, stop=True)),
                                            reads=["vc", "vc_ones", "pts_"], writes=[ock_])
                              P.add("dve", lambda e: e.tensor_copy(out=accs[:, 0:512], in_=ocA[:, 0:512]), reads=["oc#0"], writes=["accs"])
                              P.add("dve", lambda e: e.tensor_copy(out=accs[:, 512:768], in_=ocB[:, 0:256]), reads=["oc#1"], writes=["accs"])
                              for cpair in range(3):
                                  he, ho = 2 * cpair, 2 * cpair + 1
                                  ktl = [(KAx, "KAx") if (h // 3) == (h % 2) else (KAy, "KAy") for h in (he, ho)]
                                  voffs = [17 * 320 + SWOFF[(h % 2, h // 3)] for h in (he, ho)]
                                  new_keys(None, ktl, QA, cpair * 2176 + 2048, 2176, VN, "VN", voffs, mnsw, "mnsw", [(ocA, "oc#0", 0), (ocA, "oc#0", 128)])
                                  P.add("dve", (lambda e, cpair=cpair: e.tensor_tensor(out=accs[:, cpair * 256:(cpair + 1) * 256], in0=ocA[:, 0:256], in1=accs[:, cpair * 256:(cpair + 1) * 256], op=ALU.add)),
                                        reads=["oc#0", "accs"], writes=["accs"])
                              vmq = sb("vmq", 8 * 384, st=sst)
                              qbx = sb("qbx", 512, st=sst)
                              ptm = sb("ptm", 256, st=sst)
                              P.add("pool", lambda e: e.memset(sap(vmq, 64, [[192, 16], [1, 64]]), 1.0), writes=["vmq_ones"])
                              P.add("pool", lambda e: e.memset(qbx[:], 0.0), writes=["qbx"])
                              for ch in range(2):
                                  for hd in range(2):
                                      P.add("dve", (lambda e, ch=ch, hd=hd: e.tensor_copy(out=sap(qbx, ch * 16 + hd * 8, [[32, 16], [1, 8]], nparts=64, p0=hd * 64),
                                                                                        in_=sap(QX, ch * 2176 + 2048, [[8, 16], [1, 8]], nparts=64, p0=hd * 64))),
                                            reads=["QX"], writes=["qbx"])
                              for qd in range(4):
                                  k32, k32k = k32s.next(); v32, v32k = v32s.next(); kdt, kdtk = kdts.next()
                                  P.add("sp", (lambda e, k32=k32, qd=qd: e.dma_start(out=sap(k32, 0, [[256, 8], [1, 256]]), in_=dap(cmk, qd * 4 * 65536, [[256, 128], [128 * 256, 8], [1, 256]]))),
                                        writes=[k32k], dma_sem=k32k)
                                  P.add("sp", (lambda e, v32=v32, qd=qd: e.dma_start(out=sap(v32, 0, [[256, 8], [1, 256]]), in_=dap(cmv, qd * 4 * 65536, [[256, 128], [128 * 256, 8], [1, 256]]))),
                                        writes=[v32k], dma_sem=v32k)
                                  for pr in range(2):
                                      P.add("pool", (lambda e, v32=v32, pr=pr: e.tensor_copy(out=sap(vmq, pr * 192, [[384, 8], [128, 2], [1, 64]]), in_=sap(v32, pr * 128, [[256, 8], [64, 2], [1, 64]]))),
                                            reads=[v32k], writes=["vmq"])
                                  transposes32(k32, k32k, 2048, 16, kdt, kdtk)
                                  sq_, sqk_ = sqs.next()
                                  for sl in range(4):
                                      for blk in range(2):
                                          for ch in range(2):
                                              idx = (sl * 2 + blk) * 2 + ch
                                              sg_ = qd * 4 + sl
                                              P.add("pe", (lambda e, idx=idx, sg_=sg_, ch=ch, sq_=sq_, kdt=kdt: e.matmul(sq_[:, idx * 16:(idx + 1) * 16], lhsT=kdt[:, idx * 128:(idx + 1) * 128],
                                                                                                                     rhs=qbx[:, (sg_ * 2 + ch) * 16:(sg_ * 2 + ch + 1) * 16], start=True, stop=True)),
                                                    reads=[kdtk, "qbx"], writes=[sqk_])
                                  P.add("act", (lambda e, sq_=sq_: e.activation(out=ptm[:], in_=sq_[:, 0:256], func=AF.Exp, scale=SCALE)), reads=[sqk_], writes=["ptm"])
                                  for sl in range(4):
                                      sg_ = qd * 4 + sl
                                      for x in range(4):
                                          for blk in range(2):
                                              voff = (sl * 2 + blk) * 384 + (x // 2) * 192 + (x % 2) * 64
                                              pcol = ((sl * 2 + blk) * 2 + x // 2) * 16 + (x % 2) * 8
                                              P.add("pe", (lambda e, sg_=sg_, x=x, blk=blk, voff=voff, pcol=pcol: e.matmul(ocB[:, x * 128 + sg_ * 8: x * 128 + sg_ * 8 + 8], lhsT=vmq[:, voff:voff + 128],
                                                                                                                       rhs=ptm[:, pcol:pcol + 8], start=(blk == 0), stop=(blk == 1))),
                                                    reads=["vmq", "vmq_ones", "ptm"], writes=["oc#1"])
                              P.add("dve", lambda e: e.tensor_copy(out=accx[:], in_=ocB[:]), reads=["oc#1"], writes=["accx"])
                          else:
                              p_ = pas - 1
                              qkey_s = "QB"
                              vds = [sb("vd%d" % i, 16 * 192, st=sst) for i in range(2)]
                              vdr = Ring("vd", vds)
                              qbd = sb("qbd", 256, st=sst)
                              ptds = Ring("ptd", [sb("ptd%d" % i, 256, st=sst) for i in range(2)])
                              for i, vt in enumerate(vds):
                                  P.add("pool", (lambda e, vt=vt: e.memset(sap(vt, 64, [[192, 16], [1, 64]]), 1.0)), writes=["vd#%d_ones" % i])
                              P.add("pool", lambda e: e.memset(qbd[:], 0.0), writes=["qbd"])
                              for hd in range(2):
                                  P.add("dve", (lambda e, hd=hd: e.tensor_copy(out=sap(qbd, hd * 8, [[16, 16], [1, 8]], nparts=64, p0=hd * 64),
                                                                               in_=sap(QB, 2048, [[8, 16], [1, 8]], nparts=64, p0=hd * 64))), reads=["QB"], writes=["qbd"])
                              def mk_seq(s_):
                                  st_ = {}

                                  def A1():
                                      k32, k32k = k32s.next(); v32, v32k = v32s.next(); kdt, kdtk = kdts.next(); vd, vdk = vdr.next()
                                      st_.update(kdt=kdt, kdtk=kdtk, vd=vd, vdk=vdk)
                                      P.add("sp", (lambda e: e.dma_start(out=sap(k32, 0, [[128, 16], [1, 128]]),
                                                                         in_=dap(cdk, s_ * 2048 * 384 + p_ * 128, [[16 * 384, 128], [384, 16], [1, 128]]))), writes=[k32k], dma_sem=k32k)
                                      P.add("act", (lambda e: e.dma_start(out=sap(v32, 0, [[128, 16], [1, 128]]),
                                                                          in_=dap(cdv, s_ * 2048 * 384 + p_ * 128, [[16 * 384, 128], [384, 16], [1, 128]]))), writes=[v32k], dma_sem=v32k)
                                      P.add("dve", (lambda e: e.tensor_copy(out=sap(vd, 0, [[192, 16], [128, 2], [1, 64]]), in_=sap(v32, 0, [[128, 16], [64, 2], [1, 64]]))),
                                            reads=[v32k], writes=[vdk])
                                      transposes32(k32, k32k, 2048, 16, kdt, kdtk)

                                  def A2():
                                      kdt, kdtk = st_["kdt"], st_["kdtk"]
                                      sq_, sqk_ = sqs.next(); ptd, ptdk = ptds.next()
                                      st_.update(ptd=ptd, ptdk=ptdk)
                                      for c in range(16):
                                          P.add("pe", (lambda e, c=c: e.matmul(sq_[:, c * 16:(c + 1) * 16], lhsT=kdt[:, c * 128:(c + 1) * 128],
                                                                               rhs=qbd[:, s_ * 16:(s_ + 1) * 16], start=True, stop=True)), reads=[kdtk, "qbd"], writes=[sqk_])
                                      P.add("act", (lambda e: e.activation(out=ptd[:], in_=sq_[:, 0:256], func=AF.Exp, scale=SCALE)), reads=[sqk_], writes=[ptdk])
                                      P.add("dve", (lambda e: e.tensor_tensor(out=ptd[:], in0=ptd[:], in1=mdil[:], op=ALU.mult)), reads=[ptdk, "mdil"], writes=[ptdk])

                                  def B():
                                      vd, vdk, ptd, ptdk = st_["vd"], st_["vdk"], st_["ptd"], st_["ptdk"]
                                      for hd in range(2):
                                          for c in range(16):
                                              P.add("pe", (lambda e, hd=hd, c=c: e.matmul(ocA[:, hd * 128 + s_ * 8: hd * 128 + s_ * 8 + 8], lhsT=vd[:, c * 192 + hd * 64: c * 192 + hd * 64 + 128],
                                                                                           rhs=ptd[:, c * 16 + hd * 8: c * 16 + hd * 8 + 8], start=(c == 0), stop=(c == 15))),
                                                    reads=[vdk, vdk + "_ones", ptdk], writes=["oc#0"])
                                  return A1, A2, B

                              seqs = [mk_seq(s_) for s_ in range(16)]
                              LAGS = (1, 0)
                              for i_ in range(16 + LAGS[0] + LAGS[1]):
                                  if i_ < 16:
                                      seqs[i_][0]()
                                  if LAGS[0] <= i_ < 16 + LAGS[0]:
                                      seqs[i_ - LAGS[0]][1]()
                                  if i_ >= LAGS[0] + LAGS[1]:
                                      seqs[i_ - LAGS[0] - LAGS[1]][2]()
                              P.add("dve", lambda e: e.tensor_copy(out=accs[:, 0:256], in_=ocA[:, 0:256]), reads=["oc#0"], writes=["accs"])
                              new_keys(None, [(KB, "KB"), (KB, "KB")], QB, 2048, 4096, VN, "VN", [17 * 192, 17 * 192 + 64], mndil, "mndil", [(ocB, "oc#1", 0), (ocB, "oc#1", 128)])
                              P.add("dve", lambda e: e.tensor_tensor(out=accs[:, 0:256], in0=ocB[:, 0:256], in1=accs[:, 0:256], op=ALU.add), reads=["oc#1", "accs"], writes=["accs"])
                          P.flush()
                      sts = Ring("st", [ps("st%d" % i, 512, st=bst) for i in range(4)])
                      obs = Ring("ob", [ps("ob%d" % i, 512, st=bst) for i in range(2)])

                      def v_lhsT(vt, off, odd):
                          return vt[:, off:off + 128]

                      def unit_pair(qsrc, qkey, qcols, ksrcs, kcols2, vsrc, vkey, voffs2, mask, mkey, acc_cols, first):
                          pt, pk = pts.next()
                          for hd in range(2):
                              st, sk = sts.next()
                              kt, kkey = ksrcs[hd]
                              for blk in range(2):
                                  ks, kstep = kcols2[blk]
                                  P.add("pe", (lambda e, st=st, hd=hd, blk=blk, kt=kt, ks=ks, kstep=kstep: e.matmul(
                                      st[:, blk * 128: (blk + 1) * 128],
                                      lhsT=sap(kt, ks, [[kstep, 128]], nparts=64, p0=hd * 64),
                                      rhs=sap(qsrc, qcols[0], [[qcols[1], 128]], nparts=64, p0=hd * 64), start=True, stop=True)),
                                      reads=[kkey, qkey], writes=[sk])
                              P.add("act", (lambda e, st=st, pt=pt, hd=hd: e.activation(out=pt[:, hd * 256:(hd + 1) * 256], in_=st[:, 0:256], func=AF.Exp, scale=SCALE)),
                                    reads=[sk], writes=[pk])
                          if mask is not None:
                              P.add("dve", (lambda e, pt=pt: e.tensor_tensor(out=pt[:], in0=pt[:], in1=mask[:], op=ALU.mult)), reads=[pk, mkey], writes=[pk])

                          def stage2():
                              ob, ok = obs.next()
                              for hd in range(2):
                                  for blk in range(2):
                                      P.add("pe", (lambda e, ob=ob, pt=pt, hd=hd, blk=blk: e.matmul(
                                          ob[:, hd * 128:(hd + 1) * 128], lhsT=v_lhsT(vsrc, voffs2[hd][blk], hd == 1),
                                          rhs=pt[:, hd * 256 + blk * 128: hd * 256 + (blk + 1) * 128], start=(blk == 0), stop=(blk == 1))),
                                          reads=[vkey, vkey + "_ones", pk], writes=[ok])
                              a0, astep = acc_cols
                              dst = sap(acc, a0, [[2176, 2], [astep, 128]])
                              src = sap(ob, 0, [[128, 2], [1, 128]])
                              if first:
                                  P.add("dve", (lambda e: e.tensor_copy(out=dst, in_=src)), reads=[ok], writes=["acc"])
                              else:
                                  P.add("dve", (lambda e: e.tensor_tensor(out=dst, in0=src, in1=dst, op=ALU.add)), reads=[ok, "acc"], writes=["acc"])
                          pend.append(stage2)
                          if len(pend) > 2:
                              pend.pop(0)()

                      pend = []

                      def drain():
                          while pend:
                              pend.pop(0)()

                      def finish_pair(chunk, c0, n, sink_heads=None):
                          if sink_heads is not None:
                              he, ho = sink_heads
                              P.add("act", (lambda e: e.activation(out=sap(acc, c0, [[1, n]], nparts=64, p0=64), in_=sap(acc, c0, [[1, n]], nparts=64, p0=64),
                                                                   func=AF.Identity, bias=esink[64:128, he:he + 1])), reads=["acc", "esink"], writes=["acc"])
                              P.add("act", (lambda e: e.activation(out=sap(acc, 2176 + c0, [[1, n]], nparts=64, p0=0), in_=sap(acc, 2176 + c0, [[1, n]], nparts=64, p0=0),
                                                                   func=AF.Identity, bias=esink[0:64, ho:ho + 1])), reads=["acc", "esink"], writes=["acc"])
                          P.add("dve", (lambda e: e.reciprocal(out=sap(rl, c0, [[1, n]], nparts=64, p0=0), in_=sap(acc, c0, [[1, n]], nparts=64, p0=64))),
                                reads=["acc"], writes=["rl"])
                          P.add("dve", (lambda e: e.reciprocal(out=sap(rl, c0, [[1, n]], nparts=64, p0=64), in_=sap(acc, 2176 + c0, [[1, n]], nparts=64, p0=0))),
                                reads=["acc"], writes=["rl"])
                          P.add("dve", (lambda e: e.tensor_tensor(out=sap(catT, chunk * CT + c0, [[1, n]], nparts=64, p0=0), in0=sap(acc, c0, [[1, n]], nparts=64, p0=0),
                                                                  in1=sap(rl, c0, [[1, n]], nparts=64, p0=0), op=ALU.mult)), reads=["acc", "rl"], writes=["catT"])
                          P.add("dve", (lambda e: e.tensor_tensor(out=sap(catT, chunk * CT + c0, [[1, n]], nparts=64, p0=64), in0=sap(acc, 2176 + c0, [[1, n]], nparts=64, p0=64),
                                                                  in1=sap(rl, c0, [[1, n]], nparts=64, p0=64), op=ALU.mult)), reads=["acc", "rl"], writes=["catT"])

                      if pas == 0:
                          for cpair in range(3):
                              he, ho = 2 * cpair, 2 * cpair + 1
                              ksrcs = []
                              for h in (he, ho):
                                  kv = h // 3
                                  ksrcs.append((KAx, "KAx") if kv == (h % 2) else (KAy, "KAy"))
                              for j in range(16):
                                  voffs2 = [[(j + blk) * 320 + {(0, 0): 0, (1, 1): 64, (0, 1): 128, (1, 0): 192}[(h % 2, h // 3)] for blk in range(2)] for h in (he, ho)]
                                  m, mk_ = (mfirst, "mfirst") if j == 0 else (mstd, "mstd")
                                  unit_pair(QA, "QA", (cpair * 2176 + j * 128, 1), ksrcs, [(j * 128, 1), (128 + j * 128, 1)],
                                            VN, "VN", voffs2, m, mk_, (j * 128, 1), True)
                              drain()
                              P.add("dve", (lambda e, cpair=cpair: e.tensor_copy(out=sap(acc, 2048, [[2176, 2], [1, 128]]), in_=sap(accs, cpair * 256, [[128, 2], [1, 128]]))),
                                    reads=["accs"], writes=["acc"])
                              finish_pair(cpair, 0, 2176, sink_heads=(he, ho))
                          for cpair in range(2):
                              for j in range(16):
                                  voffs2 = [[blk * 384 + cpair * 192 + hd * 64 for blk in range(2)] for hd in range(2)]
                                  unit_pair(QX, "QX", (cpair * 2176 + j * 128, 1), [(MKT, "MKT"), (MKT, "MKT")],
                                            [(cpair * 256, 1), (cpair * 256 + 128, 1)], MV, "MV", voffs2, None, None, (j * 128, 1), True)
                              drain()
                              P.add("dve", (lambda e, cpair=cpair: e.tensor_copy(out=sap(acc, 2048, [[2176, 2], [1, 128]]), in_=sap(accx, cpair * 256, [[128, 2], [1, 128]]))),
                                    reads=["accx"], writes=["acc"])
                              finish_pair(6 + cpair, 0, 2176)
                      else:
                          p_ = pas - 1
                          kb2 = [(KB, "KB"), (KB, "KB")]
                          for j in range(16):
                              voffs2 = [[(j + blk) * 192 + hd * 64 for blk in range(2)] for hd in range(2)]
                              m, mk_ = (mfirst, "mfirst") if j == 0 else (mstd, "mstd")
                              unit_pair(QB, "QB", (j * 128, 1), kb2, [(2048 + (j - 1) * 128, 1), (2048 + j * 128, 1)],
                                        VN, "VN", voffs2, m, mk_, (j * 128, 1), True)
                          for n in range(4):
                              for c in range(4):
                                  voffs2 = [[((n + blk) * 4 + c) * 192 + hd * 64 for blk in range(2)] for hd in range(2)]
                                  m, mk_ = (mfirst, "mfirst") if n == 0 else (mstd, "mstd")
                                  unit_pair(QB, "QB", (n * 512 + c, 4), kb2, [(2048 + (n - 1) * 512 + c, 4), (2048 + n * 512 + c, 4)],
                                            V4, "V4", voffs2, m, mk_, (n * 512 + c, 4), False)
                          for c in range(16):
                              voffs2 = [[(blk * 16 + c) * 192 + hd * 64 for blk in range(2)] for hd in range(2)]
                              unit_pair(QB, "QB", (c, 16), kb2, [(c, 16), (2048 + c, 16)], V16, "V16", voffs2, mfirst, "mfirst", (c, 16), False)
                          drain()
                          P.add("dve", lambda e: e.tensor_copy(out=sap(acc, 2048, [[2176, 2], [1, 128]]), in_=sap(accs, 0, [[128, 2], [1, 128]])), reads=["accs"], writes=["acc"])
                          finish_pair(3 + p_, 0, 2176)
                      P.flush()


            except _StopBuild:
                P.flush()
                break

        if STOP[0] > 8:
            with contextlib.ExitStack() as cst:
                NTGM = 640
                while pending_cp:
                    maybe_copy(force=True)
                x1 = sb("x1", 5 * 1024, F32, st=cst)
                hT = sb("hT", 8 * NTGM, st=cst)
                actT = sb("actT", NFC * NTGM, st=cst)
                wd = sb("wd", NFC * 1024, st=cst)
                wo = sb("wo", 8 * 1024, st=cst)
                wgs = Ring("wg", [sb("wg%d" % i, 2048, st=cst) for i in range(2)])
                wus = Ring("wu", [sb("wu%d" % i, 2048, st=cst) for i in range(2)])
                gpost = sb("gpost", D, F32, st=cst); gffn = sb("gffn", D, F32, st=cst); gpffn = sb("gpffn", D, F32, st=cst)
                xts = Ring("xt", [sb("cxt%d" % i, D, F32, st=cst) for i in range(1)])
                ysbs = Ring("ysb", [sb("ysb%d" % i, D, F32, st=cst) for i in range(1)])
                hbs = Ring("hb", [sb("chb%d" % i, D, st=cst) for i in range(2)])
                junk = sb("cjunk", D, st=cst)
                sgs = Ring("sg", [sb("sg%d" % i, 512, F32, st=cst) for i in range(2)])
                ssa = Ring("ssa", [sb("ssa%d" % i, 2, F32, st=cst) for i in range(2)])
                sss = Ring("ss", [sb("css%d" % i, 1, F32, st=cst) for i in range(2)])
                rss = Ring("rs", [sb("crs%d" % i, 1, F32, st=cst) for i in range(2)])
                yps = Ring("yp", [ps("yp%d" % i, 512, st=cst) for i in range(4)])
                gps = Ring("gp", [ps("gp%d" % i, 512, st=cst) for i in range(3)])
                ups = gps
                pTs = Ring("pT", [ps("cpT%d" % i, 1024, BF16, st=cst) for i in range(1)])
                P.add("sp", lambda e: e.dma_start(out=sap(wo, 0, [[1024, 8], [1, 1024]]), in_=dap(wob, 0, [[1024, 128], [128 * 1024, 8], [1, 1024]])),
                      reads=["wob"], writes=["wo"], dma_sem="wo")
                P.add("sp", lambda e: e.dma_start(out=sap(wd, 0, [[1024, NFC], [1, 1024]]), in_=dap(wdb, 0, [[1024, 128], [128 * 1024, NFC], [1, 1024]])),
                      reads=["wdb"], writes=["wd"], dma_sem="wd")
                for gt, gsrc, gk in ((gpost, g_post, "gpost"), (gffn, g_ffn, "gffn"), (gpffn, g_pffn, "gpffn")):
                    P.add("sp", (lambda e, gt=gt, gsrc=gsrc: e.dma_start(out=gt[:], in_=dap(gsrc, 0, [[0, 128], [1, D]]))), writes=[gk], dma_sem=gk)

                def sumsq(src_aps, src_keys, ss, sk):
                    sa, sak = ssa.next()
                    for i, (ap_, k_) in enumerate(zip(src_aps, src_keys)):
                        P.add("act", (lambda e, ap_=ap_, i=i, sa=sa: e.activation(out=junk[:, 0:512], in_=ap_, func=AF.Square, accum_out=sa[:, i:i + 1])),
                              reads=[k_], writes=["cjunk", sak])
                    P.add("dve", (lambda e, sa=sa, ss=ss: e.tensor_tensor(out=ss[:], in0=sa[:, 0:1], in1=sa[:, 1:2], op=ALU.add)), reads=[sak], writes=[sk])

                groups = [[0, 1, 2, 3], [4, 5, 6, 7], [8, 9, 10, 11], [12, 13, 14, 15, 16]]
                for grp in groups:
                    ntile = len(grp)
                    NTG = ntile * 128
                    cpend = []
                    for slot, T in enumerate(grp):
                        xsrc, xrow = (xo, T * 128) if T < 16 else (xs, 0)
                        ccol = T * 128
                        ypa, yka = yps.next(); ypb, ykb = yps.next()
                        for eh, yp in ((0, ypa), (1, ypb)):
                            for k in range(8):
                                P.add("pe", (lambda e, yp=yp, eh=eh, k=k, ccol=ccol: e.matmul(yp[:, 0:512], lhsT=catT[:, k * CT + ccol: k * CT + ccol + 128],
                                                                                               rhs=wo[:, k * 1024 + eh * 512: k * 1024 + (eh + 1) * 512], start=(k == 0), stop=(k == 7))),
                                      reads=["catT", "wo"], writes=[yka if eh == 0 else ykb])
                        ss, sk = sss.next(); rs, rk = rss.next()
                        sumsq([ypa[:, 0:512], ypb[:, 0:512]], [yka, ykb], ss, sk)
                        rstd_ops(ss, sk, rs, rk)
                        xt, xk = xts.next()
                        P.add("sp", (lambda e, xt=xt, xsrc=xsrc, xrow=xrow: e.dma_start(out=xt[:], in_=xsrc[xrow:xrow + 128, :])), writes=[xk], dma_sem=xk)
                        x1k = "x1_%d" % slot
                        for eh, yp, yk in ((0, ypa, yka), (1, ypb, ykb)):
                            P.add("dve", (lambda e, yp=yp, eh=eh, rs=rs, slot=slot: e.scalar_tensor_tensor(
                                out=x1[:, slot * 1024 + eh * 512: slot * 1024 + (eh + 1) * 512], in0=yp[:, 0:512], scalar=rs[:, 0:1],
                                in1=gpost[:, eh * 512:(eh + 1) * 512], op0=ALU.mult, op1=ALU.mult)), reads=[yk, rk, "gpost"], writes=[x1k])
                        P.add("pool", (lambda e, xt=xt, slot=slot: e.tensor_tensor(out=x1[:, slot * 1024:(slot + 1) * 1024], in0=x1[:, slot * 1024:(slot + 1) * 1024],
                                                                                   in1=xt[:], op=ALU.add)), reads=[x1k, xk], writes=[x1k])
                        ss2, sk2 = sss.next(); rs2, rk2 = rss.next(); hb, hk = hbs.next(); pT, pk = pTs.next()
                        sumsq([x1[:, slot * 1024: slot * 1024 + 512], x1[:, slot * 1024 + 512:(slot + 1) * 1024]], [x1k, x1k], ss2, sk2)
                        rstd_ops(ss2, sk2, rs2, rk2)
                        P.add("dve", (lambda e, hb=hb, rs2=rs2, slot=slot: e.scalar_tensor_tensor(out=hb[:], in0=x1[:, slot * 1024:(slot + 1) * 1024], scalar=rs2[:, 0:1],
                                                                                                  in1=gffn[:], op0=ALU.mult, op1=ALU.mult)), reads=[x1k, rk2, "gffn"], writes=[hk])
                        def stage2(pT=pT, pk=pk, hb=hb, hk=hk, slot=slot, NTG=NTG):
                            for dc in range(8):
                                P.add("pe", (lambda e, dc=dc: e.transpose(out=pT[:, dc * 128:(dc + 1) * 128], in_=hb[:, dc * 128:(dc + 1) * 128], identity=ident[:])),
                                      reads=[hk, "ident"], writes=[pk])
                            P.add("act", (lambda e: e.activation(out=sap(hT, slot * 128, [[NTG, 8], [1, 128]]), in_=sap(pT, 0, [[128, 8], [1, 128]]), func=AF.Copy)),
                                  reads=[pk], writes=["hT"])
                        cpend.append(stage2)
                        if len(cpend) > 1:
                            cpend.pop(0)()
                    while cpend:
                        cpend.pop(0)()
                    tbs = [(0, 512)] if ntile == 4 else [(0, 512), (512, 128)]
                    for fc in range(NFC):
                        if fc % 2 == 0:
                            nf = min(2, NFC - fc) * 128
                            wg, wgk = wgs.next(); wu, wuk = wus.next()
                            P.add("sp", (lambda e, wg=wg, fc=fc, nf=nf: e.dma_start(out=sap(wg, 0, [[256, 8], [1, nf]]), in_=dap(wgb, fc * 128, [[DFF, 128], [128 * DFF, 8], [1, nf]]))),
                                  reads=["wgb"], writes=[wgk], dma_sem=wgk)
                            P.add("sp", (lambda e, wu=wu, fc=fc, nf=nf: e.dma_start(out=sap(wu, 0, [[256, 8], [1, nf]]), in_=dap(wub, fc * 128, [[DFF, 128], [128 * DFF, 8], [1, nf]]))),
                                  reads=["wub"], writes=[wuk], dma_sem=wuk)
                        fo = (fc % 2) * 128
                        for (tb0, n) in tbs:
                            gp, gk = gps.next(); up, uk = ups.next(); sg, sgk = sgs.next()
                            for wt, wk_, pp, ppk in ((wg, wgk, gp, gk), (wu, wuk, up, uk)):
                                for dc in range(8):
                                    P.add("pe", (lambda e, wt=wt, pp=pp, dc=dc, tb0=tb0, n=n, NTG=NTG, fo=fo: e.matmul(pp[:, 0:n], lhsT=wt[:, dc * 256 + fo: dc * 256 + fo + 128],
                                                                                                               rhs=hT[:, dc * NTG + tb0: dc * NTG + tb0 + n], start=(dc == 0), stop=(dc == 7))),
                                          reads=[wk_, "hT"], writes=[ppk])
                            P.add("act", (lambda e, gp=gp, sg=sg, n=n: e.activation(out=sg[:, 0:n], in_=gp[:, 0:n], func=AF.Silu)), reads=[gk], writes=[sgk])
                            P.add("dve", (lambda e, up=up, sg=sg, fc=fc, tb0=tb0, n=n, NTG=NTG: e.tensor_tensor(out=actT[:, fc * NTG + tb0: fc * NTG + tb0 + n], in0=up[:, 0:n],
                                                                                                              in1=sg[:, 0:n], op=ALU.mult)), reads=[uk, sgk], writes=["actT"])
                    for slot, T in enumerate(grp):
                        ypa, yka = yps.next(); ypb, ykb = yps.next()
                        for eh, yp in ((0, ypa), (1, ypb)):
                            for fc in range(NFC):
                                P.add("pe", (lambda e, yp=yp, eh=eh, fc=fc, slot=slot, NTG=NTG: e.matmul(yp[:, 0:512], lhsT=actT[:, fc * NTG + slot * 128: fc * NTG + (slot + 1) * 128],
                                                                                                        rhs=wd[:, fc * 1024 + eh * 512: fc * 1024 + (eh + 1) * 512], start=(fc == 0), stop=(fc == NFC - 1))),
                                      reads=["actT", "wd"], writes=[yka if eh == 0 else ykb])
                        ss, sk = sss.next(); rs, rk = rss.next(); ysb, ysk = ysbs.next()
                        sumsq([ypa[:, 0:512], ypb[:, 0:512]], [yka, ykb], ss, sk)
                        rstd_ops(ss, sk, rs, rk)
                        x1k = "x1_%d" % slot
                        for eh, yp, yk in ((0, ypa, yka), (1, ypb, ykb)):
                            P.add("dve", (lambda e, yp=yp, eh=eh, rs=rs, ysb=ysb: e.scalar_tensor_tensor(
                                out=ysb[:, eh * 512:(eh + 1) * 512], in0=yp[:, 0:512], scalar=rs[:, 0:1],
                                in1=gpffn[:, eh * 512:(eh + 1) * 512], op0=ALU.mult, op1=ALU.mult)), reads=[yk, rk, "gpffn"], writes=[ysk])
                        P.add("pool", (lambda e, ysb=ysb, slot=slot: e.tensor_tensor(out=ysb[:], in0=ysb[:], in1=x1[:, slot * 1024:(slot + 1) * 1024], op=ALU.add)),
                              reads=[ysk, x1k], writes=[ysk])
                        if T < 16:
                            P.add("sp", (lambda e, ysb=ysb, T=T: e.dma_start(out=yo[T * 128:(T + 1) * 128, :], in_=ysb[:])), reads=[ysk], dma_sem=ysk)
                        else:
                            P.add("sp", (lambda e, ysb=ysb: e.dma_start(out=ys[:, :], in_=ysb[:])), reads=[ysk], dma_sem=ysk)
                P.flush()
        P.flush(final=True)
    return nc


def _consts(hf):
    f32 = np.float32
    c = {}
    c["c_ident"] = np.eye(128, dtype=f32)
    R = np.zeros((128, 128), f32)
    for pp in range(128):
        if (pp % 64) < 32:
            R[pp + 32, pp] = -1.0
        else:
            R[pp - 32, pp] = 1.0
    c["c_rot"] = R
    inv = np.power(f32(10000.0), -(np.arange(32, dtype=f32) / f32(32))).astype(f32)
    pos = np.concatenate([(hf - 1) * 2048 + np.arange(2048), hf * 2048 + np.arange(2048), PAST + (np.arange(128) % 8)]).astype(f32)
    ang = (pos[:, None] * inv[None, :]).astype(f32)
    fidx = (np.arange(128) % 64) % 32
    c["c_cos"] = np.ascontiguousarray(np.cos(ang).astype(f32)[:, fidx].T)
    c["c_sin"] = np.ascontiguousarray(np.sin(ang).astype(f32)[:, fidx].T)
    kk = np.arange(128)[:, None]
    qq = np.arange(128)[None, :]
    prev = (kk >= qq).astype(f32)
    diag = (kk <= qq).astype(f32)
    c["c_mstd"] = np.concatenate([prev, diag, prev, diag], axis=1)
    c["c_mfirst"] = np.concatenate([prev * f32(hf), diag, prev * f32(hf), diag], axis=1)
    i8 = np.arange(8)
    msw = (np.arange(128)[:, None] >= i8[None, :]).astype(f32)
    c["c_msw"] = np.ascontiguousarray(np.tile(msw[:, None, :], (1, 96, 1)).reshape(128, 768))
    s_ = np.arange(128) // 8
    j_ = np.arange(128) % 8
    same = (s_[:, None] == s_[None, :])
    dji = j_[None, :] - j_[:, None]
    c["c_mnsw"] = (same & (dji >= 0)).astype(f32)
    mult_new = (dji >= 0).astype(f32) + ((dji >= 0) & (dji % 4 == 0)).astype(f32) + (dji == 0).astype(f32)
    c["c_mndil"] = (same.astype(f32) * mult_new).astype(f32)
    g = np.arange(128)[:, None, None]
    cc = np.arange(16)[None, :, None]
    ii = np.arange(8)[None, None, :]
    p3 = (cc == ii)
    p2 = ((cc % 4) == (ii % 4)) & ((g > 96) | ((g == 96) & (cc >= ii)))
    p1 = (g > 120) | ((g == 120) & (cc >= ii))
    mult = p3.astype(f32) + p2.astype(f32) + p1.astype(f32)
    c["c_mdil"] = np.ascontiguousarray(np.tile(mult[:, :, None, :], (1, 1, 2, 1)).reshape(128, 256))
    c["c_nhalf"] = np.full((128, 1), -0.5, f32)
    return c


_NC_CACHE = {}


def kernel(x_prompt, x_sample, cache_swa_k, cache_swa_v, cache_dil_k, cache_dil_v,
           cache_mem_k, cache_mem_v, mem_prompt, g_pre_mix, w_in, sinks, w_mem_kv, w_o,
           g_post_mix, g_pre_ffn, w_gate, w_up, w_down, g_post_ffn):
    f32 = np.float32
    A = lambda a: np.ascontiguousarray(np.asarray(a, dtype=f32))
    x_prompt = A(x_prompt); x_sample = A(x_sample)
    shared = {
        "w_in": A(w_in)[0], "w_mem": A(w_mem_kv)[0], "w_o": A(w_o)[0],
        "w_gate": A(w_gate)[0], "w_up": A(w_up)[0], "w_down": A(w_down)[0],
        "g_pre": A(g_pre_mix).reshape(1, D), "g_post": A(g_post_mix).reshape(1, D),
        "g_ffn": A(g_pre_ffn).reshape(1, D), "g_pffn": A(g_post_ffn).reshape(1, D),
        "sinks": A(sinks).reshape(1, 6),
    }
    csk = A(cache_swa_k)[0].reshape(128, 128, 128); csv = A(cache_swa_v)[0].reshape(128, 128, 128)
    cdk = A(cache_dil_k)[0].reshape(128, 2048, 384); cdv = A(cache_dil_v)[0].reshape(128, 2048, 384)
    cmk = A(cache_mem_k)[0].reshape(128, 256, 256); cmv = A(cache_mem_v)[0].reshape(128, 256, 256)
    mem_prompt = A(mem_prompt)
    cst = [_consts(0), _consts(1)]
    in_maps = []
    for c in range(8):
        b, hf = c // 2, c % 2
        m = dict(shared)
        m["xo"] = x_prompt[b, hf * 2048:(hf + 1) * 2048]
        m["xh"] = x_prompt[b, 0:2048] if hf == 1 else np.zeros((2048, D), f32)
        m["xs"] = x_sample[c * 16:(c + 1) * 16].reshape(128, D)
        m["memp"] = mem_prompt[b]
        sl = slice(c * 16, (c + 1) * 16)
        m["csk"] = csk[sl]; m["csv"] = csv[sl]; m["cdk"] = cdk[sl]; m["cdv"] = cdv[sl]
        m["cmk"] = cmk[sl]; m["cmv"] = cmv[sl]
        m.update(cst[hf])
        in_maps.append(m)
    if "nc" not in _NC_CACHE:
        _NC_CACHE["nc"] = build()
    res = run_bass_kernel_spmd(_NC_CACHE["nc"], in_maps, core_ids=list(range(8)))
    R = res.results
    y_prompt = np.zeros((4, 4096, D), f32); y_sample = np.zeros((128, 8, D), f32)
    p_swa_k = np.zeros((1, 4, 128, 2, 64), f32); p_swa_v = np.zeros_like(p_swa_k)
    p_dil_k = np.zeros((1, 4, 2048, 6, 64), f32); p_dil_v = np.zeros_like(p_dil_k)
    p_mem_k = np.zeros((1, 4, 256, 4, 64), f32); p_mem_v = np.zeros_like(p_mem_k)
    s_swa_k = np.zeros((1, 128, 128, 2, 64), f32); s_swa_v = np.zeros_like(s_swa_k)
    s_dil_k = np.zeros((1, 128, 2048, 6, 64), f32); s_dil_v = np.zeros_like(s_dil_k)
    for c in range(8):
        b, hf = c // 2, c % 2
        r = R[c]
        y_prompt[b, hf * 2048:(hf + 1) * 2048] = r["yo"]
        y_sample[c * 16:(c + 1) * 16] = r["ys"].reshape(16, 8, D)
        if hf == 1:
            p_swa_k[0, b] = r["o_swa_k"].reshape(128, 2, 64); p_swa_v[0, b] = r["o_swa_v"].reshape(128, 2, 64)
            p_dil_k[0, b] = r["o_dil_k"].reshape(2048, 6, 64); p_dil_v[0, b] = r["o_dil_v"].reshape(2048, 6, 64)
        else:
            p_mem_k[0, b] = r["o_mem_k"].reshape(256, 4, 64); p_mem_v[0, b] = r["o_mem_v"].reshape(256, 4, 64)
        sl = slice(c * 16, (c + 1) * 16)
        s_swa_k[0, sl] = r["s_swa_k"].reshape(16, 128, 2, 64); s_swa_v[0, sl] = r["s_swa_v"].reshape(16, 128, 2, 64)
        s_dil_k[0, sl] = r["s_dil_k"].reshape(16, 2048, 6, 64); s_dil_v[0, sl] = r["s_dil_v"].reshape(16, 2048, 6, 64)
    return (y_prompt, y_sample, p_swa_k, p_swa_v, p_dil_k, p_dil_v, p_mem_k, p_mem_v,
            s_swa_k, s_swa_v, s_dil_k, s_dil_v)
```

```python
import contextlib
import numpy as np
import concourse.bass as bass
import concourse.mybir as mybir
from concourse.ap import AP
from concourse.bass_utils import run_bass_kernel_spmd

F32 = mybir.dt.float32
BF16 = mybir.dt.bfloat16
AF = mybir.ActivationFunctionType
ALU = mybir.AluOpType

ENGS = ("pe", "act", "dve", "pool", "sp")
PSUM_RINGS = {"pT", "zp", "rp", "vp", "st", "ob", "sq", "oc", "on", "yp", "gp", "up", "tp"}
D = 1024
DFF = 2816
NFC = 22
EPS = 1e-6
SCALE = 0.125
NOWN = 2048
NS = 128
PAST = 16384


class Op:
    __slots__ = ("eng", "fn", "deps", "is_dma", "sem", "sem_target", "signal", "sig_idx", "emitted")

    def __init__(self, eng, fn, is_dma):
        self.eng = eng
        self.fn = fn
        self.deps = ()
        self.is_dma = is_dma
        self.sem = None
        self.sem_target = 0
        self.signal = False
        self.sig_idx = 0
        self.emitted = False


class Prog:
    def __init__(self, nc):
        self.nc = nc
        self.seg = {e: [] for e in ENGS}
        self.state = {}
        self.dma_sems = {}
        self.last_dma = {}
        self.last_compute = {}
        self.fence_deps = set()
        self.esems = {e: nc.alloc_semaphore("s_" + e) for e in ENGS}
        self.sigcount = {e: 0 for e in ENGS}
        self.waited = {e: {} for e in ENGS}

    def add(self, eng, fn, reads=(), writes=(), dma_sem=None, nofence=False):
        if _STOPPED[0]:
            return None
        op = Op(eng, fn, dma_sem is not None)
        if dma_sem is not None:
            ent = self.dma_sems.get(dma_sem)
            if ent is None:
                ent = [self.nc.alloc_semaphore("d_" + str(len(self.dma_sems))), 0]
                self.dma_sems[dma_sem] = ent
            ent[1] += 16
            op.sem = dma_sem
            op.sem_target = ent[1]
            if not nofence:
                self.last_dma[dma_sem] = op
        else:
            self.last_compute[eng] = op
        xr = [k for k in reads if k.split("#")[0] in PSUM_RINGS]
        if xr:
            reads = [k for k in reads if k not in xr]
            writes = list(writes) + xr
        deps = set(self.fence_deps)
        for k in reads:
            s = self.state.get(k)
            if s is None:
                s = [None, []]
                self.state[k] = s
            if s[0] is not None:
                deps.add(s[0])
            s[1].append(op)
        for k in writes:
            s = self.state.get(k)
            if s is None:
                s = [None, []]
                self.state[k] = s
            if s[0] is not None:
                deps.add(s[0])
            deps.update(s[1])
            s[0] = op
            s[1] = []
        deps.discard(op)
        op.deps = deps
        self.seg[eng].append(op)
        return op

    @staticmethod
    def _skip(d, op):
        return (not d.is_dma) and d.eng == op.eng and (not op.is_dma) and d.eng == "pe"

    def flush(self, final=False):
        nc = self.nc
        for e in ENGS:
            for op in self.seg[e]:
                for d in op.deps:
                    if d.is_dma or d.emitted or self._skip(d, op):
                        continue
                    d.signal = True
        for e in ENGS:
            lc = self.last_compute.get(e)
            if lc is not None and not lc.emitted:
                lc.signal = True
        for e in ENGS:
            for op in self.seg[e]:
                if op.signal and not op.is_dma:
                    self.sigcount[e] += 1
                    op.sig_idx = self.sigcount[e]
        with nc.Block() as block:
            engmap = {"pe": block.tensor, "act": block.scalar, "dve": block.vector,
                      "pool": block.gpsimd, "sp": block.sync}

            def make(e):
                def body(eng):
                    waited = self.waited[e]
                    for op in self.seg[e]:
                        need = {}
                        for d in op.deps:
                            if d.is_dma:
                                key = ("d", d.sem)
                                val = d.sem_target
                                h = self.dma_sems[d.sem][0]
                            else:
                                if self._skip(d, op):
                                    continue
                                if d.emitted and not d.signal:
                                    continue
                                key = ("e", d.eng)
                                val = d.sig_idx
                                h = self.esems[d.eng]
                            if waited.get(key, 0) >= val:
                                continue
                            if key not in need or need[key][1] < val:
                                need[key] = (h, val)
                        for key, (h, val) in need.items():
                            eng.wait_ge(h, val)
                            waited[key] = val
                        ins = op.fn(eng)
                        if op.is_dma:
                            ins.then_inc(self.dma_sems[op.sem][0], 16)
                        elif op.signal:
                            ins.then_inc(self.esems[e], 1)
                    if final and e == "sp":
                        for k, ent in self.dma_sems.items():
                            eng.wait_ge(ent[0], ent[1])
                return body

            for e in ENGS:
                engmap[e](make(e))
        for e in ENGS:
            for op in self.seg[e]:
                op.emitted = True
                op.fn = None
            self.seg[e] = []
        f = set(self.last_compute.values())
        f.update(self.last_dma.values())
        self.fence_deps = f


class Ring:
    def __init__(self, name, tiles):
        self.name = name
        self.tiles = tiles
        self.i = 0

    def next(self):
        k = self.i % len(self.tiles)
        self.i += 1
        return self.tiles[k], "%s#%d" % (self.name, k)


STOP = [99]


class _StopBuild(Exception):
    pass


_STOPPED = [False]


def chk(x):
    if STOP[0] <= x:
        _STOPPED[0] = True


def build():
    _STOPPED[0] = False
    nc = bass.Bass("TRN2", target_bir_lowering=False)
    P = Prog(nc)

    def din(name, shape):
        return nc.dram_tensor(name, list(shape), F32, kind="ExternalInput").ap()

    def dout(name, shape):
        return nc.dram_tensor(name, list(shape), F32, kind="ExternalOutput").ap()

    xo = din("xo", [NOWN, D]); xh = din("xh", [NOWN, D]); xs = din("xs", [NS, D])
    memp = din("memp", [256, D])
    w_in = din("w_in", [D, 2048]); w_mem = din("w_mem", [D, 512]); w_o = din("w_o", [D, D])
    w_gate = din("w_gate", [D, DFF]); w_up = din("w_up", [D, DFF]); w_down = din("w_down", [DFF, D])
    g_pre = din("g_pre", [1, D]); g_post = din("g_post", [1, D]); g_ffn = din("g_ffn", [1, D]); g_pffn = din("g_pffn", [1, D])
    sinks = din("sinks", [1, 6])
    csk = din("csk", [16, 128, 128]); csv = din("csv", [16, 128, 128])
    cdk = din("cdk", [16, 2048, 384]); cdv = din("cdv", [16, 2048, 384])
    cmk = din("cmk", [16, 256, 256]); cmv = din("cmv", [16, 256, 256])
    c_ident = din("c_ident", [128, 128]); c_rot = din("c_rot", [128, 128])
    c_cos = din("c_cos", [128, 4224]); c_sin = din("c_sin", [128, 4224])
    c_mstd = din("c_mstd", [128, 512]); c_mfirst = din("c_mfirst", [128, 512])
    c_msw = din("c_msw", [128, 768]); c_mnsw = din("c_mnsw", [128, 128])
    c_mdil = din("c_mdil", [128, 256]); c_mndil = din("c_mndil", [128, 128])
    c_nhalf = din("c_nhalf", [128, 1])

    yo = dout("yo", [NOWN, D]); ys = dout("ys", [NS, D])
    o_swa_k = dout("o_swa_k", [128, 128]); o_swa_v = dout("o_swa_v", [128, 128])
    o_dil_k = dout("o_dil_k", [NOWN, 384]); o_dil_v = dout("o_dil_v", [NOWN, 384])
    o_mem_k = dout("o_mem_k", [256, 256]); o_mem_v = dout("o_mem_v", [256, 256])
    s_swa_k = dout("s_swa_k", [16, 128, 128]); s_swa_v = dout("s_swa_v", [16, 128, 128])
    s_dil_k = dout("s_dil_k", [16, 2048, 384]); s_dil_v = dout("s_dil_v", [16, 2048, 384])

    def dap(t, off, dims):
        return AP(t.tensor, off, [list(d) for d in dims])

    es = contextlib.ExitStack()
    with es:
        uid = [0]

        def sb(name, cols, dt=BF16, st=None):
            uid[0] += 1
            return (st or es).enter_context(nc.sbuf_tensor("%s_%d" % (name, uid[0]), [128, cols], dt))

        def ps(name, cols, dt=F32, st=None):
            uid[0] += 1
            return (st or es).enter_context(nc.psum_tensor("%s_%d" % (name, uid[0]), [128, cols], dt))

        def sap(t, off, dims, nparts=128, p0=0):
            W = t.shape[1]
            return AP(t, p0 * W + off, [[W, nparts]] + [list(d) for d in dims])

        ident = sb("ident", 128); rot = sb("rot", 128)
        mstd = sb("mstd", 512); mfirst = sb("mfirst", 512)
        msw = sb("msw", 768); mnsw = sb("mnsw", 128); mdil = sb("mdil", 256); mndil = sb("mndil", 128)
        nhalf = sb("nhalf", 1, F32)
        esink = sb("esink", 6, F32)
        catT = sb("catT", 8 * 2176)
        CT = 2176

        for dst, src, key in ((ident, c_ident, "ident"), (rot, c_rot, "rot"), (mstd, c_mstd, "mstd"),
                              (mfirst, c_mfirst, "mfirst"), (msw, c_msw, "msw"), (mnsw, c_mnsw, "mnsw"),
                              (mdil, c_mdil, "mdil"), (mndil, c_mndil, "mndil")):
            P.add("pool", (lambda e, d=dst, s=src: e.dma_start(out=d[:], in_=s[:, :])), writes=[key], dma_sem="c_" + key)
        P.add("sp", lambda e: e.dma_start(out=nhalf[:], in_=c_nhalf[:, :]), writes=["nhalf"], dma_sem="c_nhalf")
        P.add("sp", lambda e: e.dma_start(out=esink[:], in_=dap(sinks, 0, [[0, 128], [1, 6]])), writes=["esink"], dma_sem="c_sink")
        P.add("act", lambda e: e.activation(out=esink[:], in_=esink[:], func=AF.Exp), reads=["esink"], writes=["esink"])

        epsb = sb("epsb", 1, F32)
        P.add("pool", lambda e: e.memset(epsb[:], EPS), writes=["epsb"])

        def rstd_ops(ss, sk, rs, rk):
            P.add("act", (lambda e: e.activation(out=rs[:], in_=ss[:], func=AF.Sqrt, scale=1.0 / D, bias=epsb[:, 0:1])), reads=[sk, "epsb"], writes=[rk])
            P.add("dve", (lambda e: e.reciprocal(out=rs[:], in_=rs[:])), reads=[rk], writes=[rk])

        identf = sb("identf", 128, F32)
        P.add("sp", lambda e: e.dma_start(out=identf[:], in_=c_ident[:, :]), writes=["identf"], dma_sem="c_identf")
        pending_cp = []
        for s_ in range(16):
            pending_cp.append((s_dil_k, cdk, s_, "cp_dk"))
            pending_cp.append((s_dil_v, cdv, s_, "cp_dv"))
        cp_ctr = [0]

        def maybe_copy(force=False):
            cp_ctr[0] += 1
            if pending_cp and (force or cp_ctr[0] % 7 == 0):
                dst, src, s_, key = pending_cp.pop(0)
                P.add("act", (lambda e: e.dma_start(out=dst[s_, 0:2040, :], in_=src[s_, 8:2048, :])), dma_sem=key, nofence=True)

        wob = nc.dram_tensor("wob", [D, D], BF16, kind="Internal").ap()
        wgb = nc.dram_tensor("wgb", [D, DFF], BF16, kind="Internal").ap()
        wub = nc.dram_tensor("wub", [D, DFF], BF16, kind="Internal").ap()
        wdb = nc.dram_tensor("wdb", [DFF, D], BF16, kind="Internal").ap()

        uts = nc.dram_tensor("uts", [128, 8 * 4224], BF16, kind="Internal").ap()

        def cast_weights():
            P.add("pool", lambda e: e.dma_start(out=wob[:, :], in_=w_o[:, :]), writes=["wob"], dma_sem="wob")
            P.add("pool", lambda e: e.dma_start(out=wdb[:, :], in_=w_down[:, :]), writes=["wdb"], dma_sem="wdb")
            P.add("pool", lambda e: e.dma_start(out=dap(wgb, 0, [[DFF, D], [1408, 2], [1, 1408]]), in_=dap(w_gate, 0, [[DFF, D], [1408, 2], [1, 1408]])), writes=["wgb"], dma_sem="wgb")
            P.add("pool", lambda e: e.dma_start(out=dap(wub, 0, [[DFF, D], [1408, 2], [1, 1408]]), in_=dap(w_up, 0, [[DFF, D], [1408, 2], [1, 1408]])), writes=["wub"], dma_sem="wub")
        P.add("act", lambda e: e.dma_start(out=s_swa_k[:, 0:120, :], in_=csk[:, 8:128, :]), dma_sem="cp_sk", nofence=True)
        P.add("act", lambda e: e.dma_start(out=s_swa_v[:, 0:120, :], in_=csv[:, 8:128, :]), dma_sem="cp_sv", nofence=True)

        P.flush()
        for pas in range(4):
            if STOP[0] <= 2 * pas:
                break
            try:
              with contextlib.ExitStack() as pst:
                  if pas == 0:
                      QA = sb("QA", 3 * 2176, st=pst)
                      KAx = sb("KAx", 2304, st=pst); KAy = sb("KAy", 2304, st=pst)
                      QX = sb("QX", 2 * 2176, st=pst)
                      VN = sb("VA", 18 * 320, st=pst)
                      MKT = sb("MKT", 512, st=pst)
                      MV = sb("MV", 2 * 384, st=pst)
                      vbufs = [(VN, "VN")]
                      vmem = [(MV, "MV")]
                  else:
                      QB = sb("QB", 2176, st=pst)
                      KB = sb("KB", 4224, st=pst)
                      VN = sb("V1", 18 * 192, st=pst)
                      V4 = sb("V4", 20 * 192, st=pst)
                      V16 = sb("V16", 32 * 192, st=pst)
                      vbufs = [(VN, "VN"), (V4, "V4"), (V16, "V16")]
                      vmem = []
                  if pas == 0:
                      P.add("pool", lambda e: e.memset(sap(VN, 64, [[320, 18], [128, 2], [1, 64]]), 1.0), writes=["VN_ones"])
                      P.add("pool", lambda e: e.memset(sap(MV, 64, [[192, 4], [1, 64]]), 1.0), writes=["MV_ones"])
                  else:
                      for vt, vk in vbufs:
                          P.add("pool", (lambda e, t=vt: e.memset(sap(t, 64, [[192, t.shape[1] // 192], [1, 64]]), 1.0)), writes=[vk + "_ones"])

                  with contextlib.ExitStack() as ast:
                      wcols = 1024 if pas == 0 else 384
                      wb = sb("wb", 8 * wcols, st=ast)
                      uT = sb("uT", 8 * 2048, st=ast)
                      gpre = sb("gpre", D, F32, st=ast)
                      xts = Ring("xt", [sb("xt%d" % i, D, F32, st=ast) for i in range(3)])
                      hbs = Ring("hb", [sb("hb%d" % i, D, st=ast) for i in range(3)])
                      junk = sb("junk", D, st=ast)
                      sss = Ring("ss", [sb("ss%d" % i, 1, F32, st=ast) for i in range(3)])
                      rss = Ring("rs", [sb("rs%d" % i, 1, F32, st=ast) for i in range(3)])
                      coss = Ring("cos", [sb("cos%d" % i, 512, F32, st=ast) for i in range(2)])
                      sins = Ring("sin", [sb("sin%d" % i, 512, F32, st=ast) for i in range(2)])
                      zbs = Ring("zb", [sb("zb%d" % i, 512, st=ast) for i in range(2)])
                      t1s = Ring("t1", [sb("t1_%d" % i, 512, F32, st=ast) for i in range(2)])
                      t2s = Ring("t2", [sb("t2_%d" % i, 512, F32, st=ast) for i in range(2)])
                      ksts = Ring("kst", [sb("kst%d" % i, 512, st=ast) for i in range(2)])
                      pTs = Ring("pT", [ps("pT%d" % i, 1024, BF16, st=ast) for i in range(2)])
                      zps = Ring("zp", [ps("zp%d" % i, 512, st=ast) for i in range(2)])
                      rps = Ring("rp", [ps("rp%d" % i, 512, st=ast) for i in range(2)])
                      vps = Ring("vp", [ps("vp%d" % i, 512, st=ast) for i in range(2)])

                      P.add("sp", lambda e: e.dma_start(out=gpre[:], in_=dap(g_pre, 0, [[0, 128], [1, D]])), writes=["gpre"], dma_sem="gpre")

                      if pas == 0:
                          segs = [(0, 384, 0), (384, 128, 384), (448, 64, 512), (384, 64, 576), (1792, 256, 640), (512, 128, 896)]
                      else:
                          p_ = pas - 1
                          segs = [(640 + 128 * p_, 128, 0), (1024 + 128 * p_, 128, 128), (1408 + 128 * p_, 128, 256)]
                      for (sc, n, off) in segs:
                          P.add("pool", (lambda e, sc=sc, n=n, off=off: e.dma_start(
                              out=sap(wb, off, [[wcols, 8], [1, n]]),
                              in_=dap(w_in, sc, [[2048, 128], [128 * 2048, 8], [1, n]]))),
                              writes=["wb"], dma_sem="wb")

                      def ut_store(tok0, ntok, skey):
                          nb = (ntok + 511) // 512
                          P.add("sp", (lambda e: e.dma_start(out=dap(uts, tok0, [[8 * 4224, 128], [4224, 8], [1, ntok]]), in_=sap(uT, 0, [[2048, 8], [1, ntok]]))),
                                reads=["uT%d" % b for b in range(nb)], writes=[skey], dma_sem=skey)

                      def ut_load(tok0, ntok, skey):
                          for b in range((ntok + 511) // 512):
                              n_ = min(512, ntok - b * 512)
                              P.add("sp", (lambda e, b=b, n_=n_: e.dma_start(out=sap(uT, b * 512, [[2048, 8], [1, n_]]), in_=dap(uts, tok0 + b * 512, [[8 * 4224, 128], [4224, 8], [1, n_]]))),
                                    reads=[skey], writes=["uT%d" % b], dma_sem="uTl%d" % b)

                      def norm_T(xsrc, row0, ntok, gate_key="uT"):
                          npend = []
                          for t in range(ntok // 128):
                              xt, xk = xts.next(); hb, hk = hbs.next(); ss, sk = sss.next(); rs, rk = rss.next()
                              pT, pk = pTs.next()
                              r0 = row0 + t * 128
                              maybe_copy()
                              P.add("sp", (lambda e, xt=xt, r0=r0: e.dma_start(out=xt[:], in_=xsrc[r0:r0 + 128, :])), writes=[xk], dma_sem=xk)
                              P.add("act", (lambda e, xt=xt, ss=ss: e.activation(out=junk[:], in_=xt[:], func=AF.Square, accum_out=ss[:])),
                                    reads=[xk], writes=["junk", sk])
                              rstd_ops(ss, sk, rs, rk)
                              P.add("dve", (lambda e, xt=xt, rs=rs, hb=hb: e.scalar_tensor_tensor(out=hb[:], in0=xt[:], scalar=rs[:, 0:1], in1=gpre[:], op0=ALU.mult, op1=ALU.mult)),
                                    reads=[xk, rk, "gpre"], writes=[hk])
                              for dc in range(8):
                                  P.add("pe", (lambda e, pT=pT, hb=hb, dc=dc: e.transpose(out=pT[:, dc * 128:(dc + 1) * 128], in_=hb[:, dc * 128:(dc + 1) * 128], identity=ident[:])),
                                        reads=[hk, "ident"], writes=[pk])
                              def ev(pT=pT, pk=pk, t=t):
                                  P.add("act", (lambda e: e.activation(out=sap(uT, t * 128, [[2048, 8], [1, 128]]), in_=sap(pT, 0, [[128, 8], [1, 128]]), func=AF.Copy)),
                                        reads=[pk], writes=["uT%d" % (t // 4)])
                              npend.append(ev)
                              if len(npend) > 1:
                                  npend.pop(0)()
                          while npend:
                              npend.pop(0)()

                      def fm_proj(woff, dst, dkey, dcol, tb0, ntb, ctab0, rope):
                          zp, zk = zps.next()
                          for dc in range(8):
                              P.add("pe", (lambda e, zp=zp, dc=dc: e.matmul(zp[:, 0:ntb], lhsT=wb[:, dc * wcols + woff: dc * wcols + woff + 128],
                                                                             rhs=uT[:, dc * 2048 + tb0: dc * 2048 + tb0 + ntb], start=(dc == 0), stop=(dc == 7))),
                                    reads=["wb", "uT%d" % (tb0 // 512)], writes=[zk])
                          if not rope:
                              P.add("act", (lambda e, zp=zp: e.activation(out=dst[:, dcol:dcol + ntb], in_=zp[:, 0:ntb], func=AF.Copy)),
                                    reads=[zk], writes=[dkey])
                              return
                          zb, zbk = zbs.next(); rp, rk = rps.next(); t1, t1k = t1s.next(); t2, t2k = t2s.next()
                          cs, ck = cur_tab["cos"]; sn, snk = cur_tab["sin"]
                          P.add("act", (lambda e, zp=zp, zb=zb: e.activation(out=zb[:, 0:ntb], in_=zp[:, 0:ntb], func=AF.Copy)), reads=[zk], writes=[zbk])
                          P.add("pe", (lambda e, rp=rp, zb=zb: e.matmul(rp[:, 0:ntb], lhsT=rot[:], rhs=zb[:, 0:ntb], start=True, stop=True)),
                                reads=[zbk, "rot"], writes=[rk])
                          P.add("dve", (lambda e, zp=zp, t1=t1, cs=cs: e.tensor_tensor(out=t1[:, 0:ntb], in0=zp[:, 0:ntb], in1=cs[:, 0:ntb], op=ALU.mult)),
                                reads=[zk, ck], writes=[t1k])
                          P.add("dve", (lambda e, rp=rp, t2=t2, sn=sn: e.tensor_tensor(out=t2[:, 0:ntb], in0=rp[:, 0:ntb], in1=sn[:, 0:ntb], op=ALU.mult)),
                                reads=[rk, snk], writes=[t2k])
                          P.add("pool", (lambda e, t1=t1, t2=t2: e.tensor_tensor(out=dst[:, dcol:dcol + ntb], in0=t1[:, 0:ntb], in1=t2[:, 0:ntb], op=ALU.add)),
                                reads=[t1k, t2k], writes=[dkey])

                      cur_tab = {}

                      def load_tab(tcol, ntb):
                          cs, ck = coss.next(); sn, snk = sins.next()
                          P.add("sp", (lambda e, cs=cs: e.dma_start(out=cs[:, 0:ntb], in_=c_cos[:, tcol:tcol + ntb])), writes=[ck], dma_sem=ck)
                          P.add("sp", (lambda e, sn=sn: e.dma_start(out=sn[:, 0:ntb], in_=c_sin[:, tcol:tcol + ntb])), writes=[snk], dma_sem=snk)
                          cur_tab["cos"] = (cs, ck); cur_tab["sin"] = (sn, snk)

                      def tm_proj(woff, ncols, tok_off, tok_step, dst, dkey, blk):
                          vp, vk = vps.next()
                          last = tok_off + 127 * tok_step
                          ukeys = ["uT%d" % b for b in range(tok_off // 512, last // 512 + 1)]
                          for dc in range(8):
                              P.add("pe", (lambda e, vp=vp, dc=dc: e.matmul(vp[:, 0:ncols], lhsT=sap(uT, dc * 2048 + tok_off, [[tok_step, 128]]),
                                                                             rhs=wb[:, dc * wcols + woff: dc * wcols + woff + ncols], start=(dc == 0), stop=(dc == 7))),
                                    reads=["wb"] + ukeys, writes=[vk])
                          bs = 320 if pas == 0 else 192
                          P.add("act", (lambda e, vp=vp: e.activation(out=sap(dst, blk * bs, [[128, 2], [1, 64]]), in_=sap(vp, 0, [[64, 2], [1, 64]]), func=AF.Copy)),
                                reads=[vk], writes=[dkey])
                          if pas == 0:
                              P.add("act", (lambda e, vp=vp: e.activation(out=dst[:, blk * 320 + 256: blk * 320 + 320], in_=vp[:, 0:64], func=AF.Copy)),
                                    reads=[vk], writes=[dkey])

                      def k_out(src, skey, scol, ntile, emit_dma):
                          pT, pk = pTs.next(); kst, kk = ksts.next()
                          for j in range(ntile):
                              P.add("pe", (lambda e, pT=pT, j=j: e.transpose(out=pT[:, j * 128:(j + 1) * 128], in_=src[:, scol + j * 128: scol + (j + 1) * 128], identity=ident[:])),
                                    reads=[skey, "ident"], writes=[pk])
                          P.add("act", (lambda e, pT=pT, kst=kst: e.activation(out=kst[:, 0:ntile * 128], in_=pT[:, 0:ntile * 128], func=AF.Copy)),
                                reads=[pk], writes=[kk])
                          emit_dma(kst, kk)

                      chk(0.1)
                      if pas == 0:
                          wm = sb("wm", 8 * 512, st=ast)
                          P.add("pool", lambda e: e.dma_start(out=sap(wm, 0, [[512, 8], [1, 512]]), in_=dap(w_mem, 0, [[512, 128], [128 * 512, 8], [1, 512]])),
                                writes=["wm"], dma_sem="wm")
                          for blk in range(2):
                              hb, hk = hbs.next(); pT, pk = pTs.next()
                              P.add("pool", (lambda e, hb=hb, blk=blk: e.dma_start(out=hb[:], in_=memp[blk * 128:(blk + 1) * 128, :])), writes=[hk], dma_sem=hk)
                              for dc in range(8):
                                  P.add("pe", (lambda e, pT=pT, hb=hb, dc=dc: e.transpose(out=pT[:, dc * 128:(dc + 1) * 128], in_=hb[:, dc * 128:(dc + 1) * 128], identity=ident[:])),
                                        reads=[hk, "ident"], writes=[pk])
                              P.add("act", (lambda e, pT=pT, blk=blk: e.activation(out=sap(uT, blk * 128, [[2048, 8], [1, 128]]), in_=sap(pT, 0, [[128, 8], [1, 128]]), func=AF.Copy)),
                                    reads=[pk], writes=["uT0"])
                          for ch in range(2):
                              zp, zk = zps.next()
                              for dc in range(8):
                                  P.add("pe", (lambda e, zp=zp, dc=dc, ch=ch: e.matmul(zp[:, 0:256], lhsT=wm[:, dc * 512 + ch * 128: dc * 512 + (ch + 1) * 128],
                                                                                       rhs=uT[:, dc * 2048: dc * 2048 + 256], start=(dc == 0), stop=(dc == 7))),
                                        reads=["wm", "uT0"], writes=[zk])
                              P.add("act", (lambda e, zp=zp, ch=ch: e.activation(out=MKT[:, ch * 256:(ch + 1) * 256], in_=zp[:, 0:256], func=AF.Copy)),
                                    reads=[zk], writes=["MKT"])
                          mkn = sb("mkn", 512, st=ast)
                          for blk in range(2):
                              vp, vk = vps.next()
                              for dc in range(8):
                                  P.add("pe", (lambda e, vp=vp, dc=dc, blk=blk: e.matmul(vp[:, 0:512], lhsT=uT[:, dc * 2048 + blk * 128: dc * 2048 + (blk + 1) * 128],
                                                                                         rhs=wm[:, dc * 512:(dc + 1) * 512], start=(dc == 0), stop=(dc == 7))),
                                        reads=["wm", "uT0"], writes=[vk])
                              P.add("act", (lambda e, vp=vp, blk=blk: e.activation(out=mkn[:, blk * 256:(blk + 1) * 256], in_=vp[:, 0:256], func=AF.Copy)),
                                    reads=[vk], writes=["mkn"])
                              P.add("act", (lambda e, vp=vp, blk=blk: e.activation(out=sap(MV, blk * 384, [[192, 2], [128, 2], [1, 64]]), in_=sap(vp, 256, [[128, 2], [64, 2], [1, 64]]), func=AF.Copy)),
                                    reads=[vk], writes=["MV"])
                          P.add("pool", lambda e: e.dma_start(out=dap(o_mem_k, 0, [[256, 128], [128 * 256, 2], [1, 256]]), in_=sap(mkn, 0, [[256, 2], [1, 256]])),
                                reads=["mkn"], dma_sem="o_mem_k")
                          for blk in range(2):
                              for pr in range(2):
                                  P.add("pool", (lambda e, blk=blk, pr=pr: e.dma_start(out=dap(o_mem_v, blk * 128 * 256 + pr * 128, [[256, 128], [64, 2], [1, 64]]),
                                                                                       in_=sap(MV, blk * 384 + pr * 192, [[128, 2], [1, 64]]))),
                                        reads=["MV"], dma_sem="o_mem_v")

                          chk(0.2)
                          norm_T(xh, 1920, 128)
                          chk(0.21)
                          load_tab(1920, 128)
                          chk(0.22)
                          fm_proj(384, KAx, "KAx", 0, 0, 128, None, True)
                          chk(0.23)
                          fm_proj(512, KAy, "KAy", 0, 0, 128, None, True)
                          chk(0.24)
                          tm_proj(896, 128, 0, 1, VN, "VN", 0)
                          chk(0.3)
                          norm_T(xo, 0, 2048)
                          ut_store(2048, 2048, "uts_own")
                          chk(0.4)
                          for tb in range(4):
                              load_tab(2048 + tb * 512, 512)
                              for c in range(3):
                                  fm_proj(c * 128, QA, "QA", c * 2176 + tb * 512, tb * 512, 512, None, True)
                              fm_proj(384, KAx, "KAx", 128 + tb * 512, tb * 512, 512, None, True)
                              fm_proj(512, KAy, "KAy", 128 + tb * 512, tb * 512, 512, None, True)
                              for c in range(2):
                                  fm_proj(640 + c * 128, QX, "QX", c * 2176 + tb * 512, tb * 512, 512, None, False)
                          for j in range(16):
                              tm_proj(896, 128, j * 128, 1, VN, "VN", 1 + j)
                          chk(0.5)
                          k_out(KAx, "KAx", 128 + 15 * 128, 1,
                                lambda kst, kk: P.add("pool", (lambda e: e.dma_start(out=o_swa_k[:, :], in_=kst[:, 0:128])), reads=[kk], dma_sem=kk))
                          P.add("pool", lambda e: e.dma_start(out=dap(o_swa_v, 0, [[128, 128], [64, 2], [1, 64]]), in_=sap(VN, 16 * 320, [[128, 2], [1, 64]])), reads=["VN"], dma_sem="o_swa_v")
                          chk(0.6)
                          norm_T(xs, 0, 128)
                          ut_store(4096, 128, "uts_smp")
                          load_tab(4096, 128)
                          for c in range(3):
                              fm_proj(c * 128, QA, "QA", c * 2176 + 2048, 0, 128, None, True)
                          fm_proj(384, KAx, "KAx", 2176, 0, 128, None, True)
                          fm_proj(512, KAy, "KAy", 2176, 0, 128, None, True)
                          for c in range(2):
                              fm_proj(640 + c * 128, QX, "QX", c * 2176 + 2048, 0, 128, None, False)
                          tm_proj(896, 128, 0, 1, VN, "VN", 17)

                          def sw_new_k(kst, kk):
                              for s in range(16):
                                  P.add("pool", (lambda e, s=s: e.dma_start(out=s_swa_k[s, 120:128, :], in_=kst[s * 8:(s + 1) * 8, 0:128])), reads=[kk], dma_sem=kk)
                          k_out(KAx, "KAx", 2176, 1, sw_new_k)
                          for s in range(16):
                              P.add("pool", (lambda e, s=s: e.dma_start(out=dap(s_swa_v, s * 128 * 128 + 120 * 128, [[128, 8], [64, 2], [1, 64]]), in_=sap(VN, 17 * 320, [[128, 2], [1, 64]], nparts=8, p0=s * 8))),
                                    reads=["VN"], dma_sem="s_swa_v_new")
                      else:
                          p_ = pas - 1
                          if pas == 1:
                              norm_T(xh, 0, 2048)
                              ut_store(0, 2048, "uts_halo")
                          else:
                              ut_load(0, 2048, "uts_halo")
                          for tb in range(4):
                              load_tab(tb * 512, 512)
                              fm_proj(128, KB, "KB", tb * 512, tb * 512, 512, None, True)
                          tm_proj(256, 128, 15 * 128, 1, VN, "VN", 0)
                          for c in range(4):
                              tm_proj(256, 128, 1536 + c, 4, V4, "V4", c)
                          for c in range(16):
                              tm_proj(256, 128, c, 16, V16, "V16", c)
                          ut_load(2048, 2048, "uts_own")
                          for tb in range(4):
                              load_tab(2048 + tb * 512, 512)
                              fm_proj(0, QB, "QB", tb * 512, tb * 512, 512, None, True)
                              fm_proj(128, KB, "KB", 2048 + tb * 512, tb * 512, 512, None, True)
                          for j in range(16):
                              tm_proj(256, 128, j * 128, 1, VN, "VN", 1 + j)
                          for n in range(4):
                              for c in range(4):
                                  tm_proj(256, 128, n * 512 + c, 4, V4, "V4", (n + 1) * 4 + c)
                          for c in range(16):
                              tm_proj(256, 128, c, 16, V16, "V16", 16 + c)
                          for q4 in range(4):
                              k_out(KB, "KB", 2048 + q4 * 512, 4,
                                    lambda kst, kk, q4=q4: P.add("pool", (lambda e: e.dma_start(
                                        out=dap(o_dil_k, q4 * 512 * 384 + p_ * 128, [[384, 128], [128 * 384, 4], [1, 128]]),
                                        in_=sap(kst, 0, [[128, 4], [1, 128]]))), reads=[kk], dma_sem=kk))
                          for hd in range(2):
                              P.add("pool", (lambda e, hd=hd: e.dma_start(out=dap(o_dil_v, p_ * 128 + hd * 64, [[384, 128], [128 * 384, 16], [1, 64]]),
                                                                          in_=sap(VN, 192 + hd * 128, [[192, 16], [1, 64]]))), reads=["VN"], dma_sem="o_dil_v")
                          ut_load(4096, 128, "uts_smp")
                          load_tab(4096, 128)
                          fm_proj(0, QB, "QB", 2048, 0, 128, None, True)
                          fm_proj(128, KB, "KB", 4096, 0, 128, None, True)
                          tm_proj(256, 128, 0, 1, VN, "VN", 17)

                          def dil_new_k(kst, kk):
                              for s in range(16):
                                  P.add("pool", (lambda e, s=s: e.dma_start(out=s_dil_k[s, 2040:2048, p_ * 128:(p_ + 1) * 128], in_=kst[s * 8:(s + 1) * 8, 0:128])),
                                        reads=[kk], dma_sem=kk)
                          k_out(KB, "KB", 4096, 1, dil_new_k)
                          for s in range(16):
                              P.add("pool", (lambda e, s=s: e.dma_start(out=dap(s_dil_v, s * 2048 * 384 + 2040 * 384 + p_ * 128, [[384, 8], [64, 2], [1, 64]]), in_=sap(VN, 17 * 192, [[128, 2], [1, 64]], nparts=8, p0=s * 8))),
                                    reads=["VN"], dma_sem="s_dil_v_new")

                      P.flush()
                  if STOP[0] <= 2 * pas + 1:
                      break
                  with contextlib.ExitStack() as bst:
                      acc = sb("acc", 2 * 2176, F32, st=bst)
                      rl = sb("rl", 2176, F32, st=bst)
                      pts = Ring("pt", [sb("pt%d" % i, 512, st=bst) for i in range(4)])

                      if pas == 0:
                          cast_weights()
                      accs = sb("accs", 6 * 128, F32, st=bst)
                      accx = sb("accx", 4 * 128, F32, st=bst)
                      with contextlib.ExitStack() as sst:
                          k32s = Ring("k32", [sb("k32_%d" % i, 2048, F32, st=sst) for i in range(2)])
                          v32s = Ring("v32", [sb("v32_%d" % i, 2048, F32, st=sst) for i in range(2)])
                          kdts = Ring("kdt", [sb("kdt%d" % i, 2048, st=sst) for i in range(2)])
                          ptn = sb("ptn", 256, st=sst)
                          tps = Ring("tp", [ps("tp%d" % i, 512, st=sst) for i in range(2)])
                          sqs = Ring("sq", [ps("sq%d" % i, 512, st=sst) for i in range(2)])
                          snr = Ring("st", [ps("sn%d" % i, 512, st=sst) for i in range(2)])
                          ocA = ps("ocA", 512, st=sst); ocB = ps("ocB", 512, st=sst)

                          def transposes32(src, skey, ncols_src, n, dst, dkey):
                              for q in range((n + 3) // 4):
                                  tp, tpk = tps.next()
                                  m_ = min(4, n - q * 4)
                                  for c4 in range(m_):
                                      c = q * 4 + c4
                                      P.add("pe", (lambda e, tp=tp, c4=c4, c=c: e.transpose(out=tp[:, c4 * 128:(c4 + 1) * 128], in_=src[:, c * 128:(c + 1) * 128], identity=identf[:])),
                                            reads=[skey, "identf"], writes=[tpk])
                                  P.add("act", (lambda e, tp=tp, q=q, m_=m_: e.activation(out=dst[:, q * 512: q * 512 + m_ * 128], in_=tp[:, 0:m_ * 128], func=AF.Copy)),
                                        reads=[tpk], writes=[dkey])

                          def new_keys(pairs, ktiles, qt, qbase, kcol, vt, vkey, voffs, masknew, mnkey, outs):
                              for i, (half, (kt, kkey)) in enumerate(zip((0, 1), ktiles)):
                                  sn, snk = snr.next()
                                  P.add("pe", (lambda e, sn=sn, kt=kt, half=half, i=i: e.matmul(sn[:, 0:128], lhsT=sap(kt, kcol, [[1, 128]], nparts=64, p0=half * 64),
                                                                                                rhs=sap(qt, qbase, [[1, 128]], nparts=64, p0=half * 64), start=True, stop=True)),
                                        reads=[kkey, qkey_s], writes=[snk])
                                  P.add("act", (lambda e, sn=sn, i=i: e.activation(out=ptn[:, i * 128:(i + 1) * 128], in_=sn[:, 0:128], func=AF.Exp, scale=SCALE)),
                                        reads=[snk], writes=["ptn"])
                              P.add("pool", (lambda e: e.tensor_tensor(out=sap(ptn, 0, [[128, 2], [1, 128]]), in0=sap(ptn, 0, [[128, 2], [1, 128]]),
                                                                       in1=sap(masknew, 0, [[0, 2], [1, 128]]), op=ALU.mult)), reads=["ptn", mnkey], writes=["ptn"])
                              for i in range(2):
                                  ob_, ok_, ocol = outs[i]
                                  P.add("pe", (lambda e, i=i, ob_=ob_, ocol=ocol: e.matmul(ob_[:, ocol:ocol + 128], lhsT=vt[:, voffs[i]: voffs[i] + 128],
                                                                                           rhs=ptn[:, i * 128:(i + 1) * 128], start=True, stop=True)),
                                        reads=[vkey, vkey + "_ones", "ptn"], writes=[ok_])

                          if pas == 0:
                              SWOFF = {(0, 0): 0, (1, 1): 64, (0, 1): 128, (1, 0): 192}
                              qkey_s = "QA"
                              vc = sb("vc", 16 * 320, st=sst)
                              qbs = sb("qbs", 768, st=sst)
                              pts_ = sb("pts_", 768, st=sst)
                              P.add("pool", lambda e: e.memset(sap(vc, 64, [[320, 16], [128, 2], [1, 64]]), 1.0), writes=["vc_ones"])
                              P.add("pool", lambda e: e.memset(qbs[:], 0.0), writes=["qbs"])
                              for h in range(6):
                                  P.add("dve", (lambda e, h=h: e.tensor_copy(out=sap(qbs, h * 8, [[48, 16], [1, 8]], nparts=64, p0=(h // 3) * 64),
                                                                             in_=sap(QA, (h // 2) * 2176 + 2048, [[8, 16], [1, 8]], nparts=64, p0=(h % 2) * 64))),
                                        reads=["QA"], writes=["qbs"])
                              k32, k32k = k32s.next(); v32, v32k = v32s.next(); kdt, kdtk = kdts.next()
                              P.add("sp", lambda e: e.dma_start(out=sap(k32, 0, [[128, 16], [1, 128]]), in_=dap(csk, 0, [[128, 128], [128 * 128, 16], [1, 128]])), writes=[k32k], dma_sem=k32k)
                              P.add("act", lambda e: e.dma_start(out=sap(v32, 0, [[128, 16], [1, 128]]), in_=dap(csv, 0, [[128, 128], [128 * 128, 16], [1, 128]])), writes=[v32k], dma_sem=v32k)
                              P.add("pool", lambda e: e.tensor_copy(out=sap(vc, 0, [[320, 16], [128, 2], [1, 64]]), in_=sap(v32, 0, [[128, 16], [64, 2], [1, 64]])), reads=[v32k], writes=["vc"])
                              P.add("pool", lambda e: e.tensor_copy(out=sap(vc, 256, [[320, 16], [1, 64]]), in_=sap(v32, 0, [[128, 16], [1, 64]])), reads=[v32k], writes=["vc"])
                              transposes32(k32, k32k, 2048, 16, kdt, kdtk)
                              sqa, sqak = sqs.next(); sqb, sqbk = sqs.next()
                              for s_ in range(16):
                                  sq_, sqk_ = (sqa, sqak) if s_ < 8 else (sqb, sqbk)
                                  P.add("pe", (lambda e, s_=s_, sq_=sq_: e.matmul(sq_[:, (s_ % 8) * 48:(s_ % 8 + 1) * 48], lhsT=kdt[:, s_ * 128:(s_ + 1) * 128],
                                                                                  rhs=qbs[:, s_ * 48:(s_ + 1) * 48], start=True, stop=True)), reads=[kdtk, "qbs"], writes=[sqk_])
                              for i_, (sq_, sqk_) in enumerate(((sqa, sqak), (sqb, sqbk))):
                                  P.add("act", (lambda e, i_=i_, sq_=sq_: e.activation(out=pts_[:, i_ * 384:(i_ + 1) * 384], in_=sq_[:, 0:384], func=AF.Exp, scale=SCALE)),
                                        reads=[sqk_], writes=["pts_"])
                              P.add("pool", lambda e: e.tensor_tensor(out=pts_[:], in0=pts_[:], in1=msw[:], op=ALU.mult), reads=["pts_", "msw"], writes=["pts_"])
                              for s_ in range(16):
                                  for h in range(6):
                                      oc_, ock_, oco = (ocA, "oc#0", h * 128) if h < 4 else (ocB, "oc#1", (h - 4) * 128)
                                      voff = s_ * 320 + SWOFF[(h % 2, h // 3)]
                                      P.add("pe", (lambda e, s_=s_, h=h, oc_=oc_, oco=oco, voff=voff: e.matmul(oc_[:, oco + s_ * 8: oco + s_ * 8 + 8], lhsT=vc[:, voff:voff + 128],
                                                                                                            rhs=pts_[:, s_ * 48 + h * 8: s_ * 48 + h * 8 + 8], start=True, stop=True)),
                                            reads=["vc", "vc_ones", "pts_"], writes=[ock_])
                              P.add("dve", lambda e: e.tensor_copy(out=accs[:, 0:512], in_=ocA[:, 0:512]), reads=["oc#0"], writes=["accs"])
                              P.add("dve", lambda e: e.tensor_copy(out=accs[:, 512:768], in_=ocB[:, 0:256]), reads=["oc#1"], writes=["accs"])
                              for cpair in range(3):
                                  he, ho = 2 * cpair, 2 * cpair + 1
                                  ktl = [(KAx, "KAx") if (h // 3) == (h % 2) else (KAy, "KAy") for h in (he, ho)]
                                  voffs = [17 * 320 + SWOFF[(h % 2, h // 3)] for h in (he, ho)]
                                  new_keys(None, ktl, QA, cpair * 2176 + 2048, 2176, VN, "VN", voffs, mnsw, "mnsw", [(ocA, "oc#0", 0), (ocA, "oc#0", 128)])
                                  P.add("dve", (lambda e, cpair=cpair: e.tensor_tensor(out=accs[:, cpair * 256:(cpair + 1) * 256], in0=ocA[:, 0:256], in1=accs[:, cpair * 256:(cpair + 1) * 256], op=ALU.add)),
                                        reads=["oc#0", "accs"], writes=["accs"])
                              vmq = sb("vmq", 8 * 384, st=sst)
                              qbx = sb("qbx", 512, st=sst)
                              ptm = sb("ptm", 256, st=sst)
                              P.add("pool", lambda e: e.memset(sap(vmq, 64, [[192, 16], [1, 64]]), 1.0), writes=["vmq_ones"])
                              P.add("pool", lambda e: e.memset(qbx[:], 0.0), writes=["qbx"])
                              for ch in range(2):
                                  for hd in range(2):
                                      P.add("dve", (lambda e, ch=ch, hd=hd: e.tensor_copy(out=sap(qbx, ch * 16 + hd * 8, [[32, 16], [1, 8]], nparts=64, p0=hd * 64),
                                                                                        in_=sap(QX, ch * 2176 + 2048, [[8, 16], [1, 8]], nparts=64, p0=hd * 64))),
                                            reads=["QX"], writes=["qbx"])
                              for qd in range(4):
                                  k32, k32k = k32s.next(); v32, v32k = v32s.next(); kdt, kdtk = kdts.next()
                                  P.add("sp", (lambda e, k32=k32, qd=qd: e.dma_start(out=sap(k32, 0, [[256, 8], [1, 256]]), in_=dap(cmk, qd * 4 * 65536, [[256, 128], [128 * 256, 8], [1, 256]]))),
                                        writes=[k32k], dma_sem=k32k)
                                  P.add("act", (lambda e, v32=v32, qd=qd: e.dma_start(out=sap(v32, 0, [[256, 8], [1, 256]]), in_=dap(cmv, qd * 4 * 65536, [[256, 128], [128 * 256, 8], [1, 256]]))),
                                        writes=[v32k], dma_sem=v32k)
                                  for pr in range(2):
                                      P.add("pool", (lambda e, v32=v32, pr=pr: e.tensor_copy(out=sap(vmq, pr * 192, [[384, 8], [128, 2], [1, 64]]), in_=sap(v32, pr * 128, [[256, 8], [64, 2], [1, 64]]))),
                                            reads=[v32k], writes=["vmq"])
                                  transposes32(k32, k32k, 2048, 16, kdt, kdtk)
                                  sq_, sqk_ = sqs.next()
                                  for sl in range(4):
                                      for blk in range(2):
                                          for ch in range(2):
                                              idx = (sl * 2 + blk) * 2 + ch
                                              sg_ = qd * 4 + sl
                                              P.add("pe", (lambda e, idx=idx, sg_=sg_, ch=ch, sq_=sq_, kdt=kdt: e.matmul(sq_[:, idx * 16:(idx + 1) * 16], lhsT=kdt[:, idx * 128:(idx + 1) * 128],
                                                                                                                     rhs=qbx[:, (sg_ * 2 + ch) * 16:(sg_ * 2 + ch + 1) * 16], start=True, stop=True)),
                                                    reads=[kdtk, "qbx"], writes=[sqk_])
                                  P.add("act", (lambda e, sq_=sq_: e.activation(out=ptm[:], in_=sq_[:, 0:256], func=AF.Exp, scale=SCALE)), reads=[sqk_], writes=["ptm"])
                                  for sl in range(4):
                                      sg_ = qd * 4 + sl
                                      for x in range(4):
                                          for blk in range(2):
                                              voff = (sl * 2 + blk) * 384 + (x // 2) * 192 + (x % 2) * 64
                                              pcol = ((sl * 2 + blk) * 2 + x // 2) * 16 + (x % 2) * 8
                                              P.add("pe", (lambda e, sg_=sg_, x=x, blk=blk, voff=voff, pcol=pcol: e.matmul(ocB[:, x * 128 + sg_ * 8: x * 128 + sg_ * 8 + 8], lhsT=vmq[:, voff:voff + 128],
                                                                                                                       rhs=ptm[:, pcol:pcol + 8], start=(blk == 0), stop=(blk == 1))),
                                                    reads=["vmq", "vmq_ones", "ptm"], writes=["oc#1"])
                              P.add("dve", lambda e: e.tensor_copy(out=accx[:], in_=ocB[:]), reads=["oc#1"], writes=["accx"])
                          else:
                              p_ = pas - 1
                              qkey_s = "QB"
                              vds = [sb("vd%d" % i, 16 * 192, st=sst) for i in range(2)]
                              vdr = Ring("vd", vds)
                              qbd = sb("qbd", 256, st=sst)
                              ptds = Ring("ptd", [sb("ptd%d" % i, 256, st=sst) for i in range(2)])
                              for i, vt in enumerate(vds):
                                  P.add("pool", (lambda e, vt=vt: e.memset(sap(vt, 64, [[192, 16], [1, 64]]), 1.0)), writes=["vd#%d_ones" % i])
                              P.add("pool", lambda e: e.memset(qbd[:], 0.0), writes=["qbd"])
                              for hd in range(2):
                                  P.add("dve", (lambda e, hd=hd: e.tensor_copy(out=sap(qbd, hd * 8, [[16, 16], [1, 8]], nparts=64, p0=hd * 64),
                                                                               in_=sap(QB, 2048, [[8, 16], [1, 8]], nparts=64, p0=hd * 64))), reads=["QB"], writes=["qbd"])
                              def mk_seq(s_):
                                  st_ = {}

                                  def A1():
                                      k32, k32k = k32s.next(); v32, v32k = v32s.next(); kdt, kdtk = kdts.next(); vd, vdk = vdr.next()
                                      st_.update(kdt=kdt, kdtk=kdtk, vd=vd, vdk=vdk)
                                      P.add("sp", (lambda e: e.dma_start(out=sap(k32, 0, [[128, 16], [1, 128]]),
                                                                         in_=dap(cdk, s_ * 2048 * 384 + p_ * 128, [[16 * 384, 128], [384, 16], [1, 128]]))), writes=[k32k], dma_sem=k32k)
                                      P.add("act", (lambda e: e.dma_start(out=sap(v32, 0, [[128, 16], [1, 128]]),
                                                                          in_=dap(cdv, s_ * 2048 * 384 + p_ * 128, [[16 * 384, 128], [384, 16], [1, 128]]))), writes=[v32k], dma_sem=v32k)
                                      P.add("dve", (lambda e: e.tensor_copy(out=sap(vd, 0, [[192, 16], [128, 2], [1, 64]]), in_=sap(v32, 0, [[128, 16], [64, 2], [1, 64]]))),
                                            reads=[v32k], writes=[vdk])
                                      transposes32(k32, k32k, 2048, 16, kdt, kdtk)

                                  def A2():
                                      kdt, kdtk = st_["kdt"], st_["kdtk"]
                                      sq_, sqk_ = sqs.next(); ptd, ptdk = ptds.next()
                                      st_.update(ptd=ptd, ptdk=ptdk)
                                      for c in range(16):
                                          P.add("pe", (lambda e, c=c: e.matmul(sq_[:, c * 16:(c + 1) * 16], lhsT=kdt[:, c * 128:(c + 1) * 128],
                                                                               rhs=qbd[:, s_ * 16:(s_ + 1) * 16], start=True, stop=True)), reads=[kdtk, "qbd"], writes=[sqk_])
                                      P.add("act", (lambda e: e.activation(out=ptd[:], in_=sq_[:, 0:256], func=AF.Exp, scale=SCALE)), reads=[sqk_], writes=[ptdk])
                                      P.add("dve", (lambda e: e.tensor_tensor(out=ptd[:], in0=ptd[:], in1=mdil[:], op=ALU.mult)), reads=[ptdk, "mdil"], writes=[ptdk])

                                  def B():
                                      vd, vdk, ptd, ptdk = st_["vd"], st_["vdk"], st_["ptd"], st_["ptdk"]
                                      for hd in range(2):
                                          for c in range(16):
                                              P.add("pe", (lambda e, hd=hd, c=c: e.matmul(ocA[:, hd * 128 + s_ * 8: hd * 128 + s_ * 8 + 8], lhsT=vd[:, c * 192 + hd * 64: c * 192 + hd * 64 + 128],
                                                                                           rhs=ptd[:, c * 16 + hd * 8: c * 16 + hd * 8 + 8], start=(c == 0), stop=(c == 15))),
                                                    reads=[vdk, vdk + "_ones", ptdk], writes=["oc#0"])
                                  return A1, A2, B

                              seqs = [mk_seq(s_) for s_ in range(16)]
                              LAGS = (1, 0)
                              for i_ in range(16 + LAGS[0] + LAGS[1]):
                                  if i_ < 16:
                                      seqs[i_][0]()
                                  if LAGS[0] <= i_ < 16 + LAGS[0]:
                                      seqs[i_ - LAGS[0]][1]()
                                  if i_ >= LAGS[0] + LAGS[1]:
                                      seqs[i_ - LAGS[0] - LAGS[1]][2]()
                              P.add("dve", lambda e: e.tensor_copy(out=accs[:, 0:256], in_=ocA[:, 0:256]), reads=["oc#0"], writes=["accs"])
                              new_keys(None, [(KB, "KB"), (KB, "KB")], QB, 2048, 4096, VN, "VN", [17 * 192, 17 * 192 + 64], mndil, "mndil", [(ocB, "oc#1", 0), (ocB, "oc#1", 128)])
                              P.add("dve", lambda e: e.tensor_tensor(out=accs[:, 0:256], in0=ocB[:, 0:256], in1=accs[:, 0:256], op=ALU.add), reads=["oc#1", "accs"], writes=["accs"])
                          P.flush()
                      sts = Ring("st", [ps("st%d" % i, 512, st=bst) for i in range(4)])
                      obs = Ring("ob", [ps("ob%d" % i, 512, st=bst) for i in range(2)])

                      def v_lhsT(vt, off, odd):
                          return vt[:, off:off + 128]

                      def unit_pair(qsrc, qkey, qcols, ksrcs, kcols2, vsrc, vkey, voffs2, mask, mkey, acc_cols, first):
                          pt, pk = pts.next()
                          maybe_copy()
                          for hd in range(2):
                              st, sk = sts.next()
                              kt, kkey = ksrcs[hd]
                              for blk in range(2):
                                  ks, kstep = kcols2[blk]
                                  P.add("pe", (lambda e, st=st, hd=hd, blk=blk, kt=kt, ks=ks, kstep=kstep: e.matmul(
                                      st[:, blk * 128: (blk + 1) * 128],
                                      lhsT=sap(kt, ks, [[kstep, 128]], nparts=64, p0=hd * 64),
                                      rhs=sap(qsrc, qcols[0], [[qcols[1], 128]], nparts=64, p0=hd * 64), start=True, stop=True)),
                                      reads=[kkey, qkey], writes=[sk])
                              P.add("act", (lambda e, st=st, pt=pt, hd=hd: e.activation(out=pt[:, hd * 256:(hd + 1) * 256], in_=st[:, 0:256], func=AF.Exp, scale=SCALE)),
                                    reads=[sk], writes=[pk])
                          if mask is not None:
                              ucnt[0] += 1
                              P.add("dve" if ucnt[0] % 2 else "pool", (lambda e, pt=pt: e.tensor_tensor(out=pt[:], in0=pt[:], in1=mask[:], op=ALU.mult)), reads=[pk, mkey], writes=[pk])

                          def stage2():
                              ob, ok = obs.next()
                              for hd in range(2):
                                  for blk in range(2):
                                      P.add("pe", (lambda e, ob=ob, pt=pt, hd=hd, blk=blk: e.matmul(
                                          ob[:, hd * 128:(hd + 1) * 128], lhsT=v_lhsT(vsrc, voffs2[hd][blk], hd == 1),
                                          rhs=pt[:, hd * 256 + blk * 128: hd * 256 + (blk + 1) * 128], start=(blk == 0), stop=(blk == 1))),
                                          reads=[vkey, vkey + "_ones", pk], writes=[ok])
                              a0, astep = acc_cols
                              dst = sap(acc, a0, [[2176, 2], [astep, 128]])
                              src = sap(ob, 0, [[128, 2], [1, 128]])
                              if first:
                                  P.add("dve", (lambda e: e.tensor_copy(out=dst, in_=src)), reads=[ok], writes=["acc"])
                              else:
                                  P.add("dve", (lambda e: e.tensor_tensor(out=dst, in0=src, in1=dst, op=ALU.add)), reads=[ok, "acc"], writes=["acc"])
                          pend.append(stage2)
                          if len(pend) > 2:
                              pend.pop(0)()

                      pend = []
                      ucnt = [0]

                      def drain():
                          while pend:
                              pend.pop(0)()

                      def finish_pair(chunk, c0, n, sink_heads=None):
                          if sink_heads is not None:
                              he, ho = sink_heads
                              P.add("act", (lambda e: e.activation(out=sap(acc, c0, [[1, n]], nparts=64, p0=64), in_=sap(acc, c0, [[1, n]], nparts=64, p0=64),
                                                                   func=AF.Identity, bias=esink[64:128, he:he + 1])), reads=["acc", "esink"], writes=["acc"])
                              P.add("act", (lambda e: e.activation(out=sap(acc, 2176 + c0, [[1, n]], nparts=64, p0=0), in_=sap(acc, 2176 + c0, [[1, n]], nparts=64, p0=0),
                                                                   func=AF.Identity, bias=esink[0:64, ho:ho + 1])), reads=["acc", "esink"], writes=["acc"])
                          P.add("dve", (lambda e: e.reciprocal(out=sap(rl, c0, [[1, n]], nparts=64, p0=0), in_=sap(acc, c0, [[1, n]], nparts=64, p0=64))),
                                reads=["acc"], writes=["rl"])
                          P.add("dve", (lambda e: e.reciprocal(out=sap(rl, c0, [[1, n]], nparts=64, p0=64), in_=sap(acc, 2176 + c0, [[1, n]], nparts=64, p0=0))),
                                reads=["acc"], writes=["rl"])
                          P.add("dve", (lambda e: e.tensor_tensor(out=sap(catT, chunk * CT + c0, [[1, n]], nparts=64, p0=0), in0=sap(acc, c0, [[1, n]], nparts=64, p0=0),
                                                                  in1=sap(rl, c0, [[1, n]], nparts=64, p0=0), op=ALU.mult)), reads=["acc", "rl"], writes=["catT"])
                          P.add("dve", (lambda e: e.tensor_tensor(out=sap(catT, chunk * CT + c0, [[1, n]], nparts=64, p0=64), in0=sap(acc, 2176 + c0, [[1, n]], nparts=64, p0=64),
                                                                  in1=sap(rl, c0, [[1, n]], nparts=64, p0=64), op=ALU.mult)), reads=["acc", "rl"], writes=["catT"])

                      if pas == 0:
                          for cpair in range(3):
                              he, ho = 2 * cpair, 2 * cpair + 1
                              ksrcs = []
                              for h in (he, ho):
                                  kv = h // 3
                                  ksrcs.append((KAx, "KAx") if kv == (h % 2) else (KAy, "KAy"))
                              for j in range(16):
                                  voffs2 = [[(j + blk) * 320 + {(0, 0): 0, (1, 1): 64, (0, 1): 128, (1, 0): 192}[(h % 2, h // 3)] for blk in range(2)] for h in (he, ho)]
                                  m, mk_ = (mfirst, "mfirst") if j == 0 else (mstd, "mstd")
                                  unit_pair(QA, "QA", (cpair * 2176 + j * 128, 1), ksrcs, [(j * 128, 1), (128 + j * 128, 1)],
                                            VN, "VN", voffs2, m, mk_, (j * 128, 1), True)
                              drain()
                              P.add("dve", (lambda e, cpair=cpair: e.tensor_copy(out=sap(acc, 2048, [[2176, 2], [1, 128]]), in_=sap(accs, cpair * 256, [[128, 2], [1, 128]]))),
                                    reads=["accs"], writes=["acc"])
                              finish_pair(cpair, 0, 2176, sink_heads=(he, ho))
                          for cpair in range(2):
                              for j in range(16):
                                  voffs2 = [[blk * 384 + cpair * 192 + hd * 64 for blk in range(2)] for hd in range(2)]
                                  unit_pair(QX, "QX", (cpair * 2176 + j * 128, 1), [(MKT, "MKT"), (MKT, "MKT")],
                                            [(cpair * 256, 1), (cpair * 256 + 128, 1)], MV, "MV", voffs2, None, None, (j * 128, 1), True)
                              drain()
                              P.add("dve", (lambda e, cpair=cpair: e.tensor_copy(out=sap(acc, 2048, [[2176, 2], [1, 128]]), in_=sap(accx, cpair * 256, [[128, 2], [1, 128]]))),
                                    reads=["accx"], writes=["acc"])
                              finish_pair(6 + cpair, 0, 2176)
                      else:
                          p_ = pas - 1
                          kb2 = [(KB, "KB"), (KB, "KB")]
                          for j in range(16):
                              voffs2 = [[(j + blk) * 192 + hd * 64 for blk in range(2)] for hd in range(2)]
                              m, mk_ = (mfirst, "mfirst") if j == 0 else (mstd, "mstd")
                              unit_pair(QB, "QB", (j * 128, 1), kb2, [(2048 + (j - 1) * 128, 1), (2048 + j * 128, 1)],
                                        VN, "VN", voffs2, m, mk_, (j * 128, 1), True)
                          for n in range(4):
                              for c in range(4):
                                  voffs2 = [[((n + blk) * 4 + c) * 192 + hd * 64 for blk in range(2)] for hd in range(2)]
                                  m, mk_ = (mfirst, "mfirst") if n == 0 else (mstd, "mstd")
                                  unit_pair(QB, "QB", (n * 512 + c, 4), kb2, [(2048 + (n - 1) * 512 + c, 4), (2048 + n * 512 + c, 4)],
                                            V4, "V4", voffs2, m, mk_, (n * 512 + c, 4), False)
                          for c in range(16):
                              voffs2 = [[(blk * 16 + c) * 192 + hd * 64 for blk in range(2)] for hd in range(2)]
                              unit_pair(QB, "QB", (c, 16), kb2, [(c, 16), (2048 + c, 16)], V16, "V16", voffs2, mfirst, "mfirst", (c, 16), False)
                          drain()
                          P.add("dve", lambda e: e.tensor_copy(out=sap(acc, 2048, [[2176, 2], [1, 128]]), in_=sap(accs, 0, [[128, 2], [1, 128]])), reads=["accs"], writes=["acc"])
                          finish_pair(3 + p_, 0, 2176)
                      P.flush()


            except _StopBuild:
                P.flush()
                break

        if STOP[0] > 8:
            with contextlib.ExitStack() as cst:
                NTGM = 640
                while pending_cp:
                    maybe_copy(force=True)
                x1 = sb("x1", 5 * 1024, F32, st=cst)
                hT = sb("hT", 8 * NTGM, st=cst)
                actT = sb("actT", NFC * NTGM, st=cst)
                wd = sb("wd", NFC * 1024, st=cst)
                wo = sb("wo", 8 * 1024, st=cst)
                wgs = Ring("wg", [sb("wg%d" % i, 2048, st=cst) for i in range(2)])
                wus = Ring("wu", [sb("wu%d" % i, 2048, st=cst) for i in range(2)])
                gpost = sb("gpost", D, F32, st=cst); gffn = sb("gffn", D, F32, st=cst); gpffn = sb("gpffn", D, F32, st=cst)
                xts = Ring("xt", [sb("cxt%d" % i, D, F32, st=cst) for i in range(1)])
                ysbs = Ring("ysb", [sb("ysb%d" % i, D, F32, st=cst) for i in range(1)])
                hbs = Ring("hb", [sb("chb%d" % i, D, st=cst) for i in range(2)])
                junk = sb("cjunk", D, st=cst)
                sgs = Ring("sg", [sb("sg%d" % i, 512, F32, st=cst) for i in range(2)])
                ssa = Ring("ssa", [sb("ssa%d" % i, 2, F32, st=cst) for i in range(2)])
                sss = Ring("ss", [sb("css%d" % i, 1, F32, st=cst) for i in range(2)])
                rss = Ring("rs", [sb("crs%d" % i, 1, F32, st=cst) for i in range(2)])
                yps = Ring("yp", [ps("yp%d" % i, 512, st=cst) for i in range(4)])
                gps = Ring("gp", [ps("gp%d" % i, 512, st=cst) for i in range(3)])
                ups = gps
                pTs = Ring("pT", [ps("cpT%d" % i, 1024, BF16, st=cst) for i in range(1)])
                P.add("sp", lambda e: e.dma_start(out=sap(wo, 0, [[1024, 8], [1, 1024]]), in_=dap(wob, 0, [[1024, 128], [128 * 1024, 8], [1, 1024]])),
                      reads=["wob"], writes=["wo"], dma_sem="wo")
                P.add("sp", lambda e: e.dma_start(out=sap(wd, 0, [[1024, NFC], [1, 1024]]), in_=dap(wdb, 0, [[1024, 128], [128 * 1024, NFC], [1, 1024]])),
                      reads=["wdb"], writes=["wd"], dma_sem="wd")
                for gt, gsrc, gk in ((gpost, g_post, "gpost"), (gffn, g_ffn, "gffn"), (gpffn, g_pffn, "gpffn")):
                    P.add("sp", (lambda e, gt=gt, gsrc=gsrc: e.dma_start(out=gt[:], in_=dap(gsrc, 0, [[0, 128], [1, D]]))), writes=[gk], dma_sem=gk)

                def sumsq(src_aps, src_keys, ss, sk):
                    sa, sak = ssa.next()
                    for i, (ap_, k_) in enumerate(zip(src_aps, src_keys)):
                        P.add("act", (lambda e, ap_=ap_, i=i, sa=sa: e.activation(out=junk[:, 0:512], in_=ap_, func=AF.Square, accum_out=sa[:, i:i + 1])),
                              reads=[k_], writes=["cjunk", sak])
                    P.add("dve", (lambda e, sa=sa, ss=ss: e.tensor_tensor(out=ss[:], in0=sa[:, 0:1], in1=sa[:, 1:2], op=ALU.add)), reads=[sak], writes=[sk])

                groups = [[0, 1, 2, 3], [4, 5, 6, 7], [8, 9, 10, 11], [12, 13, 14, 15, 16]]
                for grp in groups:
                    ntile = len(grp)
                    NTG = ntile * 128
                    cpend = []
                    for slot, T in enumerate(grp):
                        xsrc, xrow = (xo, T * 128) if T < 16 else (xs, 0)
                        ccol = T * 128
                        ypa, yka = yps.next(); ypb, ykb = yps.next()
                        for eh, yp in ((0, ypa), (1, ypb)):
                            for k in range(8):
                                P.add("pe", (lambda e, yp=yp, eh=eh, k=k, ccol=ccol: e.matmul(yp[:, 0:512], lhsT=catT[:, k * CT + ccol: k * CT + ccol + 128],
                                                                                               rhs=wo[:, k * 1024 + eh * 512: k * 1024 + (eh + 1) * 512], start=(k == 0), stop=(k == 7))),
                                      reads=["catT", "wo"], writes=[yka if eh == 0 else ykb])
                        ss, sk = sss.next(); rs, rk = rss.next()
                        sumsq([ypa[:, 0:512], ypb[:, 0:512]], [yka, ykb], ss, sk)
                        rstd_ops(ss, sk, rs, rk)
                        xt, xk = xts.next()
                        P.add("sp", (lambda e, xt=xt, xsrc=xsrc, xrow=xrow: e.dma_start(out=xt[:], in_=xsrc[xrow:xrow + 128, :])), writes=[xk], dma_sem=xk)
                        x1k = "x1_%d" % slot
                        for eh, yp, yk in ((0, ypa, yka), (1, ypb, ykb)):
                            P.add("dve", (lambda e, yp=yp, eh=eh, rs=rs, slot=slot: e.scalar_tensor_tensor(
                                out=x1[:, slot * 1024 + eh * 512: slot * 1024 + (eh + 1) * 512], in0=yp[:, 0:512], scalar=rs[:, 0:1],
                                in1=gpost[:, eh * 512:(eh + 1) * 512], op0=ALU.mult, op1=ALU.mult)), reads=[yk, rk, "gpost"], writes=[x1k])
                        P.add("pool", (lambda e, xt=xt, slot=slot: e.tensor_tensor(out=x1[:, slot * 1024:(slot + 1) * 1024], in0=x1[:, slot * 1024:(slot + 1) * 1024],
                                                                                   in1=xt[:], op=ALU.add)), reads=[x1k, xk], writes=[x1k])
                        ss2, sk2 = sss.next(); rs2, rk2 = rss.next(); hb, hk = hbs.next(); pT, pk = pTs.next()
                        sumsq([x1[:, slot * 1024: slot * 1024 + 512], x1[:, slot * 1024 + 512:(slot + 1) * 1024]], [x1k, x1k], ss2, sk2)
                        rstd_ops(ss2, sk2, rs2, rk2)
                        P.add("dve", (lambda e, hb=hb, rs2=rs2, slot=slot: e.scalar_tensor_tensor(out=hb[:], in0=x1[:, slot * 1024:(slot + 1) * 1024], scalar=rs2[:, 0:1],
                                                                                                  in1=gffn[:], op0=ALU.mult, op1=ALU.mult)), reads=[x1k, rk2, "gffn"], writes=[hk])
                        def stage2(pT=pT, pk=pk, hb=hb, hk=hk, slot=slot, NTG=NTG):
                            for dc in range(8):
                                P.add("pe", (lambda e, dc=dc: e.transpose(out=pT[:, dc * 128:(dc + 1) * 128], in_=hb[:, dc * 128:(dc + 1) * 128], identity=ident[:])),
                                      reads=[hk, "ident"], writes=[pk])
                            P.add("act", (lambda e: e.activation(out=sap(hT, slot * 128, [[NTG, 8], [1, 128]]), in_=sap(pT, 0, [[128, 8], [1, 128]]), func=AF.Copy)),
                                  reads=[pk], writes=["hT"])
                        cpend.append(stage2)
                        if len(cpend) > 1:
                            cpend.pop(0)()
                    while cpend:
                        cpend.pop(0)()
                    tbs = [(0, 512)] if ntile == 4 else [(0, 512), (512, 128)]
                    for fc in range(NFC):
                        if fc % 2 == 0:
                            nf = min(2, NFC - fc) * 128
                            wg, wgk = wgs.next(); wu, wuk = wus.next()
                            P.add("sp", (lambda e, wg=wg, fc=fc, nf=nf: e.dma_start(out=sap(wg, 0, [[256, 8], [1, nf]]), in_=dap(wgb, fc * 128, [[DFF, 128], [128 * DFF, 8], [1, nf]]))),
                                  reads=["wgb"], writes=[wgk], dma_sem=wgk)
                            P.add("sp", (lambda e, wu=wu, fc=fc, nf=nf: e.dma_start(out=sap(wu, 0, [[256, 8], [1, nf]]), in_=dap(wub, fc * 128, [[DFF, 128], [128 * DFF, 8], [1, nf]]))),
                                  reads=["wub"], writes=[wuk], dma_sem=wuk)
                        fo = (fc % 2) * 128
                        for (tb0, n) in tbs:
                            gp, gk = gps.next(); up, uk = ups.next(); sg, sgk = sgs.next()
                            for wt, wk_, pp, ppk in ((wg, wgk, gp, gk), (wu, wuk, up, uk)):
                                for dc in range(8):
                                    P.add("pe", (lambda e, wt=wt, pp=pp, dc=dc, tb0=tb0, n=n, NTG=NTG, fo=fo: e.matmul(pp[:, 0:n], lhsT=wt[:, dc * 256 + fo: dc * 256 + fo + 128],
                                                                                                               rhs=hT[:, dc * NTG + tb0: dc * NTG + tb0 + n], start=(dc == 0), stop=(dc == 7))),
                                          reads=[wk_, "hT"], writes=[ppk])
                            P.add("act", (lambda e, gp=gp, sg=sg, n=n: e.activation(out=sg[:, 0:n], in_=gp[:, 0:n], func=AF.Silu)), reads=[gk], writes=[sgk])
                            P.add("dve", (lambda e, up=up, sg=sg, fc=fc, tb0=tb0, n=n, NTG=NTG: e.tensor_tensor(out=actT[:, fc * NTG + tb0: fc * NTG + tb0 + n], in0=up[:, 0:n],
                                                                                                              in1=sg[:, 0:n], op=ALU.mult)), reads=[uk, sgk], writes=["actT"])
                    for slot, T in enumerate(grp):
                        ypa, yka = yps.next(); ypb, ykb = yps.next()
                        for eh, yp in ((0, ypa), (1, ypb)):
                            for fc in range(NFC):
                                P.add("pe", (lambda e, yp=yp, eh=eh, fc=fc, slot=slot, NTG=NTG: e.matmul(yp[:, 0:512], lhsT=actT[:, fc * NTG + slot * 128: fc * NTG + (slot + 1) * 128],
                                                                                                        rhs=wd[:, fc * 1024 + eh * 512: fc * 1024 + (eh + 1) * 512], start=(fc == 0), stop=(fc == NFC - 1))),
                                      reads=["actT", "wd"], writes=[yka if eh == 0 else ykb])
                        ss, sk = sss.next(); rs, rk = rss.next(); ysb, ysk = ysbs.next()
                        sumsq([ypa[:, 0:512], ypb[:, 0:512]], [yka, ykb], ss, sk)
                        rstd_ops(ss, sk, rs, rk)
                        x1k = "x1_%d" % slot
                        for eh, yp, yk in ((0, ypa, yka), (1, ypb, ykb)):
                            P.add("dve", (lambda e, yp=yp, eh=eh, rs=rs, ysb=ysb: e.scalar_tensor_tensor(
                                out=ysb[:, eh * 512:(eh + 1) * 512], in0=yp[:, 0:512], scalar=rs[:, 0:1],
                                in1=gpffn[:, eh * 512:(eh + 1) * 512], op0=ALU.mult, op1=ALU.mult)), reads=[yk, rk, "gpffn"], writes=[ysk])
                        P.add("pool", (lambda e, ysb=ysb, slot=slot: e.tensor_tensor(out=ysb[:], in0=ysb[:], in1=x1[:, slot * 1024:(slot + 1) * 1024], op=ALU.add)),
                              reads=[ysk, x1k], writes=[ysk])
                        if T < 16:
                            P.add("sp", (lambda e, ysb=ysb, T=T: e.dma_start(out=yo[T * 128:(T + 1) * 128, :], in_=ysb[:])), reads=[ysk], dma_sem=ysk)
                        else:
                            P.add("sp", (lambda e, ysb=ysb: e.dma_start(out=ys[:, :], in_=ysb[:])), reads=[ysk], dma_sem=ysk)
                P.flush()
        P.flush(final=True)
    return nc


def _consts(hf):
    f32 = np.float32
    c = {}
    c["c_ident"] = np.eye(128, dtype=f32)
    R = np.zeros((128, 128), f32)
    for pp in range(128):
        if (pp % 64) < 32:
            R[pp + 32, pp] = -1.0
        else:
            R[pp - 32, pp] = 1.0
    c["c_rot"] = R
    inv = np.power(f32(10000.0), -(np.arange(32, dtype=f32) / f32(32))).astype(f32)
    pos = np.concatenate([(hf - 1) * 2048 + np.arange(2048), hf * 2048 + np.arange(2048), PAST + (np.arange(128) % 8)]).astype(f32)
    ang = (pos[:, None] * inv[None, :]).astype(f32)
    fidx = (np.arange(128) % 64) % 32
    c["c_cos"] = np.ascontiguousarray(np.cos(ang).astype(f32)[:, fidx].T)
    c["c_sin"] = np.ascontiguousarray(np.sin(ang).astype(f32)[:, fidx].T)
    kk = np.arange(128)[:, None]
    qq = np.arange(128)[None, :]
    prev = (kk >= qq).astype(f32)
    diag = (kk <= qq).astype(f32)
    c["c_mstd"] = np.concatenate([prev, diag, prev, diag], axis=1)
    c["c_mfirst"] = np.concatenate([prev * f32(hf), diag, prev * f32(hf), diag], axis=1)
    i8 = np.arange(8)
    msw = (np.arange(128)[:, None] >= i8[None, :]).astype(f32)
    c["c_msw"] = np.ascontiguousarray(np.tile(msw[:, None, :], (1, 96, 1)).reshape(128, 768))
    s_ = np.arange(128) // 8
    j_ = np.arange(128) % 8
    same = (s_[:, None] == s_[None, :])
    dji = j_[None, :] - j_[:, None]
    c["c_mnsw"] = (same & (dji >= 0)).astype(f32)
    mult_new = (dji >= 0).astype(f32) + ((dji >= 0) & (dji % 4 == 0)).astype(f32) + (dji == 0).astype(f32)
    c["c_mndil"] = (same.astype(f32) * mult_new).astype(f32)
    g = np.arange(128)[:, None, None]
    cc = np.arange(16)[None, :, None]
    ii = np.arange(8)[None, None, :]
    p3 = (cc == ii)
    p2 = ((cc % 4) == (ii % 4)) & ((g > 96) | ((g == 96) & (cc >= ii)))
    p1 = (g > 120) | ((g == 120) & (cc >= ii))
    mult = p3.astype(f32) + p2.astype(f32) + p1.astype(f32)
    c["c_mdil"] = np.ascontiguousarray(np.tile(mult[:, :, None, :], (1, 1, 2, 1)).reshape(128, 256))
    c["c_nhalf"] = np.full((128, 1), -0.5, f32)
    return c


_NC_CACHE = {}


def kernel(x_prompt, x_sample, cache_swa_k, cache_swa_v, cache_dil_k, cache_dil_v,
           cache_mem_k, cache_mem_v, mem_prompt, g_pre_mix, w_in, sinks, w_mem_kv, w_o,
           g_post_mix, g_pre_ffn, w_gate, w_up, w_down, g_post_ffn):
    f32 = np.float32
    A = lambda a: np.ascontiguousarray(np.asarray(a, dtype=f32))
    x_prompt = A(x_prompt); x_sample = A(x_sample)
    shared = {
        "w_in": A(w_in)[0], "w_mem": A(w_mem_kv)[0], "w_o": A(w_o)[0],
        "w_gate": A(w_gate)[0], "w_up": A(w_up)[0], "w_down": A(w_down)[0],
        "g_pre": A(g_pre_mix).reshape(1, D), "g_post": A(g_post_mix).reshape(1, D),
        "g_ffn": A(g_pre_ffn).reshape(1, D), "g_pffn": A(g_post_ffn).reshape(1, D),
        "sinks": A(sinks).reshape(1, 6),
    }
    csk = A(cache_swa_k)[0].reshape(128, 128, 128); csv = A(cache_swa_v)[0].reshape(128, 128, 128)
    cdk = A(cache_dil_k)[0].reshape(128, 2048, 384); cdv = A(cache_dil_v)[0].reshape(128, 2048, 384)
    cmk = A(cache_mem_k)[0].reshape(128, 256, 256); cmv = A(cache_mem_v)[0].reshape(128, 256, 256)
    mem_prompt = A(mem_prompt)
    cst = [_consts(0), _consts(1)]
    in_maps = []
    for c in range(8):
        b, hf = c // 2, c % 2
        m = dict(shared)
        m["xo"] = x_prompt[b, hf * 2048:(hf + 1) * 2048]
        m["xh"] = x_prompt[b, 0:2048] if hf == 1 else np.zeros((2048, D), f32)
        m["xs"] = x_sample[c * 16:(c + 1) * 16].reshape(128, D)
        m["memp"] = mem_prompt[b]
        sl = slice(c * 16, (c + 1) * 16)
        m["csk"] = csk[sl]; m["csv"] = csv[sl]; m["cdk"] = cdk[sl]; m["cdv"] = cdv[sl]
        m["cmk"] = cmk[sl]; m["cmv"] = cmv[sl]
        m.update(cst[hf])
        in_maps.append(m)
    if "nc" not in _NC_CACHE:
        _NC_CACHE["nc"] = build()
    res = run_bass_kernel_spmd(_NC_CACHE["nc"], in_maps, core_ids=list(range(8)))
    R = res.results
    y_prompt = np.zeros((4, 4096, D), f32); y_sample = np.zeros((128, 8, D), f32)
    p_swa_k = np.zeros((1, 4, 128, 2, 64), f32); p_swa_v = np.zeros_like(p_swa_k)
    p_dil_k = np.zeros((1, 4, 2048, 6, 64), f32); p_dil_v = np.zeros_like(p_dil_k)
    p_mem_k = np.zeros((1, 4, 256, 4, 64), f32); p_mem_v = np.zeros_like(p_mem_k)
    s_swa_k = np.zeros((1, 128, 128, 2, 64), f32); s_swa_v = np.zeros_like(s_swa_k)
    s_dil_k = np.zeros((1, 128, 2048, 6, 64), f32); s_dil_v = np.zeros_like(s_dil_k)
    for c in range(8):
        b, hf = c // 2, c % 2
        r = R[c]
        y_prompt[b, hf * 2048:(hf + 1) * 2048] = r["yo"]
        y_sample[c * 16:(c + 1) * 16] = r["ys"].reshape(16, 8, D)
        if hf == 1:
            p_swa_k[0, b] = r["o_swa_k"].reshape(128, 2, 64); p_swa_v[0, b] = r["o_swa_v"].reshape(128, 2, 64)
            p_dil_k[0, b] = r["o_dil_k"].reshape(2048, 6, 64); p_dil_v[0, b] = r["o_dil_v"].reshape(2048, 6, 64)
        else:
            p_mem_k[0, b] = r["o_mem_k"].reshape(256, 4, 64); p_mem_v[0, b] = r["o_mem_v"].reshape(256, 4, 64)
        sl = slice(c * 16, (c + 1) * 16)
        s_swa_k[0, sl] = r["s_swa_k"].reshape(16, 128, 2, 64); s_swa_v[0, sl] = r["s_swa_v"].reshape(16, 128, 2, 64)
        s_dil_k[0, sl] = r["s_dil_k"].reshape(16, 2048, 6, 64); s_dil_v[0, sl] = r["s_dil_v"].reshape(16, 2048, 6, 64)
    return (y_prompt, y_sample, p_swa_k, p_swa_v, p_dil_k, p_dil_v, p_mem_k, p_mem_v,
            s_swa_k, s_swa_v, s_dil_k, s_dil_v)
```

```python
import contextlib
import numpy as np
import concourse.bass as bass
import concourse.mybir as mybir
from concourse.ap import AP
from concourse.bass_utils import run_bass_kernel_spmd

F32 = mybir.dt.float32
BF16 = mybir.dt.bfloat16
AF = mybir.ActivationFunctionType
ALU = mybir.AluOpType

ENGS = ("pe", "act", "dve", "pool", "sp")
PSUM_RINGS = {"pT", "zp", "rp", "vp", "st", "ob", "sq", "oc", "on", "yp", "gp", "up", "tp"}
D = 1024
DFF = 2816
NFC = 22
EPS = 1e-6
SCALE = 0.125
NOWN = 2048
NS = 128
PAST = 16384


class Op:
    __slots__ = ("eng", "fn", "deps", "is_dma", "sem", "sem_target", "signal", "sig_idx", "emitted")

    def __init__(self, eng, fn, is_dma):
        self.eng = eng
        self.fn = fn
        self.deps = ()
        self.is_dma = is_dma
        self.sem = None
        self.sem_target = 0
        self.signal = False
        self.sig_idx = 0
        self.emitted = False


class Prog:
    def __init__(self, nc):
        self.nc = nc
        self.seg = {e: [] for e in ENGS}
        self.state = {}
        self.dma_sems = {}
        self.last_dma = {}
        self.last_compute = {}
        self.fence_deps = set()
        self.esems = {e: nc.alloc_semaphore("s_" + e) for e in ENGS}
        self.sigcount = {e: 0 for e in ENGS}
        self.waited = {e: {} for e in ENGS}

    def add(self, eng, fn, reads=(), writes=(), dma_sem=None, nofence=False):
        if _STOPPED[0]:
            return None
        op = Op(eng, fn, dma_sem is not None)
        if dma_sem is not None:
            ent = self.dma_sems.get(dma_sem)
            if ent is None:
                ent = [self.nc.alloc_semaphore("d_" + str(len(self.dma_sems))), 0]
                self.dma_sems[dma_sem] = ent
            ent[1] += 16
            op.sem = dma_sem
            op.sem_target = ent[1]
            if not nofence:
                self.last_dma[dma_sem] = op
        else:
            self.last_compute[eng] = op
        xr = [k for k in reads if k.split("#")[0] in PSUM_RINGS]
        if xr:
            reads = [k for k in reads if k not in xr]
            writes = list(writes) + xr
        deps = set(self.fence_deps)
        for k in reads:
            s = self.state.get(k)
            if s is None:
                s = [None, []]
                self.state[k] = s
            if s[0] is not None:
                deps.add(s[0])
            s[1].append(op)
        for k in writes:
            s = self.state.get(k)
            if s is None:
                s = [None, []]
                self.state[k] = s
            if s[0] is not None:
                deps.add(s[0])
            deps.update(s[1])
            s[0] = op
            s[1] = []
        deps.discard(op)
        op.deps = deps
        self.seg[eng].append(op)
        return op

    @staticmethod
    def _skip(d, op):
        return (not d.is_dma) and d.eng == op.eng and (not op.is_dma) and d.eng == "pe"

    def flush(self, final=False):
        nc = self.nc
        for e in ENGS:
            for op in self.seg[e]:
                for d in op.deps:
                    if d.is_dma or d.emitted or self._skip(d, op):
                        continue
                    d.signal = True
        for e in ENGS:
            lc = self.last_compute.get(e)
            if lc is not None and not lc.emitted:
                lc.signal = True
        for e in ENGS:
            for op in self.seg[e]:
                if op.signal and not op.is_dma:
                    self.sigcount[e] += 1
                    op.sig_idx = self.sigcount[e]
        with nc.Block() as block:
            engmap = {"pe": block.tensor, "act": block.scalar, "dve": block.vector,
                      "pool": block.gpsimd, "sp": block.sync}

            def make(e):
                def body(eng):
                    waited = self.waited[e]
                    for op in self.seg[e]:
                        need = {}
                        for d in op.deps:
                            if d.is_dma:
                                key = ("d", d.sem)
                                val = d.sem_target
                                h = self.dma_sems[d.sem][0]
                            else:
                                if self._skip(d, op):
                                    continue
                                if d.emitted and not d.signal:
                                    continue
                                key = ("e", d.eng)
                                val = d.sig_idx
                                h = self.esems[d.eng]
                            if waited.get(key, 0) >= val:
                                continue
                            if key not in need or need[key][1] < val:
                                need[key] = (h, val)
                        for key, (h, val) in need.items():
                            eng.wait_ge(h, val)
                            waited[key] = val
                        ins = op.fn(eng)
                        if op.is_dma:
                            ins.then_inc(self.dma_sems[op.sem][0], 16)
                        elif op.signal:
                            ins.then_inc(self.esems[e], 1)
                    if final and e == "sp":
                        for k, ent in self.dma_sems.items():
                            eng.wait_ge(ent[0], ent[1])
                return body

            for e in ENGS:
                engmap[e](make(e))
        for e in ENGS:
            for op in self.seg[e]:
                op.emitted = True
                op.fn = None
            self.seg[e] = []
        f = set(self.last_compute.values())
        f.update(self.last_dma.values())
        self.fence_deps = f


class Ring:
    def __init__(self, name, tiles):
        self.name = name
        self.tiles = tiles
        self.i = 0

    def next(self):
        k = self.i % len(self.tiles)
        self.i += 1
        return self.tiles[k], "%s#%d" % (self.name, k)


STOP = [99]


class _StopBuild(Exception):
    pass


_STOPPED = [False]


def chk(x):
    if STOP[0] <= x:
        _STOPPED[0] = True


def build():
    _STOPPED[0] = False
    nc = bass.Bass("TRN2", target_bir_lowering=False)
    P = Prog(nc)

    def din(name, shape):
        return nc.dram_tensor(name, list(shape), F32, kind="ExternalInput").ap()

    def dout(name, shape):
        return nc.dram_tensor(name, list(shape), F32, kind="ExternalOutput").ap()

    xo = din("xo", [NOWN, D]); xh = din("xh", [NOWN, D]); xs = din("xs", [NS, D])
    memp = din("memp", [256, D])
    w_in = din("w_in", [D, 2048]); w_mem = din("w_mem", [D, 512]); w_o = din("w_o", [D, D])
    w_gate = din("w_gate", [D, DFF]); w_up = din("w_up", [D, DFF]); w_down = din("w_down", [DFF, D])
    g_pre = din("g_pre", [1, D]); g_post = din("g_post", [1, D]); g_ffn = din("g_ffn", [1, D]); g_pffn = din("g_pffn", [1, D])
    sinks = din("sinks", [1, 6])
    csk = din("csk", [16, 128, 128]); csv = din("csv", [16, 128, 128])
    cdk = din("cdk", [16, 2048, 384]); cdv = din("cdv", [16, 2048, 384])
    cmk = din("cmk", [16, 256, 256]); cmv = din("cmv", [16, 256, 256])
    c_ident = din("c_ident", [128, 128]); c_rot = din("c_rot", [128, 128])
    c_cos = din("c_cos", [128, 4224]); c_sin = din("c_sin", [128, 4224])
    c_mstd = din("c_mstd", [128, 512]); c_mfirst = din("c_mfirst", [128, 512])
    c_msw = din("c_msw", [128, 768]); c_mnsw = din("c_mnsw", [128, 128])
    c_mdil = din("c_mdil", [128, 256]); c_mndil = din("c_mndil", [128, 128])
    c_nhalf = din("c_nhalf", [128, 1])

    yo = dout("yo", [NOWN, D]); ys = dout("ys", [NS, D])
    o_swa_k = dout("o_swa_k", [128, 128]); o_swa_v = dout("o_swa_v", [128, 128])
    o_dil_k = dout("o_dil_k", [NOWN, 384]); o_dil_v = dout("o_dil_v", [NOWN, 384])
    o_mem_k = dout("o_mem_k", [256, 256]); o_mem_v = dout("o_mem_v", [256, 256])
    s_swa_k = dout("s_swa_k", [16, 128, 128]); s_swa_v = dout("s_swa_v", [16, 128, 128])
    s_dil_k = dout("s_dil_k", [16, 2048, 384]); s_dil_v = dout("s_dil_v", [16, 2048, 384])

    def dap(t, off, dims):
        return AP(t.tensor, off, [list(d) for d in dims])

    es = contextlib.ExitStack()
    with es:
        uid = [0]

        def sb(name, cols, dt=BF16, st=None):
            uid[0] += 1
            return (st or es).enter_context(nc.sbuf_tensor("%s_%d" % (name, uid[0]), [128, cols], dt))

        def ps(name, cols, dt=F32, st=None):
            uid[0] += 1
            return (st or es).enter_context(nc.psum_tensor("%s_%d" % (name, uid[0]), [128, cols], dt))

        def sap(t, off, dims, nparts=128, p0=0):
            W = t.shape[1]
            return AP(t, p0 * W + off, [[W, nparts]] + [list(d) for d in dims])

        ident = sb("ident", 128); rot = sb("rot", 128)
        mstd = sb("mstd", 512); mfirst = sb("mfirst", 512)
        msw = sb("msw", 768); mnsw = sb("mnsw", 128); mdil = sb("mdil", 256); mndil = sb("mndil", 128)
        nhalf = sb("nhalf", 1, F32)
        esink = sb("esink", 6, F32)
        catT = sb("catT", 8 * 2176)
        CT = 2176

        for dst, src, key in ((ident, c_ident, "ident"), (rot, c_rot, "rot"), (mstd, c_mstd, "mstd"),
                              (mfirst, c_mfirst, "mfirst"), (msw, c_msw, "msw"), (mnsw, c_mnsw, "mnsw"),
                              (mdil, c_mdil, "mdil"), (mndil, c_mndil, "mndil")):
            P.add("pool", (lambda e, d=dst, s=src: e.dma_start(out=d[:], in_=s[:, :])), writes=[key], dma_sem="c_" + key)
        P.add("sp", lambda e: e.dma_start(out=nhalf[:], in_=c_nhalf[:, :]), writes=["nhalf"], dma_sem="c_nhalf")
        P.add("sp", lambda e: e.dma_start(out=esink[:], in_=dap(sinks, 0, [[0, 128], [1, 6]])), writes=["esink"], dma_sem="c_sink")
        P.add("act", lambda e: e.activation(out=esink[:], in_=esink[:], func=AF.Exp), reads=["esink"], writes=["esink"])

        epsb = sb("epsb", 1, F32)
        P.add("pool", lambda e: e.memset(epsb[:], EPS), writes=["epsb"])

        def rstd_ops(ss, sk, rs, rk):
            P.add("act", (lambda e: e.activation(out=rs[:], in_=ss[:], func=AF.Sqrt, scale=1.0 / D, bias=epsb[:, 0:1])), reads=[sk, "epsb"], writes=[rk])
            P.add("dve", (lambda e: e.reciprocal(out=rs[:], in_=rs[:])), reads=[rk], writes=[rk])

        identf = sb("identf", 128, F32)
        P.add("sp", lambda e: e.dma_start(out=identf[:], in_=c_ident[:, :]), writes=["identf"], dma_sem="c_identf")
        pending_cp = []
        for s_ in range(16):
            pending_cp.append((s_dil_k, cdk, s_, "cp_dk"))
            pending_cp.append((s_dil_v, cdv, s_, "cp_dv"))
        cp_ctr = [0]

        def maybe_copy(force=False):
            cp_ctr[0] += 1
            if pending_cp and (force or cp_ctr[0] % 7 == 0):
                dst, src, s_, key = pending_cp.pop(0)
                P.add("act", (lambda e: e.dma_start(out=dst[s_, 0:2040, :], in_=src[s_, 8:2048, :])), dma_sem=key, nofence=True)

        wob = nc.dram_tensor("wob", [D, D], BF16, kind="Internal").ap()
        wgb = nc.dram_tensor("wgb", [D, DFF], BF16, kind="Internal").ap()
        wub = nc.dram_tensor("wub", [D, DFF], BF16, kind="Internal").ap()
        wdb = nc.dram_tensor("wdb", [DFF, D], BF16, kind="Internal").ap()

        uts = nc.dram_tensor("uts", [128, 8 * 4224], BF16, kind="Internal").ap()

        def cast_weights():
            P.add("pool", lambda e: e.dma_start(out=wob[:, :], in_=w_o[:, :]), writes=["wob"], dma_sem="wob")
            P.add("pool", lambda e: e.dma_start(out=wdb[:, :], in_=w_down[:, :]), writes=["wdb"], dma_sem="wdb")
            P.add("pool", lambda e: e.dma_start(out=dap(wgb, 0, [[DFF, D], [1408, 2], [1, 1408]]), in_=dap(w_gate, 0, [[DFF, D], [1408, 2], [1, 1408]])), writes=["wgb"], dma_sem="wgb")
            P.add("pool", lambda e: e.dma_start(out=dap(wub, 0, [[DFF, D], [1408, 2], [1, 1408]]), in_=dap(w_up, 0, [[DFF, D], [1408, 2], [1, 1408]])), writes=["wub"], dma_sem="wub")
        P.add("act", lambda e: e.dma_start(out=s_swa_k[:, 0:120, :], in_=csk[:, 8:128, :]), dma_sem="cp_sk", nofence=True)
        P.add("act", lambda e: e.dma_start(out=s_swa_v[:, 0:120, :], in_=csv[:, 8:128, :]), dma_sem="cp_sv", nofence=True)

        P.flush()
        for pas in range(4):
            if STOP[0] <= 2 * pas:
                break
            try:
              with contextlib.ExitStack() as pst:
                  if pas == 0:
                      QA = sb("QA", 3 * 2176, st=pst)
                      KAx = sb("KAx", 2304, st=pst); KAy = sb("KAy", 2304, st=pst)
                      QX = sb("QX", 2 * 2176, st=pst)
                      VN = sb("VA", 18 * 320, st=pst)
                      MKT = sb("MKT", 512, st=pst)
                      MV = sb("MV", 2 * 384, st=pst)
                      vbufs = [(VN, "VN")]
                      vmem = [(MV, "MV")]
                  else:
                      QB = sb("QB", 2176, st=pst)
                      KB = sb("KB", 4224, st=pst)
                      VN = sb("V1", 18 * 192, st=pst)
                      V4 = sb("V4", 20 * 192, st=pst)
                      V16 = sb("V16", 32 * 192, st=pst)
                      vbufs = [(VN, "VN"), (V4, "V4"), (V16, "V16")]
                      vmem = []
                  if pas == 0:
                      P.add("pool", lambda e: e.memset(sap(VN, 64, [[320, 18], [128, 2], [1, 64]]), 1.0), writes=["VN_ones"])
                      P.add("pool", lambda e: e.memset(sap(MV, 64, [[192, 4], [1, 64]]), 1.0), writes=["MV_ones"])
                  else:
                      for vt, vk in vbufs:
                          P.add("pool", (lambda e, t=vt: e.memset(sap(t, 64, [[192, t.shape[1] // 192], [1, 64]]), 1.0)), writes=[vk + "_ones"])

                  with contextlib.ExitStack() as ast:
                      wcols = 1024 if pas == 0 else 384
                      wb = sb("wb", 8 * wcols, st=ast)
                      uT = sb("uT", 8 * 2048, st=ast)
                      gpre = sb("gpre", D, F32, st=ast)
                      xts = Ring("xt", [sb("xt%d" % i, D, F32, st=ast) for i in range(3)])
                      hbs = Ring("hb", [sb("hb%d" % i, D, st=ast) for i in range(3)])
                      junk = sb("junk", D, st=ast)
                      sss = Ring("ss", [sb("ss%d" % i, 1, F32, st=ast) for i in range(3)])
                      rss = Ring("rs", [sb("rs%d" % i, 1, F32, st=ast) for i in range(3)])
                      coss = Ring("cos", [sb("cos%d" % i, 512, F32, st=ast) for i in range(2)])
                      sins = Ring("sin", [sb("sin%d" % i, 512, F32, st=ast) for i in range(2)])
                      zbs = Ring("zb", [sb("zb%d" % i, 512, st=ast) for i in range(2)])
                      t1s = Ring("t1", [sb("t1_%d" % i, 512, F32, st=ast) for i in range(2)])
                      t2s = Ring("t2", [sb("t2_%d" % i, 512, F32, st=ast) for i in range(2)])
                      ksts = Ring("kst", [sb("kst%d" % i, 512, st=ast) for i in range(2)])
                      pTs = Ring("pT", [ps("pT%d" % i, 1024, BF16, st=ast) for i in range(2)])
                      zps = Ring("zp", [ps("zp%d" % i, 512, st=ast) for i in range(2)])
                      rps = Ring("rp", [ps("rp%d" % i, 512, st=ast) for i in range(2)])
                      vps = Ring("vp", [ps("vp%d" % i, 512, st=ast) for i in range(2)])

                      P.add("sp", lambda e: e.dma_start(out=gpre[:], in_=dap(g_pre, 0, [[0, 128], [1, D]])), writes=["gpre"], dma_sem="gpre")

                      if pas == 0:
                          segs = [(0, 384, 0), (384, 128, 384), (448, 64, 512), (384, 64, 576), (1792, 256, 640), (512, 128, 896)]
                      else:
                          p_ = pas - 1
                          segs = [(640 + 128 * p_, 128, 0), (1024 + 128 * p_, 128, 128), (1408 + 128 * p_, 128, 256)]
                      for (sc, n, off) in segs:
                          P.add("pool", (lambda e, sc=sc, n=n, off=off: e.dma_start(
                              out=sap(wb, off, [[wcols, 8], [1, n]]),
                              in_=dap(w_in, sc, [[2048, 128], [128 * 2048, 8], [1, n]]))),
                              writes=["wb"], dma_sem="wb")

                      def ut_store(tok0, ntok, skey):
                          nb = (ntok + 511) // 512
                          P.add("sp", (lambda e: e.dma_start(out=dap(uts, tok0, [[8 * 4224, 128], [4224, 8], [1, ntok]]), in_=sap(uT, 0, [[2048, 8], [1, ntok]]))),
                                reads=["uT%d" % b for b in range(nb)], writes=[skey], dma_sem=skey)

                      def ut_load(tok0, ntok, skey):
                          for b in range((ntok + 511) // 512):
                              n_ = min(512, ntok - b * 512)
                              P.add("sp", (lambda e, b=b, n_=n_: e.dma_start(out=sap(uT, b * 512, [[2048, 8], [1, n_]]), in_=dap(uts, tok0 + b * 512, [[8 * 4224, 128], [4224, 8], [1, n_]]))),
                                    reads=[skey], writes=["uT%d" % b], dma_sem="uTl%d" % b)

                      def norm_T(xsrc, row0, ntok, gate_key="uT"):
                          npend = []
                          for t in range(ntok // 128):
                              xt, xk = xts.next(); hb, hk = hbs.next(); ss, sk = sss.next(); rs, rk = rss.next()
                              pT, pk = pTs.next()
                              r0 = row0 + t * 128
                              maybe_copy()
                              P.add("sp", (lambda e, xt=xt, r0=r0: e.dma_start(out=xt[:], in_=xsrc[r0:r0 + 128, :])), writes=[xk], dma_sem=xk)
                              P.add("act", (lambda e, xt=xt, ss=ss: e.activation(out=junk[:], in_=xt[:], func=AF.Square, accum_out=ss[:])),
                                    reads=[xk], writes=["junk", sk])
                              rstd_ops(ss, sk, rs, rk)
                              P.add("dve", (lambda e, xt=xt, rs=rs, hb=hb: e.scalar_tensor_tensor(out=hb[:], in0=xt[:], scalar=rs[:, 0:1], in1=gpre[:], op0=ALU.mult, op1=ALU.mult)),
                                    reads=[xk, rk, "gpre"], writes=[hk])
                              for dc in range(8):
                                  P.add("pe", (lambda e, pT=pT, hb=hb, dc=dc: e.transpose(out=pT[:, dc * 128:(dc + 1) * 128], in_=hb[:, dc * 128:(dc + 1) * 128], identity=ident[:])),
                                        reads=[hk, "ident"], writes=[pk])
                              def ev(pT=pT, pk=pk, t=t):
                                  P.add("act", (lambda e: e.activation(out=sap(uT, t * 128, [[2048, 8], [1, 128]]), in_=sap(pT, 0, [[128, 8], [1, 128]]), func=AF.Copy)),
                                        reads=[pk], writes=["uT%d" % (t // 4)])
                              npend.append(ev)
                              if len(npend) > 1:
                                  npend.pop(0)()
                          while npend:
                              npend.pop(0)()

                      def fm_proj(woff, dst, dkey, dcol, tb0, ntb, ctab0, rope):
                          zp, zk = zps.next()
                          for dc in range(8):
                              P.add("pe", (lambda e, zp=zp, dc=dc: e.matmul(zp[:, 0:ntb], lhsT=wb[:, dc * wcols + woff: dc * wcols + woff + 128],
                                                                             rhs=uT[:, dc * 2048 + tb0: dc * 2048 + tb0 + ntb], start=(dc == 0), stop=(dc == 7))),
                                    reads=["wb", "uT%d" % (tb0 // 512)], writes=[zk])
                          if not rope:
                              P.add("act", (lambda e, zp=zp: e.activation(out=dst[:, dcol:dcol + ntb], in_=zp[:, 0:ntb], func=AF.Copy)),
                                    reads=[zk], writes=[dkey])
                              return
                          zb, zbk = zbs.next(); rp, rk = rps.next(); t1, t1k = t1s.next(); t2, t2k = t2s.next()
                          cs, ck = cur_tab["cos"]; sn, snk = cur_tab["sin"]
                          P.add("act", (lambda e, zp=zp, zb=zb: e.activation(out=zb[:, 0:ntb], in_=zp[:, 0:ntb], func=AF.Copy)), reads=[zk], writes=[zbk])
                          P.add("pe", (lambda e, rp=rp, zb=zb: e.matmul(rp[:, 0:ntb], lhsT=rot[:], rhs=zb[:, 0:ntb], start=True, stop=True)),
                                reads=[zbk, "rot"], writes=[rk])
                          P.add("dve", (lambda e, zp=zp, t1=t1, cs=cs: e.tensor_tensor(out=t1[:, 0:ntb], in0=zp[:, 0:ntb], in1=cs[:, 0:ntb], op=ALU.mult)),
                                reads=[zk, ck], writes=[t1k])
                          P.add("dve", (lambda e, rp=rp, t2=t2, sn=sn: e.tensor_tensor(out=t2[:, 0:ntb], in0=rp[:, 0:ntb], in1=sn[:, 0:ntb], op=ALU.mult)),
                                reads=[rk, snk], writes=[t2k])
                          P.add("pool", (lambda e, t1=t1, t2=t2: e.tensor_tensor(out=dst[:, dcol:dcol + ntb], in0=t1[:, 0:ntb], in1=t2[:, 0:ntb], op=ALU.add)),
                                reads=[t1k, t2k], writes=[dkey])

                      cur_tab = {}

                      def load_tab(tcol, ntb):
                          cs, ck = coss.next(); sn, snk = sins.next()
                          P.add("sp", (lambda e, cs=cs: e.dma_start(out=cs[:, 0:ntb], in_=c_cos[:, tcol:tcol + ntb])), writes=[ck], dma_sem=ck)
                          P.add("sp", (lambda e, sn=sn: e.dma_start(out=sn[:, 0:ntb], in_=c_sin[:, tcol:tcol + ntb])), writes=[snk], dma_sem=snk)
                          cur_tab["cos"] = (cs, ck); cur_tab["sin"] = (sn, snk)

                      def tm_proj(woff, ncols, tok_off, tok_step, dst, dkey, blk):
                          vp, vk = vps.next()
                          last = tok_off + 127 * tok_step
                          ukeys = ["uT%d" % b for b in range(tok_off // 512, last // 512 + 1)]
                          for dc in range(8):
                              P.add("pe", (lambda e, vp=vp, dc=dc: e.matmul(vp[:, 0:ncols], lhsT=sap(uT, dc * 2048 + tok_off, [[tok_step, 128]]),
                                                                             rhs=wb[:, dc * wcols + woff: dc * wcols + woff + ncols], start=(dc == 0), stop=(dc == 7))),
                                    reads=["wb"] + ukeys, writes=[vk])
                          bs = 320 if pas == 0 else 192
                          P.add("act", (lambda e, vp=vp: e.activation(out=sap(dst, blk * bs, [[128, 2], [1, 64]]), in_=sap(vp, 0, [[64, 2], [1, 64]]), func=AF.Copy)),
                                reads=[vk], writes=[dkey])
                          if pas == 0:
                              P.add("act", (lambda e, vp=vp: e.activation(out=dst[:, blk * 320 + 256: blk * 320 + 320], in_=vp[:, 0:64], func=AF.Copy)),
                                    reads=[vk], writes=[dkey])

                      def k_out(src, skey, scol, ntile, emit_dma):
                          pT, pk = pTs.next(); kst, kk = ksts.next()
                          for j in range(ntile):
                              P.add("pe", (lambda e, pT=pT, j=j: e.transpose(out=pT[:, j * 128:(j + 1) * 128], in_=src[:, scol + j * 128: scol + (j + 1) * 128], identity=ident[:])),
                                    reads=[skey, "ident"], writes=[pk])
                          P.add("act", (lambda e, pT=pT, kst=kst: e.activation(out=kst[:, 0:ntile * 128], in_=pT[:, 0:ntile * 128], func=AF.Copy)),
                                reads=[pk], writes=[kk])
                          emit_dma(kst, kk)

                      chk(0.1)
                      if pas == 0:
                          wm = sb("wm", 8 * 512, st=ast)
                          P.add("pool", lambda e: e.dma_start(out=sap(wm, 0, [[512, 8], [1, 512]]), in_=dap(w_mem, 0, [[512, 128], [128 * 512, 8], [1, 512]])),
                                writes=["wm"], dma_sem="wm")
                          for blk in range(2):
                              hb, hk = hbs.next(); pT, pk = pTs.next()
                              P.add("pool", (lambda e, hb=hb, blk=blk: e.dma_start(out=hb[:], in_=memp[blk * 128:(blk + 1) * 128, :])), writes=[hk], dma_sem=hk)
                              for dc in range(8):
                                  P.add("pe", (lambda e, pT=pT, hb=hb, dc=dc: e.transpose(out=pT[:, dc * 128:(dc + 1) * 128], in_=hb[:, dc * 128:(dc + 1) * 128], identity=ident[:])),
                                        reads=[hk, "ident"], writes=[pk])
                              P.add("act", (lambda e, pT=pT, blk=blk: e.activation(out=sap(uT, blk * 128, [[2048, 8], [1, 128]]), in_=sap(pT, 0, [[128, 8], [1, 128]]), func=AF.Copy)),
                                    reads=[pk], writes=["uT0"])
                          for ch in range(2):
                              zp, zk = zps.next()
                              for dc in range(8):
                                  P.add("pe", (lambda e, zp=zp, dc=dc, ch=ch: e.matmul(zp[:, 0:256], lhsT=wm[:, dc * 512 + ch * 128: dc * 512 + (ch + 1) * 128],
                                                                                       rhs=uT[:, dc * 2048: dc * 2048 + 256], start=(dc == 0), stop=(dc == 7))),
                                        reads=["wm", "uT0"], writes=[zk])
                              P.add("act", (lambda e, zp=zp, ch=ch: e.activation(out=MKT[:, ch * 256:(ch + 1) * 256], in_=zp[:, 0:256], func=AF.Copy)),
                                    reads=[zk], writes=["MKT"])
                          mkn = sb("mkn", 512, st=ast)
                          for blk in range(2):
                              vp, vk = vps.next()
                              for dc in range(8):
                                  P.add("pe", (lambda e, vp=vp, dc=dc, blk=blk: e.matmul(vp[:, 0:512], lhsT=uT[:, dc * 2048 + blk * 128: dc * 2048 + (blk + 1) * 128],
                                                                                         rhs=wm[:, dc * 512:(dc + 1) * 512], start=(dc == 0), stop=(dc == 7))),
                                        reads=["wm", "uT0"], writes=[vk])
                              P.add("act", (lambda e, vp=vp, blk=blk: e.activation(out=mkn[:, blk * 256:(blk + 1) * 256], in_=vp[:, 0:256], func=AF.Copy)),
                                    reads=[vk], writes=["mkn"])
                              P.add("act", (lambda e, vp=vp, blk=blk: e.activation(out=sap(MV, blk * 384, [[192, 2], [128, 2], [1, 64]]), in_=sap(vp, 256, [[128, 2], [64, 2], [1, 64]]), func=AF.Copy)),
                                    reads=[vk], writes=["MV"])
                          P.add("pool", lambda e: e.dma_start(out=dap(o_mem_k, 0, [[256, 128], [128 * 256, 2], [1, 256]]), in_=sap(mkn, 0, [[256, 2], [1, 256]])),
                                reads=["mkn"], dma_sem="o_mem_k")
                          for blk in range(2):
                              for pr in range(2):
                                  P.add("pool", (lambda e, blk=blk, pr=pr: e.dma_start(out=dap(o_mem_v, blk * 128 * 256 + pr * 128, [[256, 128], [64, 2], [1, 64]]),
                                                                                       in_=sap(MV, blk * 384 + pr * 192, [[128, 2], [1, 64]]))),
                                        reads=["MV"], dma_sem="o_mem_v")

                          chk(0.2)
                          norm_T(xh, 1920, 128)
                          chk(0.21)
                          load_tab(1920, 128)
                          chk(0.22)
                          fm_proj(384, KAx, "KAx", 0, 0, 128, None, True)
                          chk(0.23)
                          fm_proj(512, KAy, "KAy", 0, 0, 128, None, True)
                          chk(0.24)
                          tm_proj(896, 128, 0, 1, VN, "VN", 0)
                          chk(0.3)
                          norm_T(xo, 0, 2048)
                          ut_store(2048, 2048, "uts_own")
                          chk(0.4)
                          for tb in range(4):
                              load_tab(2048 + tb * 512, 512)
                              for c in range(3):
                                  fm_proj(c * 128, QA, "QA", c * 2176 + tb * 512, tb * 512, 512, None, True)
                              fm_proj(384, KAx, "KAx", 128 + tb * 512, tb * 512, 512, None, True)
                              fm_proj(512, KAy, "KAy", 128 + tb * 512, tb * 512, 512, None, True)
                              for c in range(2):
                                  fm_proj(640 + c * 128, QX, "QX", c * 2176 + tb * 512, tb * 512, 512, None, False)
                          for j in range(16):
                              tm_proj(896, 128, j * 128, 1, VN, "VN", 1 + j)
                          chk(0.5)
                          k_out(KAx, "KAx", 128 + 15 * 128, 1,
                                lambda kst, kk: P.add("pool", (lambda e: e.dma_start(out=o_swa_k[:, :], in_=kst[:, 0:128])), reads=[kk], dma_sem=kk))
                          P.add("pool", lambda e: e.dma_start(out=dap(o_swa_v, 0, [[128, 128], [64, 2], [1, 64]]), in_=sap(VN, 16 * 320, [[128, 2], [1, 64]])), reads=["VN"], dma_sem="o_swa_v")
                          chk(0.6)
                          norm_T(xs, 0, 128)
                          ut_store(4096, 128, "uts_smp")
                          load_tab(4096, 128)
                          for c in range(3):
                              fm_proj(c * 128, QA, "QA", c * 2176 + 2048, 0, 128, None, True)
                          fm_proj(384, KAx, "KAx", 2176, 0, 128, None, True)
                          fm_proj(512, KAy, "KAy", 2176, 0, 128, None, True)
                          for c in range(2):
                              fm_proj(640 + c * 128, QX, "QX", c * 2176 + 2048, 0, 128, None, False)
                          tm_proj(896, 128, 0, 1, VN, "VN", 17)

                          def sw_new_k(kst, kk):
                              for s in range(16):
                                  P.add("pool", (lambda e, s=s: e.dma_start(out=s_swa_k[s, 120:128, :], in_=kst[s * 8:(s + 1) * 8, 0:128])), reads=[kk], dma_sem=kk)
                          k_out(KAx, "KAx", 2176, 1, sw_new_k)
                          for s in range(16):
                              P.add("pool", (lambda e, s=s: e.dma_start(out=dap(s_swa_v, s * 128 * 128 + 120 * 128, [[128, 8], [64, 2], [1, 64]]), in_=sap(VN, 17 * 320, [[128, 2], [1, 64]], nparts=8, p0=s * 8))),
                                    reads=["VN"], dma_sem="s_swa_v_new")
                      else:
                          p_ = pas - 1
                          if pas == 1:
                              norm_T(xh, 0, 2048)
                              ut_store(0, 2048, "uts_halo")
                          else:
                              ut_load(0, 2048, "uts_halo")
                          for tb in range(4):
                              load_tab(tb * 512, 512)
                              fm_proj(128, KB, "KB", tb * 512, tb * 512, 512, None, True)
                          tm_proj(256, 128, 15 * 128, 1, VN, "VN", 0)
                          for c in range(4):
                              tm_proj(256, 128, 1536 + c, 4, V4, "V4", c)
                          for c in range(16):
                              tm_proj(256, 128, c, 16, V16, "V16", c)
                          ut_load(2048, 2048, "uts_own")
                          for tb in range(4):
                              load_tab(2048 + tb * 512, 512)
                              fm_proj(0, QB, "QB", tb * 512, tb * 512, 512, None, True)
                              fm_proj(128, KB, "KB", 2048 + tb * 512, tb * 512, 512, None, True)
                          for j in range(16):
                              tm_proj(256, 128, j * 128, 1, VN, "VN", 1 + j)
                          for n in range(4):
                              for c in range(4):
                                  tm_proj(256, 128, n * 512 + c, 4, V4, "V4", (n + 1) * 4 + c)
                          for c in range(16):
                              tm_proj(256, 128, c, 16, V16, "V16", 16 + c)
                          for q4 in range(4):
                              k_out(KB, "KB", 2048 + q4 * 512, 4,
                                    lambda kst, kk, q4=q4: P.add("pool", (lambda e: e.dma_start(
                                        out=dap(o_dil_k, q4 * 512 * 384 + p_ * 128, [[384, 128], [128 * 384, 4], [1, 128]]),
                                        in_=sap(kst, 0, [[128, 4], [1, 128]]))), reads=[kk], dma_sem=kk))
                          for hd in range(2):
                              P.add("pool", (lambda e, hd=hd: e.dma_start(out=dap(o_dil_v, p_ * 128 + hd * 64, [[384, 128], [128 * 384, 16], [1, 64]]),
                                                                          in_=sap(VN, 192 + hd * 128, [[192, 16], [1, 64]]))), reads=["VN"], dma_sem="o_dil_v")
                          ut_load(4096, 128, "uts_smp")
                          load_tab(4096, 128)
                          fm_proj(0, QB, "QB", 2048, 0, 128, None, True)
                          fm_proj(128, KB, "KB", 4096, 0, 128, None, True)
                          tm_proj(256, 128, 0, 1, VN, "VN", 17)

                          def dil_new_k(kst, kk):
                              for s in range(16):
                                  P.add("pool", (lambda e, s=s: e.dma_start(out=s_dil_k[s, 2040:2048, p_ * 128:(p_ + 1) * 128], in_=kst[s * 8:(s + 1) * 8, 0:128])),
                                        reads=[kk], dma_sem=kk)
                          k_out(KB, "KB", 4096, 1, dil_new_k)
                          for s in range(16):
                              P.add("pool", (lambda e, s=s: e.dma_start(out=dap(s_dil_v, s * 2048 * 384 + 2040 * 384 + p_ * 128, [[384, 8], [64, 2], [1, 64]]), in_=sap(VN, 17 * 192, [[128, 2], [1, 64]], nparts=8, p0=s * 8))),
                                    reads=["VN"], dma_sem="s_dil_v_new")

                      P.flush()
                  if STOP[0] <= 2 * pas + 1:
                      break
                  with contextlib.ExitStack() as bst:
                      acc = sb("acc", 2 * 2176, F32, st=bst)
                      rl = sb("rl", 2176, F32, st=bst)
                      pts = Ring("pt", [sb("pt%d" % i, 512, st=bst) for i in range(4)])

                      accs = sb("accs", 6 * 128, F32, st=bst)
                      accx = sb("accx", 4 * 128, F32, st=bst)
                      with contextlib.ExitStack() as sst:
                          k32s = Ring("k32", [sb("k32_%d" % i, 2048, F32, st=sst) for i in range(2)])
                          v32s = Ring("v32", [sb("v32_%d" % i, 2048, F32, st=sst) for i in range(2)])
                          kdts = Ring("kdt", [sb("kdt%d" % i, 2048, st=sst) for i in range(2)])
                          ptn = sb("ptn", 256, st=sst)
                          tps = Ring("tp", [ps("tp%d" % i, 512, st=sst) for i in range(2)])
                          sqs = Ring("sq", [ps("sq%d" % i, 512, st=sst) for i in range(2)])
                          snr = Ring("st", [ps("sn%d" % i, 512, st=sst) for i in range(2)])
                          ocA = ps("ocA", 512, st=sst); ocB = ps("ocB", 512, st=sst)

                          def transposes32(src, skey, ncols_src, n, dst, dkey):
                              for q in range((n + 3) // 4):
                                  tp, tpk = tps.next()
                                  m_ = min(4, n - q * 4)
                                  for c4 in range(m_):
                                      c = q * 4 + c4
                                      P.add("pe", (lambda e, tp=tp, c4=c4, c=c: e.transpose(out=tp[:, c4 * 128:(c4 + 1) * 128], in_=src[:, c * 128:(c + 1) * 128], identity=identf[:])),
                                            reads=[skey, "identf"], writes=[tpk])
                                  P.add("act", (lambda e, tp=tp, q=q, m_=m_: e.activation(out=dst[:, q * 512: q * 512 + m_ * 128], in_=tp[:, 0:m_ * 128], func=AF.Copy)),
                                        reads=[tpk], writes=[dkey])

                          def new_keys(pairs, ktiles, qt, qbase, kcol, vt, vkey, voffs, masknew, mnkey, outs):
                              for i, (half, (kt, kkey)) in enumerate(zip((0, 1), ktiles)):
                                  sn, snk = snr.next()
                                  P.add("pe", (lambda e, sn=sn, kt=kt, half=half, i=i: e.matmul(sn[:, 0:128], lhsT=sap(kt, kcol, [[1, 128]], nparts=64, p0=half * 64),
                                                                                                rhs=sap(qt, qbase, [[1, 128]], nparts=64, p0=half * 64), start=True, stop=True)),
                                        reads=[kkey, qkey_s], writes=[snk])
                                  P.add("act", (lambda e, sn=sn, i=i: e.activation(out=ptn[:, i * 128:(i + 1) * 128], in_=sn[:, 0:128], func=AF.Exp, scale=SCALE)),
                                        reads=[snk], writes=["ptn"])
                              P.add("pool", (lambda e: e.tensor_tensor(out=sap(ptn, 0, [[128, 2], [1, 128]]), in0=sap(ptn, 0, [[128, 2], [1, 128]]),
                                                                       in1=sap(masknew, 0, [[0, 2], [1, 128]]), op=ALU.mult)), reads=["ptn", mnkey], writes=["ptn"])
                              for i in range(2):
                                  ob_, ok_, ocol = outs[i]
                                  P.add("pe", (lambda e, i=i, ob_=ob_, ocol=ocol: e.matmul(ob_[:, ocol:ocol + 128], lhsT=vt[:, voffs[i]: voffs[i] + 128],
                                                                                           rhs=ptn[:, i * 128:(i + 1) * 128], start=True, stop=True)),
                                        reads=[vkey, vkey + "_ones", "ptn"], writes=[ok_])

                          if pas == 0:
                              SWOFF = {(0, 0): 0, (1, 1): 64, (0, 1): 128, (1, 0): 192}
                              qkey_s = "QA"
                              vc = sb("vc", 16 * 320, st=sst)
                              qbs = sb("qbs", 768, st=sst)
                              pts_ = sb("pts_", 768, st=sst)
                              P.add("pool", lambda e: e.memset(sap(vc, 64, [[320, 16], [128, 2], [1, 64]]), 1.0), writes=["vc_ones"])
                              P.add("pool", lambda e: e.memset(qbs[:], 0.0), writes=["qbs"])
                              for h in range(6):
                                  P.add("dve", (lambda e, h=h: e.tensor_copy(out=sap(qbs, h * 8, [[48, 16], [1, 8]], nparts=64, p0=(h // 3) * 64),
                                                                             in_=sap(QA, (h // 2) * 2176 + 2048, [[8, 16], [1, 8]], nparts=64, p0=(h % 2) * 64))),
                                        reads=["QA"], writes=["qbs"])
                              k32, k32k = k32s.next(); v32, v32k = v32s.next(); kdt, kdtk = kdts.next()
                              P.add("sp", lambda e: e.dma_start(out=sap(k32, 0, [[128, 16], [1, 128]]), in_=dap(csk, 0, [[128, 128], [128 * 128, 16], [1, 128]])), writes=[k32k], dma_sem=k32k)
                              P.add("act", lambda e: e.dma_start(out=sap(v32, 0, [[128, 16], [1, 128]]), in_=dap(csv, 0, [[128, 128], [128 * 128, 16], [1, 128]])), writes=[v32k], dma_sem=v32k)
                              P.add("pool", lambda e: e.tensor_copy(out=sap(vc, 0, [[320, 16], [128, 2], [1, 64]]), in_=sap(v32, 0, [[128, 16], [64, 2], [1, 64]])), reads=[v32k], writes=["vc"])
                              P.add("pool", lambda e: e.tensor_copy(out=sap(vc, 256, [[320, 16], [1, 64]]), in_=sap(v32, 0, [[128, 16], [1, 64]])), reads=[v32k], writes=["vc"])
                              transposes32(k32, k32k, 2048, 16, kdt, kdtk)
                              sqa, sqak = sqs.next(); sqb, sqbk = sqs.next()
                              for s_ in range(16):
                                  sq_, sqk_ = (sqa, sqak) if s_ < 8 else (sqb, sqbk)
                                  P.add("pe", (lambda e, s_=s_, sq_=sq_: e.matmul(sq_[:, (s_ % 8) * 48:(s_ % 8 + 1) * 48], lhsT=kdt[:, s_ * 128:(s_ + 1) * 128],
                                                                                  rhs=qbs[:, s_ * 48:(s_ + 1) * 48], start=True, stop=True)), reads=[kdtk, "qbs"], writes=[sqk_])
                              for i_, (sq_, sqk_) in enumerate(((sqa, sqak), (sqb, sqbk))):
                                  P.add("act", (lambda e, i_=i_, sq_=sq_: e.activation(out=pts_[:, i_ * 384:(i_ + 1) * 384], in_=sq_[:, 0:384], func=AF.Exp, scale=SCALE)),
                                        reads=[sqk_], writes=["pts_"])
                              P.add("pool", lambda e: e.tensor_tensor(out=pts_[:], in0=pts_[:], in1=msw[:], op=ALU.mult), reads=["pts_", "msw"], writes=["pts_"])
                              for s_ in range(16):
                                  for h in range(6):
                                      oc_, ock_, oco = (ocA, "oc#0", h * 128) if h < 4 else (ocB, "oc#1", (h - 4) * 128)
                                      voff = s_ * 320 + SWOFF[(h % 2, h // 3)]
                                      P.add("pe", (lambda e, s_=s_, h=h, oc_=oc_, oco=oco, voff=voff: e.matmul(oc_[:, oco + s_ * 8: oco + s_ * 8 + 8], lhsT=vc[:, voff:voff + 128],
                                                                                                            rhs=pts_[:, s_ * 48 + h * 8: s_ * 48 + h * 8 + 8], start=True, stop=True)),
                                            reads=["vc", "vc_ones", "pts_"], writes=[ock_])
                              P.add("dve", lambda e: e.tensor_copy(out=accs[:, 0:512], in_=ocA[:, 0:512]), reads=["oc#0"], writes=["accs"])
                              P.add("dve", lambda e: e.tensor_copy(out=accs[:, 512:768], in_=ocB[:, 0:256]), reads=["oc#1"], writes=["accs"])
                              for cpair in range(3):
                                  he, ho = 2 * cpair, 2 * cpair + 1
                                  ktl = [(KAx, "KAx") if (h // 3) == (h % 2) else (KAy, "KAy") for h in (he, ho)]
                                  voffs = [17 * 320 + SWOFF[(h % 2, h // 3)] for h in (he, ho)]
                                  new_keys(None, ktl, QA, cpair * 2176 + 2048, 2176, VN, "VN", voffs, mnsw, "mnsw", [(ocA, "oc#0", 0), (ocA, "oc#0", 128)])
                                  P.add("dve", (lambda e, cpair=cpair: e.tensor_tensor(out=accs[:, cpair * 256:(cpair + 1) * 256], in0=ocA[:, 0:256], in1=accs[:, cpair * 256:(cpair + 1) * 256], op=ALU.add)),
                                        reads=["oc#0", "accs"], writes=["accs"])
                              vmq = sb("vmq", 8 * 384, st=sst)
                              qbx = sb("qbx", 512, st=sst)
                              ptm = sb("ptm", 256, st=sst)
                              P.add("pool", lambda e: e.memset(sap(vmq, 64, [[192, 16], [1, 64]]), 1.0), writes=["vmq_ones"])
                              P.add("pool", lambda e: e.memset(qbx[:], 0.0), writes=["qbx"])
                              for ch in range(2):
                                  for hd in range(2):
                                      P.add("dve", (lambda e, ch=ch, hd=hd: e.tensor_copy(out=sap(qbx, ch * 16 + hd * 8, [[32, 16], [1, 8]], nparts=64, p0=hd * 64),
                                                                                        in_=sap(QX, ch * 2176 + 2048, [[8, 16], [1, 8]], nparts=64, p0=hd * 64))),
                                            reads=["QX"], writes=["qbx"])
                              for qd in range(4):
                                  k32, k32k = k32s.next(); v32, v32k = v32s.next(); kdt, kdtk = kdts.next()
                                  P.add("sp", (lambda e, k32=k32, qd=qd: e.dma_start(out=sap(k32, 0, [[256, 8], [1, 256]]), in_=dap(cmk, qd * 4 * 65536, [[256, 128], [128 * 256, 8], [1, 256]]))),
                                        writes=[k32k], dma_sem=k32k)
                                  P.add("act", (lambda e, v32=v32, qd=qd: e.dma_start(out=sap(v32, 0, [[256, 8], [1, 256]]), in_=dap(cmv, qd * 4 * 65536, [[256, 128], [128 * 256, 8], [1, 256]]))),
                                        writes=[v32k], dma_sem=v32k)
                                  for pr in range(2):
                                      P.add("pool", (lambda e, v32=v32, pr=pr: e.tensor_copy(out=sap(vmq, pr * 192, [[384, 8], [128, 2], [1, 64]]), in_=sap(v32, pr * 128, [[256, 8], [64, 2], [1, 64]]))),
                                            reads=[v32k], writes=["vmq"])
                                  transposes32(k32, k32k, 2048, 16, kdt, kdtk)
                                  sq_, sqk_ = sqs.next()
                                  for sl in range(4):
                                      for blk in range(2):
                                          for ch in range(2):
                                              idx = (sl * 2 + blk) * 2 + ch
                                              sg_ = qd * 4 + sl
                                              P.add("pe", (lambda e, idx=idx, sg_=sg_, ch=ch, sq_=sq_, kdt=kdt: e.matmul(sq_[:, idx * 16:(idx + 1) * 16], lhsT=kdt[:, idx * 128:(idx + 1) * 128],
                                                                                                                     rhs=qbx[:, (sg_ * 2 + ch) * 16:(sg_ * 2 + ch + 1) * 16], start=True, stop=True)),
                                                    reads=[kdtk, "qbx"], writes=[sqk_])
                                  P.add("act", (lambda e, sq_=sq_: e.activation(out=ptm[:], in_=sq_[:, 0:256], func=AF.Exp, scale=SCALE)), reads=[sqk_], writes=["ptm"])
                                  for sl in range(4):
                                      sg_ = qd * 4 + sl
                                      for x in range(4):
                                          for blk in range(2):
                                              voff = (sl * 2 + blk) * 384 + (x // 2) * 192 + (x % 2) * 64
                                              pcol = ((sl * 2 + blk) * 2 + x // 2) * 16 + (x % 2) * 8
                                              P.add("pe", (lambda e, sg_=sg_, x=x, blk=blk, voff=voff, pcol=pcol: e.matmul(ocB[:, x * 128 + sg_ * 8: x * 128 + sg_ * 8 + 8], lhsT=vmq[:, voff:voff + 128],
                                                                                                                       rhs=ptm[:, pcol:pcol + 8], start=(blk == 0), stop=(blk == 1))),
                                                    reads=["vmq", "vmq_ones", "ptm"], writes=["oc#1"])
                              P.add("dve", lambda e: e.tensor_copy(out=accx[:], in_=ocB[:]), reads=["oc#1"], writes=["accx"])
                          else:
                              p_ = pas - 1
                              qkey_s = "QB"
                              vds = [sb("vd%d" % i, 16 * 192, st=sst) for i in range(2)]
                              vdr = Ring("vd", vds)
                              qbd = sb("qbd", 256, st=sst)
                              ptds = Ring("ptd", [sb("ptd%d" % i, 256, st=sst) for i in range(2)])
                              for i, vt in enumerate(vds):
                                  P.add("pool", (lambda e, vt=vt: e.memset(sap(vt, 64, [[192, 16], [1, 64]]), 1.0)), writes=["vd#%d_ones" % i])
                              P.add("pool", lambda e: e.memset(qbd[:], 0.0), writes=["qbd"])
                              for hd in range(2):
                                  P.add("dve", (lambda e, hd=hd: e.tensor_copy(out=sap(qbd, hd * 8, [[16, 16], [1, 8]], nparts=64, p0=hd * 64),
                                                                               in_=sap(QB, 2048, [[8, 16], [1, 8]], nparts=64, p0=hd * 64))), reads=["QB"], writes=["qbd"])
                              def mk_seq(s_):
                                  st_ = {}

                                  def A1():
                                      k32, k32k = k32s.next(); v32, v32k = v32s.next(); kdt, kdtk = kdts.next(); vd, vdk = vdr.next()
                                      st_.update(kdt=kdt, kdtk=kdtk, vd=vd, vdk=vdk)
                                      P.add("sp", (lambda e: e.dma_start(out=sap(k32, 0, [[128, 16], [1, 128]]),
                                                                         in_=dap(cdk, s_ * 2048 * 384 + p_ * 128, [[16 * 384, 128], [384, 16], [1, 128]]))), writes=[k32k], dma_sem=k32k)
                                      P.add("act", (lambda e: e.dma_start(out=sap(v32, 0, [[128, 16], [1, 128]]),
                                                                          in_=dap(cdv, s_ * 2048 * 384 + p_ * 128, [[16 * 384, 128], [384, 16], [1, 128]]))), writes=[v32k], dma_sem=v32k)
                                      P.add("dve", (lambda e: e.tensor_copy(out=sap(vd, 0, [[192, 16], [128, 2], [1, 64]]), in_=sap(v32, 0, [[128, 16], [64, 2], [1, 64]]))),
                                            reads=[v32k], writes=[vdk])
                                      transposes32(k32, k32k, 2048, 16, kdt, kdtk)

                                  def A2():
                                      kdt, kdtk = st_["kdt"], st_["kdtk"]
                                      sq_, sqk_ = sqs.next(); ptd, ptdk = ptds.next()
                                      st_.update(ptd=ptd, ptdk=ptdk)
                                      for c in range(16):
                                          P.add("pe", (lambda e, c=c: e.matmul(sq_[:, c * 16:(c + 1) * 16], lhsT=kdt[:, c * 128:(c + 1) * 128],
                                                                               rhs=qbd[:, s_ * 16:(s_ + 1) * 16], start=True, stop=True)), reads=[kdtk, "qbd"], writes=[sqk_])
                                      P.add("act", (lambda e: e.activation(out=ptd[:], in_=sq_[:, 0:256], func=AF.Exp, scale=SCALE)), reads=[sqk_], writes=[ptdk])
                                      P.add("dve", (lambda e: e.tensor_tensor(out=ptd[:], in0=ptd[:], in1=mdil[:], op=ALU.mult)), reads=[ptdk, "mdil"], writes=[ptdk])

                                  def B():
                                      vd, vdk, ptd, ptdk = st_["vd"], st_["vdk"], st_["ptd"], st_["ptdk"]
                                      for hd in range(2):
                                          for c in range(16):
                                              P.add("pe", (lambda e, hd=hd, c=c: e.matmul(ocA[:, hd * 128 + s_ * 8: hd * 128 + s_ * 8 + 8], lhsT=vd[:, c * 192 + hd * 64: c * 192 + hd * 64 + 128],
                                                                                           rhs=ptd[:, c * 16 + hd * 8: c * 16 + hd * 8 + 8], start=(c == 0), stop=(c == 15))),
                                                    reads=[vdk, vdk + "_ones", ptdk], writes=["oc#0"])
                                  return A1, A2, B

                              seqs = [mk_seq(s_) for s_ in range(16)]
                              LAGS = (1, 0)
                              for i_ in range(16 + LAGS[0] + LAGS[1]):
                                  if i_ < 16:
                                      seqs[i_][0]()
                                  if LAGS[0] <= i_ < 16 + LAGS[0]:
                                      seqs[i_ - LAGS[0]][1]()
                                  if i_ >= LAGS[0] + LAGS[1]:
                                      seqs[i_ - LAGS[0] - LAGS[1]][2]()
                              P.add("dve", lambda e: e.tensor_copy(out=accs[:, 0:256], in_=ocA[:, 0:256]), reads=["oc#0"], writes=["accs"])
                              new_keys(None, [(KB, "KB"), (KB, "KB")], QB, 2048, 4096, VN, "VN", [17 * 192, 17 * 192 + 64], mndil, "mndil", [(ocB, "oc#1", 0), (ocB, "oc#1", 128)])
                              P.add("dve", lambda e: e.tensor_tensor(out=accs[:, 0:256], in0=ocB[:, 0:256], in1=accs[:, 0:256], op=ALU.add), reads=["oc#1", "accs"], writes=["accs"])
                          P.flush()
                      if pas == 0:
                          cast_weights()
                      sts = Ring("st", [ps("st%d" % i, 512, st=bst) for i in range(4)])
                      obs = Ring("ob", [ps("ob%d" % i, 512, st=bst) for i in range(2)])

                      def v_lhsT(vt, off, odd):
                          return vt[:, off:off + 128]

                      def unit_pair(qsrc, qkey, qcols, ksrcs, kcols2, vsrc, vkey, voffs2, mask, mkey, acc_cols, first):
                          pt, pk = pts.next()
                          maybe_copy()
                          for hd in range(2):
                              st, sk = sts.next()
                              kt, kkey = ksrcs[hd]
                              for blk in range(2):
                                  ks, kstep = kcols2[blk]
                                  P.add("pe", (lambda e, st=st, hd=hd, blk=blk, kt=kt, ks=ks, kstep=kstep: e.matmul(
                                      st[:, blk * 128: (blk + 1) * 128],
                                      lhsT=sap(kt, ks, [[kstep, 128]], nparts=64, p0=hd * 64),
                                      rhs=sap(qsrc, qcols[0], [[qcols[1], 128]], nparts=64, p0=hd * 64), start=True, stop=True)),
                                      reads=[kkey, qkey], writes=[sk])
                              P.add("act", (lambda e, st=st, pt=pt, hd=hd: e.activation(out=pt[:, hd * 256:(hd + 1) * 256], in_=st[:, 0:256], func=AF.Exp, scale=SCALE)),
                                    reads=[sk], writes=[pk])
                          if mask is not None:
                              ucnt[0] += 1
                              P.add("dve" if ucnt[0] % 2 else "pool", (lambda e, pt=pt: e.tensor_tensor(out=pt[:], in0=pt[:], in1=mask[:], op=ALU.mult)), reads=[pk, mkey], writes=[pk])

                          def stage2():
                              ob, ok = obs.next()
                              for hd in range(2):
                                  for blk in range(2):
                                      P.add("pe", (lambda e, ob=ob, pt=pt, hd=hd, blk=blk: e.matmul(
                                          ob[:, hd * 128:(hd + 1) * 128], lhsT=v_lhsT(vsrc, voffs2[hd][blk], hd == 1),
                                          rhs=pt[:, hd * 256 + blk * 128: hd * 256 + (blk + 1) * 128], start=(blk == 0), stop=(blk == 1))),
                                          reads=[vkey, vkey + "_ones", pk], writes=[ok])
                              a0, astep = acc_cols
                              dst = sap(acc, a0, [[2176, 2], [astep, 128]])
                              src = sap(ob, 0, [[128, 2], [1, 128]])
                              if first:
                                  P.add("dve", (lambda e: e.tensor_copy(out=dst, in_=src)), reads=[ok], writes=["acc"])
                              else:
                                  P.add("dve", (lambda e: e.tensor_tensor(out=dst, in0=src, in1=dst, op=ALU.add)), reads=[ok, "acc"], writes=["acc"])
                          pend.append(stage2)
                          if len(pend) > 2:
                              pend.pop(0)()

                      pend = []
                      ucnt = [0]

                      def drain():
                          while pend:
                              pend.pop(0)()

                      def finish_pair(chunk, c0, n, sink_heads=None):
                          if sink_heads is not None:
                              he, ho = sink_heads
                              P.add("act", (lambda e: e.activation(out=sap(acc, c0, [[1, n]], nparts=64, p0=64), in_=sap(acc, c0, [[1, n]], nparts=64, p0=64),
                                                                   func=AF.Identity, bias=esink[64:128, he:he + 1])), reads=["acc", "esink"], writes=["acc"])
                              P.add("act", (lambda e: e.activation(out=sap(acc, 2176 + c0, [[1, n]], nparts=64, p0=0), in_=sap(acc, 2176 + c0, [[1, n]], nparts=64, p0=0),
                                                                   func=AF.Identity, bias=esink[0:64, ho:ho + 1])), reads=["acc", "esink"], writes=["acc"])
                          P.add("dve", (lambda e: e.reciprocal(out=sap(rl, c0, [[1, n]], nparts=64, p0=0), in_=sap(acc, c0, [[1, n]], nparts=64, p0=64))),
                                reads=["acc"], writes=["rl"])
                          P.add("dve", (lambda e: e.reciprocal(out=sap(rl, c0, [[1, n]], nparts=64, p0=64), in_=sap(acc, 2176 + c0, [[1, n]], nparts=64, p0=0))),
                                reads=["acc"], writes=["rl"])
                          P.add("dve", (lambda e: e.tensor_tensor(out=sap(catT, chunk * CT + c0, [[1, n]], nparts=64, p0=0), in0=sap(acc, c0, [[1, n]], nparts=64, p0=0),
                                                                  in1=sap(rl, c0, [[1, n]], nparts=64, p0=0), op=ALU.mult)), reads=["acc", "rl"], writes=["catT"])
                          P.add("dve", (lambda e: e.tensor_tensor(out=sap(catT, chunk * CT + c0, [[1, n]], nparts=64, p0=64), in0=sap(acc, 2176 + c0, [[1, n]], nparts=64, p0=64),
                                                                  in1=sap(rl, c0, [[1, n]], nparts=64, p0=64), op=ALU.mult)), reads=["acc", "rl"], writes=["catT"])

                      if pas == 0:
                          for cpair in range(3):
                              he, ho = 2 * cpair, 2 * cpair + 1
                              ksrcs = []
                              for h in (he, ho):
                                  kv = h // 3
                                  ksrcs.append((KAx, "KAx") if kv == (h % 2) else (KAy, "KAy"))
                              for j in range(16):
                                  voffs2 = [[(j + blk) * 320 + {(0, 0): 0, (1, 1): 64, (0, 1): 128, (1, 0): 192}[(h % 2, h // 3)] for blk in range(2)] for h in (he, ho)]
                                  m, mk_ = (mfirst, "mfirst") if j == 0 else (mstd, "mstd")
                                  unit_pair(QA, "QA", (cpair * 2176 + j * 128, 1), ksrcs, [(j * 128, 1), (128 + j * 128, 1)],
                                            VN, "VN", voffs2, m, mk_, (j * 128, 1), True)
                              drain()
                              P.add("dve", (lambda e, cpair=cpair: e.tensor_copy(out=sap(acc, 2048, [[2176, 2], [1, 128]]), in_=sap(accs, cpair * 256, [[128, 2], [1, 128]]))),
                                    reads=["accs"], writes=["acc"])
                              finish_pair(cpair, 0, 2176, sink_heads=(he, ho))
                          for cpair in range(2):
                              for j in range(16):
                                  voffs2 = [[blk * 384 + cpair * 192 + hd * 64 for blk in range(2)] for hd in range(2)]
                                  unit_pair(QX, "QX", (cpair * 2176 + j * 128, 1), [(MKT, "MKT"), (MKT, "MKT")],
                                            [(cpair * 256, 1), (cpair * 256 + 128, 1)], MV, "MV", voffs2, None, None, (j * 128, 1), True)
                              drain()
                              P.add("dve", (lambda e, cpair=cpair: e.tensor_copy(out=sap(acc, 2048, [[2176, 2], [1, 128]]), in_=sap(accx, cpair * 256, [[128, 2], [1, 128]]))),
                                    reads=["accx"], writes=["acc"])
                              finish_pair(6 + cpair, 0, 2176)
                      else:
                          p_ = pas - 1
                          kb2 = [(KB, "KB"), (KB, "KB")]
                          for j in range(16):
                              voffs2 = [[(j + blk) * 192 + hd * 64 for blk in range(2)] for hd in range(2)]
                              m, mk_ = (mfirst, "mfirst") if j == 0 else (mstd, "mstd")
                              unit_pair(QB, "QB", (j * 128, 1), kb2, [(2048 + (j - 1) * 128, 1), (2048 + j * 128, 1)],
                                        VN, "VN", voffs2, m, mk_, (j * 128, 1), True)
                          for n in range(4):
                              for c in range(4):
                                  voffs2 = [[((n + blk) * 4 + c) * 192 + hd * 64 for blk in range(2)] for hd in range(2)]
                                  m, mk_ = (mfirst, "mfirst") if n == 0 else (mstd, "mstd")
                                  unit_pair(QB, "QB", (n * 512 + c, 4), kb2, [(2048 + (n - 1) * 512 + c, 4), (2048 + n * 512 + c, 4)],
                                            V4, "V4", voffs2, m, mk_, (n * 512 + c, 4), False)
                          for c in range(16):
                              voffs2 = [[(blk * 16 + c) * 192 + hd * 64 for blk in range(2)] for hd in range(2)]
                              unit_pair(QB, "QB", (c, 16), kb2, [(c, 16), (2048 + c, 16)], V16, "V16", voffs2, mfirst, "mfirst", (c, 16), False)
                          drain()
                          P.add("dve", lambda e: e.tensor_copy(out=sap(acc, 2048, [[2176, 2], [1, 128]]), in_=sap(accs, 0, [[128, 2], [1, 128]])), reads=["accs"], writes=["acc"])
                          finish_pair(3 + p_, 0, 2176)
                      P.flush()


            except _StopBuild:
                P.flush()
                break

        if STOP[0] > 8:
            with contextlib.ExitStack() as cst:
                NTGM = 640
                while pending_cp:
                    maybe_copy(force=True)
                x1 = sb("x1", 5 * 1024, F32, st=cst)
                hT = sb("hT", 8 * NTGM, st=cst)
                actT = sb("actT", NFC * NTGM, st=cst)
                wd = sb("wd", NFC * 1024, st=cst)
                wo = sb("wo", 8 * 1024, st=cst)
                wgs = Ring("wg", [sb("wg%d" % i, 2048, st=cst) for i in range(2)])
                wus = Ring("wu", [sb("wu%d" % i, 2048, st=cst) for i in range(2)])
                gpost = sb("gpost", D, F32, st=cst); gffn = sb("gffn", D, F32, st=cst); gpffn = sb("gpffn", D, F32, st=cst)
                xts = Ring("xt", [sb("cxt%d" % i, D, F32, st=cst) for i in range(1)])
                ysbs = Ring("ysb", [sb("ysb%d" % i, D, F32, st=cst) for i in range(1)])
                hbs = Ring("hb", [sb("chb%d" % i, D, st=cst) for i in range(2)])
                junk = sb("cjunk", D, st=cst)
                sgs = Ring("sg", [sb("sg%d" % i, 512, F32, st=cst) for i in range(2)])
                ssa = Ring("ssa", [sb("ssa%d" % i, 2, F32, st=cst) for i in range(2)])
                sss = Ring("ss", [sb("css%d" % i, 1, F32, st=cst) for i in range(2)])
                rss = Ring("rs", [sb("crs%d" % i, 1, F32, st=cst) for i in range(2)])
                yps = Ring("yp", [ps("yp%d" % i, 512, st=cst) for i in range(4)])
                gps = Ring("gp", [ps("gp%d" % i, 512, st=cst) for i in range(3)])
                ups = gps
                pTs = Ring("pT", [ps("cpT%d" % i, 1024, BF16, st=cst) for i in range(1)])
                P.add("sp", lambda e: e.dma_start(out=sap(wo, 0, [[1024, 8], [1, 1024]]), in_=dap(wob, 0, [[1024, 128], [128 * 1024, 8], [1, 1024]])),
                      reads=["wob"], writes=["wo"], dma_sem="wo")
                P.add("sp", lambda e: e.dma_start(out=sap(wd, 0, [[1024, NFC], [1, 1024]]), in_=dap(wdb, 0, [[1024, 128], [128 * 1024, NFC], [1, 1024]])),
                      reads=["wdb"], writes=["wd"], dma_sem="wd")
                for gt, gsrc, gk in ((gpost, g_post, "gpost"), (gffn, g_ffn, "gffn"), (gpffn, g_pffn, "gpffn")):
                    P.add("sp", (lambda e, gt=gt, gsrc=gsrc: e.dma_start(out=gt[:], in_=dap(gsrc, 0, [[0, 128], [1, D]]))), writes=[gk], dma_sem=gk)

                def sumsq(src_aps, src_keys, ss, sk):
                    sa, sak = ssa.next()
                    for i, (ap_, k_) in enumerate(zip(src_aps, src_keys)):
                        P.add("act", (lambda e, ap_=ap_, i=i, sa=sa: e.activation(out=junk[:, 0:512], in_=ap_, func=AF.Square, accum_out=sa[:, i:i + 1])),
                              reads=[k_], writes=["cjunk", sak])
                    P.add("dve", (lambda e, sa=sa, ss=ss: e.tensor_tensor(out=ss[:], in0=sa[:, 0:1], in1=sa[:, 1:2], op=ALU.add)), reads=[sak], writes=[sk])

                groups = [[0, 1, 2, 3], [4, 5, 6, 7], [8, 9, 10, 11], [12, 13, 14, 15, 16]]
                for grp in groups:
                    ntile = len(grp)
                    NTG = ntile * 128
                    cpend = []
                    for slot, T in enumerate(grp):
                        xsrc, xrow = (xo, T * 128) if T < 16 else (xs, 0)
                        ccol = T * 128
                        ypa, yka = yps.next(); ypb, ykb = yps.next()
                        for eh, yp in ((0, ypa), (1, ypb)):
                            for k in range(8):
                                P.add("pe", (lambda e, yp=yp, eh=eh, k=k, ccol=ccol: e.matmul(yp[:, 0:512], lhsT=catT[:, k * CT + ccol: k * CT + ccol + 128],
                                                                                               rhs=wo[:, k * 1024 + eh * 512: k * 1024 + (eh + 1) * 512], start=(k == 0), stop=(k == 7))),
                                      reads=["catT", "wo"], writes=[yka if eh == 0 else ykb])
                        ss, sk = sss.next(); rs, rk = rss.next()
                        sumsq([ypa[:, 0:512], ypb[:, 0:512]], [yka, ykb], ss, sk)
                        rstd_ops(ss, sk, rs, rk)
                        xt, xk = xts.next()
                        P.add("sp", (lambda e, xt=xt, xsrc=xsrc, xrow=xrow: e.dma_start(out=xt[:], in_=xsrc[xrow:xrow + 128, :])), writes=[xk], dma_sem=xk)
                        x1k = "x1_%d" % slot
                        for eh, yp, yk in ((0, ypa, yka), (1, ypb, ykb)):
                            P.add("dve", (lambda e, yp=yp, eh=eh, rs=rs, slot=slot: e.scalar_tensor_tensor(
                                out=x1[:, slot * 1024 + eh * 512: slot * 1024 + (eh + 1) * 512], in0=yp[:, 0:512], scalar=rs[:, 0:1],
                                in1=gpost[:, eh * 512:(eh + 1) * 512], op0=ALU.mult, op1=ALU.mult)), reads=[yk, rk, "gpost"], writes=[x1k])
                        P.add("pool", (lambda e, xt=xt, slot=slot: e.tensor_tensor(out=x1[:, slot * 1024:(slot + 1) * 1024], in0=x1[:, slot * 1024:(slot + 1) * 1024],
                                                                                   in1=xt[:], op=ALU.add)), reads=[x1k, xk], writes=[x1k])
                        ss2, sk2 = sss.next(); rs2, rk2 = rss.next(); hb, hk = hbs.next(); pT, pk = pTs.next()
                        sumsq([x1[:, slot * 1024: slot * 1024 + 512], x1[:, slot * 1024 + 512:(slot + 1) * 1024]], [x1k, x1k], ss2, sk2)
                        rstd_ops(ss2, sk2, rs2, rk2)
                        P.add("dve", (lambda e, hb=hb, rs2=rs2, slot=slot: e.scalar_tensor_tensor(out=hb[:], in0=x1[:, slot * 1024:(slot + 1) * 1024], scalar=rs2[:, 0:1],
                                                                                                  in1=gffn[:], op0=ALU.mult, op1=ALU.mult)), reads=[x1k, rk2, "gffn"], writes=[hk])
                        def stage2(pT=pT, pk=pk, hb=hb, hk=hk, slot=slot, NTG=NTG):
                            for dc in range(8):
                                P.add("pe", (lambda e, dc=dc: e.transpose(out=pT[:, dc * 128:(dc + 1) * 128], in_=hb[:, dc * 128:(dc + 1) * 128], identity=ident[:])),
                                      reads=[hk, "ident"], writes=[pk])
                            P.add("act", (lambda e: e.activation(out=sap(hT, slot * 128, [[NTG, 8], [1, 128]]), in_=sap(pT, 0, [[128, 8], [1, 128]]), func=AF.Copy)),
                                  reads=[pk], writes=["hT"])
                        cpend.append(stage2)
                        if len(cpend) > 1:
                            cpend.pop(0)()
                    while cpend:
                        cpend.pop(0)()
                    tbs = [(0, 512)] if ntile == 4 else [(0, 512), (512, 128)]
                    for fc in range(NFC):
                        if fc % 2 == 0:
                            nf = min(2, NFC - fc) * 128
                            wg, wgk = wgs.next(); wu, wuk = wus.next()
                            P.add("sp", (lambda e, wg=wg, fc=fc, nf=nf: e.dma_start(out=sap(wg, 0, [[256, 8], [1, nf]]), in_=dap(wgb, fc * 128, [[DFF, 128], [128 * DFF, 8], [1, nf]]))),
                                  reads=["wgb"], writes=[wgk], dma_sem=wgk)
                            P.add("sp", (lambda e, wu=wu, fc=fc, nf=nf: e.dma_start(out=sap(wu, 0, [[256, 8], [1, nf]]), in_=dap(wub, fc * 128, [[DFF, 128], [128 * DFF, 8], [1, nf]]))),
                                  reads=["wub"], writes=[wuk], dma_sem=wuk)
                        fo = (fc % 2) * 128
                        for (tb0, n) in tbs:
                            gp, gk = gps.next(); up, uk = ups.next(); sg, sgk = sgs.next()
                            for wt, wk_, pp, ppk in ((wg, wgk, gp, gk), (wu, wuk, up, uk)):
                                for dc in range(8):
                                    P.add("pe", (lambda e, wt=wt, pp=pp, dc=dc, tb0=tb0, n=n, NTG=NTG, fo=fo: e.matmul(pp[:, 0:n], lhsT=wt[:, dc * 256 + fo: dc * 256 + fo + 128],
                                                                                                               rhs=hT[:, dc * NTG + tb0: dc * NTG + tb0 + n], start=(dc == 0), stop=(dc == 7))),
                                          reads=[wk_, "hT"], writes=[ppk])
                            P.add("act", (lambda e, gp=gp, sg=sg, n=n: e.activation(out=sg[:, 0:n], in_=gp[:, 0:n], func=AF.Silu)), reads=[gk], writes=[sgk])
                            P.add("dve", (lambda e, up=up, sg=sg, fc=fc, tb0=tb0, n=n, NTG=NTG: e.tensor_tensor(out=actT[:, fc * NTG + tb0: fc * NTG + tb0 + n], in0=up[:, 0:n],
                                                                                                              in1=sg[:, 0:n], op=ALU.mult)), reads=[uk, sgk], writes=["actT"])
                    for slot, T in enumerate(grp):
                        ypa, yka = yps.next(); ypb, ykb = yps.next()
                        for eh, yp in ((0, ypa), (1, ypb)):
                            for fc in range(NFC):
                                P.add("pe", (lambda e, yp=yp, eh=eh, fc=fc, slot=slot, NTG=NTG: e.matmul(yp[:, 0:512], lhsT=actT[:, fc * NTG + slot * 128: fc * NTG + (slot + 1) * 128],
                                                                                                        rhs=wd[:, fc * 1024 + eh * 512: fc * 1024 + (eh + 1) * 512], start=(fc == 0), stop=(fc == NFC - 1))),
                                      reads=["actT", "wd"], writes=[yka if eh == 0 else ykb])
                        ss, sk = sss.next(); rs, rk = rss.next(); ysb, ysk = ysbs.next()
                        sumsq([ypa[:, 0:512], ypb[:, 0:512]], [yka, ykb], ss, sk)
                        rstd_ops(ss, sk, rs, rk)
                        x1k = "x1_%d" % slot
                        for eh, yp, yk in ((0, ypa, yka), (1, ypb, ykb)):
                            P.add("dve", (lambda e, yp=yp, eh=eh, rs=rs, ysb=ysb: e.scalar_tensor_tensor(
                                out=ysb[:, eh * 512:(eh + 1) * 512], in0=yp[:, 0:512], scalar=rs[:, 0:1],
                                in1=gpffn[:, eh * 512:(eh + 1) * 512], op0=ALU.mult, op1=ALU.mult)), reads=[yk, rk, "gpffn"], writes=[ysk])
                        P.add("pool", (lambda e, ysb=ysb, slot=slot: e.tensor_tensor(out=ysb[:], in0=ysb[:], in1=x1[:, slot * 1024:(slot + 1) * 1024], op=ALU.add)),
                              reads=[ysk, x1k], writes=[ysk])
                        if T < 16:
                            P.add("sp", (lambda e, ysb=ysb, T=T: e.dma_start(out=yo[T * 128:(T + 1) * 128, :], in_=ysb[:])), reads=[ysk], dma_sem=ysk)
                        else:
                            P.add("sp", (lambda e, ysb=ysb: e.dma_start(out=ys[:, :], in_=ysb[:])), reads=[ysk], dma_sem=ysk)
                P.flush()
        P.flush(final=True)
    return nc


def _consts(hf):
    f32 = np.float32
    c = {}
    c["c_ident"] = np.eye(128, dtype=f32)
    R = np.zeros((128, 128), f32)
    for pp in range(128):
        if (pp % 64) < 32:
            R[pp + 32, pp] = -1.0
        else:
            R[pp - 32, pp] = 1.0
    c["c_rot"] = R
    inv = np.power(f32(10000.0), -(np.arange(32, dtype=f32) / f32(32))).astype(f32)
    pos = np.concatenate([(hf - 1) * 2048 + np.arange(2048), hf * 2048 + np.arange(2048), PAST + (np.arange(128) % 8)]).astype(f32)
    ang = (pos[:, None] * inv[None, :]).astype(f32)
    fidx = (np.arange(128) % 64) % 32
    c["c_cos"] = np.ascontiguousarray(np.cos(ang).astype(f32)[:, fidx].T)
    c["c_sin"] = np.ascontiguousarray(np.sin(ang).astype(f32)[:, fidx].T)
    kk = np.arange(128)[:, None]
    qq = np.arange(128)[None, :]
    prev = (kk >= qq).astype(f32)
    diag = (kk <= qq).astype(f32)
    c["c_mstd"] = np.concatenate([prev, diag, prev, diag], axis=1)
    c["c_mfirst"] = np.concatenate([prev * f32(hf), diag, prev * f32(hf), diag], axis=1)
    i8 = np.arange(8)
    msw = (np.arange(128)[:, None] >= i8[None, :]).astype(f32)
    c["c_msw"] = np.ascontiguousarray(np.tile(msw[:, None, :], (1, 96, 1)).reshape(128, 768))
    s_ = np.arange(128) // 8
    j_ = np.arange(128) % 8
    same = (s_[:, None] == s_[None, :])
    dji = j_[None, :] - j_[:, None]
    c["c_mnsw"] = (same & (dji >= 0)).astype(f32)
    mult_new = (dji >= 0).astype(f32) + ((dji >= 0) & (dji % 4 == 0)).astype(f32) + (dji == 0).astype(f32)
    c["c_mndil"] = (same.astype(f32) * mult_new).astype(f32)
    g = np.arange(128)[:, None, None]
    cc = np.arange(16)[None, :, None]
    ii = np.arange(8)[None, None, :]
    p3 = (cc == ii)
    p2 = ((cc % 4) == (ii % 4)) & ((g > 96) | ((g == 96) & (cc >= ii)))
    p1 = (g > 120) | ((g == 120) & (cc >= ii))
    mult = p3.astype(f32) + p2.astype(f32) + p1.astype(f32)
    c["c_mdil"] = np.ascontiguousarray(np.tile(mult[:, :, None, :], (1, 1, 2, 1)).reshape(128, 256))
    c["c_nhalf"] = np.full((128, 1), -0.5, f32)
    return c


_NC_CACHE = {}


def kernel(x_prompt, x_sample, cache_swa_k, cache_swa_v, cache_dil_k, cache_dil_v,
           cache_mem_k, cache_mem_v, mem_prompt, g_pre_mix, w_in, sinks, w_mem_kv, w_o,
           g_post_mix, g_pre_ffn, w_gate, w_up, w_down, g_post_ffn):
    f32 = np.float32
    A = lambda a: np.ascontiguousarray(np.asarray(a, dtype=f32))
    x_prompt = A(x_prompt); x_sample = A(x_sample)
    shared = {
        "w_in": A(w_in)[0], "w_mem": A(w_mem_kv)[0], "w_o": A(w_o)[0],
        "w_gate": A(w_gate)[0], "w_up": A(w_up)[0], "w_down": A(w_down)[0],
        "g_pre": A(g_pre_mix).reshape(1, D), "g_post": A(g_post_mix).reshape(1, D),
        "g_ffn": A(g_pre_ffn).reshape(1, D), "g_pffn": A(g_post_ffn).reshape(1, D),
        "sinks": A(sinks).reshape(1, 6),
    }
    csk = A(cache_swa_k)[0].reshape(128, 128, 128); csv = A(cache_swa_v)[0].reshape(128, 128, 128)
    cdk = A(cache_dil_k)[0].reshape(128, 2048, 384); cdv = A(cache_dil_v)[0].reshape(128, 2048, 384)
    cmk = A(cache_mem_k)[0].reshape(128, 256, 256); cmv = A(cache_mem_v)[0].reshape(128, 256, 256)
    mem_prompt = A(mem_prompt)
    cst = [_consts(0), _consts(1)]
    in_maps = []
    for c in range(8):
        b, hf = c // 2, c % 2
        m = dict(shared)
        m["xo"] = x_prompt[b, hf * 2048:(hf + 1) * 2048]
        m["xh"] = x_prompt[b, 0:2048] if hf == 1 else np.zeros((2048, D), f32)
        m["xs"] = x_sample[c * 16:(c + 1) * 16].reshape(128, D)
        m["memp"] = mem_prompt[b]
        sl = slice(c * 16, (c + 1) * 16)
        m["csk"] = csk[sl]; m["csv"] = csv[sl]; m["cdk"] = cdk[sl]; m["cdv"] = cdv[sl]
        m["cmk"] = cmk[sl]; m["cmv"] = cmv[sl]
        m.update(cst[hf])
        in_maps.append(m)
    if "nc" not in _NC_CACHE:
        _NC_CACHE["nc"] = build()
    res = run_bass_kernel_spmd(_NC_CACHE["nc"], in_maps, core_ids=list(range(8)))
    R = res.results
    y_prompt = np.zeros((4, 4096, D), f32); y_sample = np.zeros((128, 8, D), f32)
    p_swa_k = np.zeros((1, 4, 128, 2, 64), f32); p_swa_v = np.zeros_like(p_swa_k)
    p_dil_k = np.zeros((1, 4, 2048, 6, 64), f32); p_dil_v = np.zeros_like(p_dil_k)
    p_mem_k = np.zeros((1, 4, 256, 4, 64), f32); p_mem_v = np.zeros_like(p_mem_k)
    s_swa_k = np.zeros((1, 128, 128, 2, 64), f32); s_swa_v = np.zeros_like(s_swa_k)
    s_dil_k = np.zeros((1, 128, 2048, 6, 64), f32); s_dil_v = np.zeros_like(s_dil_k)
    for c in range(8):
        b, hf = c // 2, c % 2
        r = R[c]
        y_prompt[b, hf * 2048:(hf + 1) * 2048] = r["yo"]
        y_sample[c * 16:(c + 1) * 16] = r["ys"].reshape(16, 8, D)
        if hf == 1:
            p_swa_k[0, b] = r["o_swa_k"].reshape(128, 2, 64); p_swa_v[0, b] = r["o_swa_v"].reshape(128, 2, 64)
            p_dil_k[0, b] = r["o_dil_k"].reshape(2048, 6, 64); p_dil_v[0, b] = r["o_dil_v"].reshape(2048, 6, 64)
        else:
            p_mem_k[0, b] = r["o_mem_k"].reshape(256, 4, 64); p_mem_v[0, b] = r["o_mem_v"].reshape(256, 4, 64)
        sl = slice(c * 16, (c + 1) * 16)
        s_swa_k[0, sl] = r["s_swa_k"].reshape(16, 128, 2, 64); s_swa_v[0, sl] = r["s_swa_v"].reshape(16, 128, 2, 64)
        s_dil_k[0, sl] = r["s_dil_k"].reshape(16, 2048, 6, 64); s_dil_v[0, sl] = r["s_dil_v"].reshape(16, 2048, 6, 64)
    return (y_prompt, y_sample, p_swa_k, p_swa_v, p_dil_k, p_dil_v, p_mem_k, p_mem_v,
            s_swa_k, s_swa_v, s_dil_k, s_dil_v)
```
